# Optimizing a Trainium2 kernel written in Bass

```python
import math
import jax, jax.numpy as jnp
from jax import lax
import numpy as np

D_MODEL = 2048
BATCH = 32
SEQ = 256
DEPTH = 1
DEC_BATCH = 4
DEC_SEQ = 2048
PAST_LEN = 256

GRID_W = 64
H_A = 8
DQK_A = 64
DV_A = 128
W_A = H_A * DV_A
H_B = 8
DH_B = 64
DV_B = 2 * DH_B
W_B = H_B * DV_B
D_FF = 5632
CHUNK = 64
Q_BLOCK = 128
ROPE_BASE = 10000.0
GATE_CAP = 15.0
EPS = 1e-6
IN_SIZES = (H_A * DQK_A, H_A * DQK_A, W_A, W_A, 4 * H_A, H_B * 2 * DH_B, H_B * 2 * DH_B, W_B, D_MODEL, D_MODEL)
N_IN = 2 * H_A * DQK_A + 2 * W_A + 4 * H_A + 2 * H_B * 2 * DH_B + W_B + 2 * D_MODEL

kernel_name = 'diffusion_hybrid_mlstm_diffattn_step'


def rmsnorm(x, w):
    xf = x.astype(jnp.float32)
    y = xf * lax.rsqrt(jnp.mean(xf * xf, axis=-1, keepdims=True) + EPS)
    return (y * w.astype(jnp.float32)).astype(x.dtype)


def split_columns(a):
    idx = []
    acc = 0
    for s in IN_SIZES[:-1]:
        acc += s
        idx.append(acc)
    return jnp.split(a, idx, axis=-1)


def split_heads(x, n_heads):
    b, t, _ = x.shape
    return x.reshape(b, t, n_heads, -1).transpose(0, 2, 1, 3)


def merge_heads(x):
    b, h, t, d = x.shape
    return x.transpose(0, 2, 1, 3).reshape(b, t, h * d)


def axial_rope_tables(n_tokens):
    rows = n_tokens // GRID_W
    row = jnp.repeat(jnp.arange(rows), GRID_W).astype(jnp.float32)
    col = jnp.tile(jnp.arange(GRID_W), rows).astype(jnp.float32)
    quarter = DH_B // 4
    inv_freq = jnp.power(ROPE_BASE, -jnp.arange(quarter, dtype=jnp.float32) / quarter)
    ang_r = row[:, None] * inv_freq
    ang_c = col[:, None] * inv_freq
    ang = jnp.concatenate([ang_r, ang_r, ang_c, ang_c], axis=-1)
    return jnp.cos(ang), jnp.sin(ang)


def apply_axial_rope(x, cos, sin):
    x1, x2, x3, x4 = jnp.split(x, 4, axis=-1)
    rot = jnp.concatenate([-x2, x1, -x4, x3], axis=-1)
    return x * cos.astype(x.dtype) + rot * sin.astype(x.dtype)


def mlstm_scan(q, k, v, i_pre, log_f, C0, n0, m0):
    B, H, T, DK = q.shape
    nc = T // CHUNK
    f32 = jnp.float32

    def chunks(a):
        return jnp.moveaxis(a.reshape(a.shape[:2] + (nc, CHUNK) + a.shape[3:]), 2, 0)

    tri = jnp.tril(jnp.ones((CHUNK, CHUNK), dtype=bool))

    def step(carry, xs):
        C, n, m = carry
        qc, kc, vc, ic, fc = xs
        b = jnp.cumsum(fc, axis=-1)
        log_d = jnp.where(tri, b[..., :, None] - b[..., None, :] + ic[..., None, :], -jnp.inf)
        m_inter = b + m[..., None]
        m_t = jnp.maximum(m_inter, jnp.max(log_d, axis=-1))
        s = jnp.einsum('bhtd,bhsd->bhts', qc, kc) * jnp.exp(log_d - m_t[..., None])
        inter = jnp.exp(m_inter - m_t)
        num = jnp.einsum('bhts,bhsv->bhtv', s, vc) + inter[..., None] * jnp.einsum('bhtd,bhdv->bhtv', qc, C)
        den = jnp.sum(s, axis=-1) + inter * jnp.einsum('bhtd,bhd->bht', qc, n)
        h = num / jnp.maximum(jnp.abs(den), jnp.exp(-m_t))[..., None]
        b_last = b[..., -1]
        log_w = b_last[..., None] - b + ic
        m_new = jnp.maximum(b_last + m, jnp.max(log_w, axis=-1))
        w = jnp.exp(log_w - m_new[..., None])
        decay = jnp.exp(b_last + m - m_new)
        C = decay[..., None, None] * C + jnp.einsum('bhs,bhsd,bhsv->bhdv', w, kc, vc)
        n = decay[..., None] * n + jnp.einsum('bhs,bhsd->bhd', w, kc)
        return (C, n, m_new), h

    xs = tuple(chunks(a.astype(f32)) for a in (q, k, v, i_pre, log_f))
    (C, n, m), h = lax.scan(step, (C0.astype(f32), n0.astype(f32), m0.astype(f32)), xs)
    h = jnp.moveaxis(h, 0, 2).reshape(B, H, T, -1)
    return h, C, n, m


def mlstm_bidirectional(q, k, v, gates, C0, n0, m0):
    def flip(a):
        return jnp.flip(a, axis=2)
    h_f, Cf, nf, mf = mlstm_scan(q, k, v, gates[0], jax.nn.log_sigmoid(gates[1]), C0[:, 0], n0[:, 0], m0[:, 0])
    h_b, Cb, nb, mb = mlstm_scan(flip(q), flip(k), flip(v), flip(gates[2]), flip(jax.nn.log_sigmoid(gates[3])),
                                 C0[:, 1], n0[:, 1], m0[:, 1])
    h = h_f + flip(h_b)
    return h, jnp.stack([Cf, Cb], axis=1), jnp.stack([nf, nb], axis=1), jnp.stack([mf, mb], axis=1)


def diff_attention(q1, q2, k1, k2, v, lam):
    B, H, T, DH = q1.shape
    nb = T // Q_BLOCK
    scale = DH ** -0.5

    def to_blocks(a):
        return jnp.moveaxis(a.reshape(B, H, nb, Q_BLOCK, DH), 2, 0)

    def block(qs):
        qb1, qb2 = qs
        s1 = jnp.einsum('bhqd,bhkd->bhqk', qb1, k1).astype(jnp.float32) * scale
        s2 = jnp.einsum('bhqd,bhkd->bhqk', qb2, k2).astype(jnp.float32) * scale
        p = jax.nn.softmax(s1, axis=-1) - lam * jax.nn.softmax(s2, axis=-1)
        return jnp.einsum('bhqk,bhkv->bhqv', p.astype(v.dtype), v)

    o = lax.map(block, (to_blocks(q1), to_blocks(q2)))
    return jnp.moveaxis(o, 0, 2).reshape(B, H, T, -1)


def conv_ffn(h, w_up, conv_w, conv_b, w_down):
    u = h @ w_up
    up = jnp.pad(u, ((0, 0), (1, 1), (0, 0)))
    u = up[:, :-2] * conv_w[0] + up[:, 1:-1] * conv_w[1] + up[:, 2:] * conv_w[2] + conv_b
    a, b = jnp.split(u, 2, axis=-1)
    return (jax.nn.silu(a) * b) @ w_down


def token_mixers(h, p, lambda_init, rope, ctx_kv, state0):
    b, t, _ = h.shape
    q_a, k_a, v_a, o_a, g_pre, q_b, k_b, v_b, gate_a, gate_b = split_columns(h @ p['w_in'])
    qa = split_heads(q_a, H_A) * (DQK_A ** -0.5)
    ka = split_heads(k_a, H_A)
    va = split_heads(v_a, H_A)
    g = (g_pre + p['b_gates']).astype(jnp.float32)
    g = GATE_CAP * jnp.tanh(g / GATE_CAP)
    g = g.reshape(b, t, 4, H_A).transpose(2, 0, 3, 1)
    h_a, C, n, m = mlstm_bidirectional(qa, ka, va, g, state0[0], state0[1], state0[2])
    h_a = rmsnorm(h_a.astype(h.dtype), p['mlstm_norm'].reshape(H_A, 1, DV_A))
    y_a = merge_heads(h_a) * jax.nn.sigmoid(o_a)
    qb = split_heads(q_b, H_B)
    kb = split_heads(k_b, H_B)
    vb = split_heads(v_b, H_B)
    q1, q2 = qb[..., :DH_B], qb[..., DH_B:]
    if rope is None:
        keys, vals = kb, vb
    else:
        cos, sin = rope
        q1 = apply_axial_rope(q1, cos, sin)
        q2 = apply_axial_rope(q2, cos, sin)
        k_lat = jnp.concatenate([apply_axial_rope(kb[..., :DH_B], cos, sin),
                                 apply_axial_rope(kb[..., DH_B:], cos, sin)], axis=-1)
        keys = jnp.concatenate([k_lat, ctx_kv[0].astype(h.dtype)], axis=2)
        vals = jnp.concatenate([vb, ctx_kv[1].astype(h.dtype)], axis=2)
    f32 = jnp.float32
    lam = (jnp.exp(jnp.sum(p['lam_q1'].astype(f32) * p['lam_k1'].astype(f32)))
           - jnp.exp(jnp.sum(p['lam_q2'].astype(f32) * p['lam_k2'].astype(f32))) + lambda_init)
    o_b = diff_attention(q1, q2, keys[..., :DH_B], keys[..., DH_B:], vals, lam)
    o_b = rmsnorm(o_b, p['diff_norm']) * (1.0 - lambda_init)
    y_b = merge_heads(o_b)
    y = jax.nn.sigmoid(gate_a) * (y_a @ p['w_pa']) + jax.nn.sigmoid(gate_b) * (y_b @ p['w_pb'])
    return y @ p['w_out'], (kb, vb), (C, n, m)


def trunk_layer(x, mod, p, lambda_init, rope, ctx_kv, state0):
    shift1, scale1, gate1, shift2, scale2, gate2 = jnp.split(mod[:, None, :], 6, axis=-1)
    h = rmsnorm(x, p['norm1']) * (1.0 + scale1) + shift1
    y, kv, st = token_mixers(h, p, lambda_init, rope, ctx_kv, state0)
    x = x + gate1 * y
    h = rmsnorm(x, p['norm2']) * (1.0 + scale2) + shift2
    x = x + gate2 * conv_ffn(h, p['w_up'], p['conv_w'], p['conv_b'], p['w_down'])
    return x, kv, st


def setup_inputs(seed: int = 0) -> dict:
    key = jax.random.key(seed)
    ks = jax.random.split(key, 32)
    f32 = jnp.float32

    def nrm(k, shape, s):
        return jax.random.normal(k, shape, f32) * s

    gate_offset = jnp.array([0.0, 3.0, 0.0, 3.0], f32)[None, :, None]
    b_gates = (gate_offset + nrm(ks[12], (DEPTH, 4, H_A), 0.3)).reshape(DEPTH, 4 * H_A)
    return {
        'x_prompt': nrm(ks[0], (BATCH, SEQ, D_MODEL), 1.0),
        'x_sample': nrm(ks[1], (DEC_BATCH, DEC_SEQ, D_MODEL), 1.0),
        'cache_k': nrm(ks[2], (DEC_BATCH, DEPTH, H_B, PAST_LEN, 2 * DH_B), 1.0),
        'cache_v': nrm(ks[3], (DEC_BATCH, DEPTH, H_B, PAST_LEN, DV_B), 1.0),
        'state_C': nrm(ks[4], (DEC_BATCH, DEPTH, 2, H_A, DQK_A, DV_A), 0.5),
        'state_n': nrm(ks[5], (DEC_BATCH, DEPTH, 2, H_A, DQK_A), 0.5),
        'state_m': nrm(ks[6], (DEC_BATCH, DEPTH, 2, H_A), 1.0),
        'c': nrm(ks[7], (DEC_BATCH, D_MODEL), 1.0),
        'c_ctx': nrm(ks[8], (D_MODEL,), 1.0),
        'w_mod': nrm(ks[9], (DEPTH, D_MODEL, 6 * D_MODEL), 0.5 * D_MODEL ** -0.5),
        'b_mod': nrm(ks[10], (DEPTH, 6 * D_MODEL), 0.01),
        'norm1': 1.0 + nrm(ks[11], (DEPTH, D_MODEL), 0.01),
        'w_in': nrm(ks[13], (DEPTH, D_MODEL, N_IN), D_MODEL ** -0.5),
        'b_gates': b_gates,
        'mlstm_norm': 1.0 + nrm(ks[14], (DEPTH, W_A), 0.01),
        'lam_q1': nrm(ks[15], (DEPTH, DH_B), 0.1),
        'lam_k1': nrm(ks[16], (DEPTH, DH_B), 0.1),
        'lam_q2': nrm(ks[17], (DEPTH, DH_B), 0.1),
        'lam_k2': nrm(ks[18], (DEPTH, DH_B), 0.1),
        'diff_norm': 1.0 + nrm(ks[19], (DEPTH, DV_B), 0.01),
        'w_pa': nrm(ks[20], (DEPTH, W_A, D_MODEL), W_A ** -0.5),
        'w_pb': nrm(ks[21], (DEPTH, W_B, D_MODEL), W_B ** -0.5),
        'w_out': nrm(ks[22], (DEPTH, D_MODEL, D_MODEL), D_MODEL ** -0.5),
        'norm2': 1.0 + nrm(ks[23], (DEPTH, D_MODEL), 0.01),
        'w_up': nrm(ks[24], (DEPTH, D_MODEL, 2 * D_FF), D_MODEL ** -0.5),
        'conv_w': nrm(ks[25], (DEPTH, 3, 2 * D_FF), 3.0 ** -0.5),
        'conv_b': nrm(ks[26], (DEPTH, 2 * D_FF), 0.01),
        'w_down': nrm(ks[27], (DEPTH, D_FF, D_MODEL), D_FF ** -0.5),
        'final_norm': 1.0 + nrm(ks[28], (D_MODEL,), 0.01),
    }


def reference(x_prompt, x_sample, cache_k, cache_v, state_C, state_n, state_m, c, c_ctx,
              w_mod, b_mod, norm1, w_in, b_gates, mlstm_norm, lam_q1, lam_k1, lam_q2, lam_k2,
              diff_norm, w_pa, w_pb, w_out, norm2, w_up, conv_w, conv_b, w_down, final_norm):
    f32 = jnp.float32
    xp = x_prompt
    xs = x_sample
    bp = xp.shape[0]
    rope = axial_rope_tables(xs.shape[1])
    new_k, new_v, new_C, new_n, new_m = [], [], [], [], []
    for l in range(DEPTH):
        p = {'w_in': w_in[l], 'b_gates': b_gates[l], 'mlstm_norm': mlstm_norm[l],
             'lam_q1': lam_q1[l], 'lam_k1': lam_k1[l], 'lam_q2': lam_q2[l], 'lam_k2': lam_k2[l],
             'diff_norm': diff_norm[l], 'w_pa': w_pa[l], 'w_pb': w_pb[l], 'w_out': w_out[l],
             'norm1': norm1[l], 'norm2': norm2[l], 'w_up': w_up[l], 'conv_w': conv_w[l],
             'conv_b': conv_b[l], 'w_down': w_down[l]}
        lambda_init = 0.8 - 0.6 * math.exp(-0.3 * l)
        mod_ctx = jax.nn.silu(c_ctx)[None, :] @ w_mod[l] + b_mod[l]
        zero_state = (jnp.zeros((bp, 2, H_A, DQK_A, DV_A), f32),
                      jnp.zeros((bp, 2, H_A, DQK_A), f32),
                      jnp.zeros((bp, 2, H_A), f32))
        xp, (k_ctx, v_ctx), (C_ctx, n_ctx, m_ctx) = trunk_layer(xp, mod_ctx, p, lambda_init, None, None, zero_state)
        new_k.append(k_ctx.astype(x_prompt.dtype))
        new_v.append(v_ctx.astype(x_prompt.dtype))
        new_C.append(C_ctx.astype(x_prompt.dtype))
        new_n.append(n_ctx.astype(x_prompt.dtype))
        new_m.append(m_ctx.astype(x_prompt.dtype))
        mod_lat = jax.nn.silu(c) @ w_mod[l] + b_mod[l]
        xs, _, _ = trunk_layer(xs, mod_lat, p, lambda_init, rope, (cache_k[:, l], cache_v[:, l]),
                               (state_C[:, l], state_n[:, l], state_m[:, l]))
    y_prompt = rmsnorm(xp, final_norm)
    y_sample = rmsnorm(xs, final_norm)
    new_cache_k = jnp.stack(new_k, axis=1)
    new_cache_v = jnp.stack(new_v, axis=1)
    new_state_C = jnp.stack(new_C, axis=1)
    new_state_n = jnp.stack(new_n, axis=1)
    new_state_m = jnp.stack(new_m, axis=1)
    return (y_prompt, y_sample, new_cache_k, new_cache_v, new_state_C, new_state_n, new_state_m)
```

```python
import math
import numpy as np
import concourse.bass as bass
import concourse.mybir as mybir
from concourse.bass_utils import run_bass_kernel_spmd

F32 = mybir.dt.float32
BF16 = mybir.dt.bfloat16
AF = mybir.ActivationFunctionType
ALU = mybir.AluOpType
AX = mybir.AxisListType

D = 2048
T = 2048
NB = 16
KC = 16
SEG = 256
NSEG = 8
H = 8
N_IN = 10272
D_FF = 5632
NFC = D_FF // 128
EPS = 1e-6
GATE_CAP = 15.0
LAMBDA_INIT = 0.8 - 0.6 * math.exp(0.0)
NEG = -30000.0

OFF_QA = 0
OFF_KA = 512
OFF_VA = 1024
OFF_OA = 2048
OFF_G = 3072
OFF_QB = 3104
OFF_KB = 4128
OFF_VB = 5152
OFF_GA = 6176
OFF_GB = 8224


class Buf:
    __slots__ = ("name", "w", "r", "excl")

    def __init__(self, name):
        self.name = name
        self.excl = False
        self.w = None
        self.r = []


class Prog:
    ENGS = ("pe", "act", "dve", "pool", "sp")

    def __init__(self, nc):
        self.nc = nc
        self.ops = {e: [] for e in self.ENGS}
        self.cnt = {e: 0 for e in self.ENGS}
        self.pending = {e: False for e in self.ENGS}
        self.waited = {e: {} for e in self.ENGS}
        self.sems = {}
        self.dma_cnt = {}
        self.nbuf = 0
        self.tag = ''
        import os
        self.skip = set(x for x in os.environ.get('SKIP', '').split(',') if x)

    def buf(self, name=None):
        self.nbuf += 1
        return Buf(name or f"b{self.nbuf}")

    def bufs(self, n, name="b"):
        return [self.buf(f"{name}{i}") for i in range(n)]

    def _wait(self, eng, key, val):
        if val <= 0:
            return
        if self.waited[eng].get(key, 0) >= val:
            return
        self.waited[eng][key] = val
        self.ops[eng].append(("wait", key, val))

    def _deps(self, eng, reads, writes, skip_self):
        need = {}

        def add(dep):
            if skip_self and dep[0] == eng:
                return
            if need.get(dep[0], 0) < dep[1]:
                need[dep[0]] = dep[1]
        for b in reads:
            if b.w is not None:
                add(b.w)
        for b in writes:
            if b.w is not None:
                add(b.w)
            for rr in b.r:
                add(rr)
        for key, val in need.items():
            self._wait(eng, key, val)

    def op(self, eng, fn, reads=(), writes=(), inc=True):
        if self.tag in self.skip:
            return
        skip_self = eng == "pe"
        xr = [b for b in reads if b.excl]
        if xr:
            reads = [b for b in reads if not b.excl]
            writes = list(writes) + [b for b in xr if b not in writes]
        self._deps(eng, reads, writes, skip_self)
        idx = self.cnt[eng] + 1
        for b in reads:
            b.r.append((eng, idx))
        for b in writes:
            b.w = (eng, idx)
            b.r = []
        if inc:
            self.cnt[eng] = idx
            self.pending[eng] = False
        else:
            self.pending[eng] = True
        self.ops[eng].append(("op", fn, inc))

    def dma(self, q, slot, fn, reads=(), writes=()):
        if self.tag in self.skip:
            return
        self._deps(q, reads, writes, False)
        key = "d:" + slot
        n = self.dma_cnt.get(key, 0) + 16
        self.dma_cnt[key] = n
        for b in reads:
            b.r.append((key, n))
        for b in writes:
            b.w = (key, n)
            b.r = []
        self.ops[q].append(("dma", fn, key))

    def barrier(self):
        for e in self.ENGS:
            assert not self.pending[e]
        for e in self.ENGS:
            for e2 in self.ENGS:
                if e2 != e:
                    self._wait(e, e2, self.cnt[e2])
            for key, n in self.dma_cnt.items():
                self._wait(e, key, n)

    def sem_keys(self):
        return list(self.ENGS) + list(self.dma_cnt.keys())

    def emit(self, eng, engine_obj, sems):
        for o in self.ops[eng]:
            if o[0] == "wait":
                engine_obj.wait_ge(sems[o[1]], o[2])
            elif o[0] == "op":
                ins = o[1](engine_obj)
                if o[2]:
                    ins.then_inc(sems[eng], 1)
            else:
                o[1](engine_obj).then_inc(sems[o[2]], 16)


def simulate(P):
    ptr = {e: 0 for e in P.ENGS}
    val = {}
    prog = True
    while prog:
        prog = False
        for e in P.ENGS:
            ops = P.ops[e]
            while ptr[e] < len(ops):
                o = ops[ptr[e]]
                if o[0] == "wait":
                    if val.get(o[1], 0) < o[2]:
                        break
                elif o[0] == "op":
                    if o[2]:
                        val[e] = val.get(e, 0) + 1
                else:
                    val[o[2]] = val.get(o[2], 0) + 16
                ptr[e] += 1
                prog = True
    stuck = {e: (ptr[e], len(P.ops[e]), P.ops[e][ptr[e]][:3] if ptr[e] < len(P.ops[e]) else None) for e in P.ENGS}
    ok = all(ptr[e] == len(P.ops[e]) for e in P.ENGS)
    return ok, stuck, val
class Arena:
    def __init__(self, tens, base_bytes, nbytes):
        self.t = tens
        self.base = base_bytes
        self.size = nbytes
        self.off = 0

    def reset(self):
        self.off = 0

    def alloc(self, shape, dt):
        n = 1
        for s_ in shape:
            n *= s_
        nb = n * (4 if dt == F32 else 2)
        nb_al = (nb + 31) // 32 * 32
        assert self.off + nb_al <= self.size, (self.off, nb_al, self.size)
        o = (self.base + self.off) // 4
        ap = self.t[:, o:o + nb_al // 4]
        self.off += nb_al
        if dt != F32:
            ap = ap.bitcast(dt)
        ap = ap[:, 0:n]
        if len(shape) == 2:
            ap = ap.rearrange("p (a b) -> p a b", b=shape[1])
        elif len(shape) == 3:
            ap = ap.rearrange("p (a b c) -> p a b c", b=shape[1], c=shape[2])
        return ap


def build_program(dbg=None, upto=99):
    from contextlib import ExitStack
    nc = bass.Bass("TRN2", target_bir_lowering=False)
    P = Prog(nc)

    def din(name, shape, dt=F32):
        return nc.dram_tensor(name, list(shape), dt, kind="ExternalInput").ap()

    def dout(name, shape, dt=F32):
        return nc.dram_tensor(name, list(shape), dt, kind="ExternalOutput").ap()

    x_d = din("x", [T, D])
    cvec_d = din("cvec", [D])
    w_mod_d = din("w_mod", [D, 6 * D])
    b_mod_d = din("b_mod", [6 * D])
    norm1_d = din("norm1", [D])
    norm2_d = din("norm2", [D])
    fnorm_d = din("final_norm", [D])
    ident_d = din("ident", [128, 128])
    w_in_d = din("w_in", [D, N_IN])
    ck_d = din("ck", [H, 256, 128])
    cv_d = din("cv", [H, 256, 128])
    cosT_d = din("cosT", [128, T])
    sinT_d = din("sinT", [128, T])
    rrot_d = din("rrot", [128, 128])
    abias_d = din("abias", [128, NSEG * 18])
    lamv_d = din("lamv", [4, 64])
    dnorm_d = din("diff_norm", [128])
    bgates_d = din("b_gates", [32])
    umask_d = din("umask", [128, 128])
    lmask_d = din("lmask", [128, 128])
    m0_d = din("m0", [2, H])
    keep_d = din("keep", [1])
    mlnorm_d = din("mlstm_norm", [1024])
    c0_d = din("c0", [2, H, 64, 128])
    w_pa_d = din("w_pa", [1024, D])
    w_pb_d = din("w_pb", [1024, D])
    w_out_d = din("w_out", [D, D])
    w_up_d = din("w_up", [D, 2 * D_FF])
    w_down_d = din("w_down", [D_FF, D])
    convw_d = din("conv_w", [3, 2 * D_FF])
    convb_d = din("conv_b", [2 * D_FF])
    n0_d = din("n0", [2, H, 64])

    y_d = dout("y", [T, D])
    nk_d = dout("nk", [NSEG, H, SEG, 128])
    nv_d = dout("nv", [NSEG, H, SEG, 128])
    nC_d = dout("nC", [NSEG, 2, H, 64, 128])
    nn_d = dout("nn", [NSEG, 2, H, 64])
    nm_d = dout("nm", [NSEG, 2, H])

    dbg_d = {}
    if dbg:
        for nm, (shape, dt) in dbg.items():
            dbg_d[nm] = dout("dbg_" + nm, shape, dt)

    es = ExitStack()

    def sb(name, shape, dt):
        return es.enter_context(nc.sbuf_tensor(name, list(shape), dt))

    ident_f = sb("ident_f", [128, 128], F32)
    ident_b = sb("ident_b", [128, 128], BF16)
    modT = sb("modT", [128, 96], F32)
    s1T = sb("s1T", [128, KC], F32)
    s2T = sb("s2T", [128, KC], F32)
    nrmT = sb("nrmT", [128, 2, KC], F32)
    small = sb("small", [128, 64], F32)
    sc_col = sb("sc_col", [128, KC], BF16)
    RA, RB, RW = 64 * 1024, 64 * 1024, 78 * 1024
    big = sb("big", [128, (RA + RB + RW) // 4], F32)
    A_ = Arena(big, 0, RA)
    B_ = Arena(big, RA, RB)
    W = Arena(big, RA + RB, RW)

    psum = [es.enter_context(nc.psum_tensor(f"ps{i}", [128, 512], F32)) for i in range(8)]
    psum_b = [p[:].bitcast(BF16) for p in psum]

    B = P.buf
    b_ident = B("ident")
    b_modT = B("modT")
    b_s = B("s12")
    b_nrm = B("nrm")
    b_cv = B("cv")
    b_sc = B("sc")
    b_bmT = B("bmT")
    b_ps = P.bufs(8, "ps")
    for b_ in b_ps:
        b_.excl = True
    b_small = B("small")

    w_in_r = w_in_d.rearrange("(k p) c -> p k c", p=128)

    P.dma("sp", "misc0", lambda e: e.dma_start(out=ident_f[:], in_=ident_d), writes=[b_ident])
    P.op("pool", lambda e: e.tensor_copy(out=ident_b[:], in_=ident_f[:]), reads=[b_ident], writes=[b_ident])

    W.reset()
    bmT = W.alloc([96], F32)
    cvT = W.alloc([KC], F32)

    def fm_load(slot, dst, src, wb):
        P.dma("sp", slot, lambda e: e.dma_start(out=dst, in_=src.rearrange("(k p) -> p k", p=128),
                                                allow_slow_non_contiguous=True), writes=[wb])
    fm_load("misc", cvT, cvec_d, b_cv)
    fm_load("misc2", bmT, b_mod_d, b_bmT)
    fm_load("misc3", nrmT[:, 0, :], norm1_d, b_nrm)
    fm_load("misc5", nrmT[:, 1, :], norm2_d, b_nrm)

    P.op("act", lambda e: e.activation(out=small[:, 0:KC], in_=cvT, func=AF.Sigmoid), reads=[b_cv], writes=[b_small])
    P.op("dve", lambda e: e.tensor_tensor(out=sc_col[:], in0=small[:, 0:KC], in1=cvT, op=ALU.mult),
         reads=[b_small, b_cv], writes=[b_sc])

    wmr = w_mod_d.rearrange("(k p) c -> p k c", p=128)
    wpan = [W.alloc([KC, 512], BF16) for _ in range(2)]
    b_wpan = P.bufs(2, "wpan")
    ps_mod = psum[0]
    for gi, grp in enumerate((0, 1, 3, 4)):
        for sub in range(4):
            pn = grp * 4 + sub
            slot = (gi * 4 + sub) % 2
            c0 = pn * 512
            P.dma("pool", f"wpan{slot}",
                  lambda e, slot=slot, c0=c0: e.dma_start(out=wpan[slot], in_=wmr[:, :, c0:c0 + 512]),
                  writes=[b_wpan[slot]])
            for jj in range(4):
                j = pn * 4 + jj
                for k in range(KC):
                    P.op("pe", lambda e, slot=slot, jj=jj, j=j, k=k: e.matmul(
                        ps_mod[:, j:j + 1], lhsT=wpan[slot][:, k, jj * 128:(jj + 1) * 128],
                        rhs=sc_col[:, k:k + 1], start=(k == 0), stop=(k == KC - 1)),
                        reads=[b_wpan[slot], b_sc], writes=[b_ps[0]], inc=(k == KC - 1))
    for c0_, c1_ in ((0, 32), (48, 80)):
        P.op("dve", lambda e, c0_=c0_, c1_=c1_: e.tensor_tensor(out=modT[:, c0_:c1_], in0=ps_mod[:, c0_:c1_], in1=bmT[:, c0_:c1_], op=ALU.add),
             reads=[b_ps[0], b_bmT], writes=[b_modT])
    P.op("dve", lambda e: e.scalar_tensor_tensor(out=s1T[:], in0=modT[:, 16:32], scalar=1.0, in1=nrmT[:, 0, :],
                                                 op0=ALU.add, op1=ALU.mult), reads=[b_modT, b_nrm], writes=[b_s])
    P.op("dve", lambda e: e.scalar_tensor_tensor(out=s2T[:], in0=modT[:, 64:80], scalar=1.0, in1=nrmT[:, 1, :],
                                                 op0=ALU.add, op1=ALU.mult), reads=[b_modT, b_nrm], writes=[b_s])

    def mod_bcast(grp, dst, b_dst, pan, b_pan, bmb, b_bmb, pbanks, sc_bc, b_scbc):
        P.op("dve", lambda e: e.tensor_copy(out=sc_bc, in_=sc_col[:].unsqueeze(2).to_broadcast([128, KC, 128])),
             reads=[b_sc], writes=[b_scbc])
        for sub in range(4):
            pn = grp * 4 + sub
            slot = sub % 2
            c0 = pn * 512
            pb = pbanks[sub % 2]
            P.dma("pool", f"wpan{slot}",
                  lambda e, slot=slot, c0=c0: e.dma_start(out=pan[slot], in_=wmr[:, :, c0:c0 + 512]),
                  writes=[b_pan[slot]])
            for k in range(KC):
                P.op("pe", lambda e, slot=slot, k=k, pb=pb: e.matmul(
                    psum[pb][:], lhsT=sc_bc[:, k, :], rhs=pan[slot][:, k, :],
                    start=(k == 0), stop=(k == KC - 1)),
                    reads=[b_pan[slot], b_scbc], writes=[b_ps[pb]], inc=(k == KC - 1))
            P.dma("sp", "bmbc", lambda e, c0=c0: e.dma_start(
                out=bmb, in_=b_mod_d[c0:c0 + 512].partition_broadcast(128)), writes=[b_bmb])
            P.op("dve", lambda e, pb=pb, sub=sub: e.tensor_tensor(
                out=dst[:, sub * 512:(sub + 1) * 512], in0=psum[pb][:], in1=bmb, op=ALU.add),
                reads=[b_ps[pb], b_bmb], writes=[b_dst])

    def norm_to_featmajor(src_fn, nblk, sT, shT_cols, dst, dst_col0, dst_bufs, xblk, b_xblk, junk, b_junk):
        for b in range(nblk):
            slot = b % 2
            P.dma("sp", f"xblk{slot}", lambda e, slot=slot, b=b: e.dma_start(out=xblk[slot], in_=src_fn(b)),
                  writes=[b_xblk[slot]])
            ssq = small[:, 32 + slot:33 + slot]
            P.op("act", lambda e, slot=slot, ssq=ssq: e.activation(out=junk, in_=xblk[slot], func=AF.Square,
                                                                    accum_out=ssq),
                 reads=[b_xblk[slot]], writes=[b_junk, b_small])
            P.op("act", lambda e, ssq=ssq: e.activation(out=ssq, in_=ssq, func=AF.Ln, scale=1.0 / D, bias=EPS),
                 reads=[b_small], writes=[b_small])
            P.op("act", lambda e, ssq=ssq: e.activation(out=ssq, in_=ssq, func=AF.Exp, scale=-0.5),
                 reads=[b_small], writes=[b_small])
            P.op("dve", lambda e, slot=slot, ssq=ssq: e.tensor_scalar(out=xblk[slot], in0=xblk[slot], scalar1=ssq,
                                                                       scalar2=None, op0=ALU.mult),
                 reads=[b_small, b_xblk[slot]], writes=[b_xblk[slot]])
            for k4 in range(4):
                pb = 4 + (k4 % 4)
                for kk in range(4):
                    k = k4 * 4 + kk
                    P.op("pe", lambda e, slot=slot, k=k, kk=kk, pb=pb: e.transpose(
                        out=psum[pb][:, kk * 128:(kk + 1) * 128], in_=xblk[slot][:, k * 128:(k + 1) * 128],
                        identity=ident_f[:]), reads=[b_xblk[slot], b_ident], writes=[b_ps[pb]], inc=(kk == 3))
                for kk in range(4):
                    k = k4 * 4 + kk
                    c0 = dst_col0 + b * 128
                    if k4 % 2 == 0:
                        P.op("act", lambda e, k=k, kk=kk, pb=pb, c0=c0: e.activation(
                            out=dst[:, k, c0:c0 + 128], in_=psum[pb][:, kk * 128:(kk + 1) * 128],
                            func=AF.Identity, scale=sT[:, k:k + 1], bias=modT[:, shT_cols + k:shT_cols + k + 1]),
                            reads=[b_ps[pb], b_s, b_modT], writes=[dst_bufs[b]])
                    else:
                        P.op("dve", lambda e, k=k, kk=kk, pb=pb, c0=c0: e.tensor_scalar(
                            out=dst[:, k, c0:c0 + 128], in0=psum[pb][:, kk * 128:(kk + 1) * 128],
                            scalar1=sT[:, k:k + 1], scalar2=modT[:, shT_cols + k:shT_cols + k + 1],
                            op0=ALU.mult, op1=ALU.add),
                            reads=[b_ps[pb], b_s, b_modT], writes=[dst_bufs[b]])

    P.barrier()
    W.reset()
    A_.reset()
    hT = A_.alloc([KC, T], BF16)
    b_hT = [B(f"hT{b}") for b in range(NB)]
    xblk = [W.alloc([D], F32) for _ in range(2)]
    b_xblk = P.bufs(2, "xblk")
    junk = W.alloc([D], BF16)
    b_junk = B("junk")
    norm_to_featmajor(lambda b: x_d[b * 128:(b + 1) * 128, :], NB, s1T, 0, hT, 0, b_hT, xblk, b_xblk, junk, b_junk)
    P.barrier()
    B_.reset()
    yT = B_.alloc([16, T], BF16)
    b_yT = [[B(f"yT{c}_{b}") for b in range(NB)] for c in range(16)]
    if upto >= 2:
        W.reset()
        Vpan = W.alloc([KC, 256], BF16)
        KQpan = [W.alloc([KC, 128], BF16) for _ in range(4)]
        Vext = W.alloc([18, 2, 130], BF16)
        KT = W.alloc([2304], BF16)
        cs = [W.alloc([2, 512], F32) for _ in range(2)]
        xf = [W.alloc([512], F32) for _ in range(2)]
        t1 = W.alloc([512], F32)
        t2 = W.alloc([512], F32)
        P12 = [W.alloc([512], BF16) for _ in range(4)]
        Qpad = [W.alloc([512], BF16) for _ in range(4)]
        kst = [W.alloc([4, 128], F32) for _ in range(2)]
        vst = [W.alloc([2, 128], F32) for _ in range(2)]
        osb = [W.alloc([128], F32) for _ in range(2)]
        ybt = [W.alloc([128], BF16) for _ in range(2)]
        sm = [W.alloc([8], F32) for _ in range(2)]
        ajunk = W.alloc([128], BF16)
        obuf = [W.alloc([2, 258], F32) for _ in range(2)]
        ckf = W.alloc([2, 128], F32)
        abias = W.alloc([NSEG * 18], F32)
        lamv = W.alloc([4, 64], F32)
        lamt = W.alloc([2, 64], F32)
        lams = W.alloc([4], F32)
        dn8 = W.alloc([128], F32)
        rrot = W.alloc([128], F32)

        b_Vpan = B("Vpan")
        b_KQpan = P.bufs(4, "KQpan")
        b_Vext = [B(f"Vext{b}") for b in range(18)]
        b_Vone = B("Vone")
        b_KT = [B(f"KT{i}") for i in range(5)]
        b_cs = P.bufs(2, "cs")
        b_xf = P.bufs(2, "xf")
        b_t1 = B("t1")
        b_t2 = B("t2")
        b_P12 = P.bufs(4, "P12")
        b_Qpad = P.bufs(4, "Qpad")
        b_kst = P.bufs(2, "kst")
        b_vst = P.bufs(2, "vst")
        b_osb = P.bufs(2, "osb")
        b_ybt = P.bufs(2, "ybt")
        b_sm = P.bufs(2, "sm")
        b_ajunk = B("ajunk")
        b_obuf = P.bufs(2, "obuf")
        b_ckf = B("ckf")
        b_const = B("aconst")

        P.tag = 'aconst'
        P.dma("sp", "ac0", lambda e: e.dma_start(out=abias, in_=abias_d), writes=[b_const])
        P.dma("sp", "ac1", lambda e: e.dma_start(out=lamv, in_=lamv_d.partition_broadcast(128)), writes=[b_const])
        P.dma("sp", "ac2", lambda e: e.dma_start(out=dn8, in_=dnorm_d.partition_broadcast(128)), writes=[b_const])
        P.dma("sp", "ac3", lambda e: e.dma_start(out=rrot, in_=rrot_d), writes=[b_const])
        P.op("dve", lambda e: e.tensor_scalar(out=dn8, in0=dn8, scalar1=1.0 - LAMBDA_INIT, scalar2=None, op0=ALU.mult),
             reads=[b_const], writes=[b_const])
        P.tag = 'lam'
        P.op("dve", lambda e: e.tensor_tensor(out=lamt[:, 0, :], in0=lamv[:, 0, :], in1=lamv[:, 1, :], op=ALU.mult),
             reads=[b_const], writes=[b_const])
        P.op("dve", lambda e: e.tensor_tensor(out=lamt[:, 1, :], in0=lamv[:, 2, :], in1=lamv[:, 3, :], op=ALU.mult),
             reads=[b_const], writes=[b_const])
        P.op("dve", lambda e: e.tensor_reduce(out=lams[:, 0:2], in_=lamt, axis=AX.X, op=ALU.add),
             reads=[b_const], writes=[b_const])
        P.op("act", lambda e: e.activation(out=lams[:, 0:2], in_=lams[:, 0:2], func=AF.Exp),
             reads=[b_const], writes=[b_const])
        P.op("dve", lambda e: e.tensor_tensor(out=lams[:, 2:3], in0=lams[:, 1:2], in1=lams[:, 0:1], op=ALU.subtract),
             reads=[b_const], writes=[b_const])
        P.op("dve", lambda e: e.tensor_scalar(out=lams[:, 3:4], in0=lams[:, 2:3], scalar1=-LAMBDA_INIT, scalar2=None,
                                              op0=ALU.add), reads=[b_const], writes=[b_const])
        neg_lam = lams[:, 3:4]
        P.tag = 'amemset'
        P.op("pool", lambda e: e.memset(Vext[:, :, :, 128:129], 1.0), writes=[b_Vone])
        for q in range(4):
            P.op("pool", lambda e, q=q: e.memset(Qpad[q], 0.0), writes=[b_Qpad[q]])

        rope_ctr = [0]

        def rope(ps_idx, tt, writer):
            i = rope_ctr[0]
            rope_ctr[0] += 1
            s_ = i % 2
            P.dma("pool", f"cs{s_}", lambda e: e.dma_start(out=cs[s_][:, 0, :], in_=cosT_d[:, tt * 512:(tt + 1) * 512]),
                  writes=[b_cs[s_]])
            P.dma("pool", f"cs{s_}", lambda e: e.dma_start(out=cs[s_][:, 1, :], in_=sinT_d[:, tt * 512:(tt + 1) * 512]),
                  writes=[b_cs[s_]])
            P.op("dve", lambda e: e.tensor_copy(out=xf[s_], in_=psum[ps_idx][:]), reads=[b_ps[ps_idx]], writes=[b_xf[s_]])
            P.op("pe", lambda e: e.matmul(psum[7][:], lhsT=rrot, rhs=xf[s_], start=True, stop=True),
                 reads=[b_xf[s_], b_const], writes=[b_ps[7]])
            P.op("dve", lambda e: e.tensor_tensor(out=t1, in0=xf[s_], in1=cs[s_][:, 0, :], op=ALU.mult),
                 reads=[b_xf[s_], b_cs[s_]], writes=[b_t1])
            P.op("dve", lambda e: e.tensor_tensor(out=t2, in0=psum[7][:], in1=cs[s_][:, 1, :], op=ALU.mult),
                 reads=[b_ps[7], b_cs[s_]], writes=[b_t2])
            writer(s_)

        import os
        ATT_HG = int(os.environ.get('ATT_HG', '4'))
        ATT_STOP = os.environ.get('ATT_STOP', 'full')
        for hg in range(ATT_HG):
            c0 = OFF_VB + hg * 256
            P.tag = 'vproj'
            P.dma("pool", "Vpan", lambda e, c0=c0: e.dma_start(out=Vpan, in_=w_in_r[:, :, c0:c0 + 256]), writes=[b_Vpan])
            for b in range(NB):
                pb = 4 + b % 2
                for k in range(KC):
                    P.op("pe", lambda e, b=b, k=k, pb=pb: e.matmul(
                        psum[pb][:, 0:256], lhsT=hT[:, k, b * 128:(b + 1) * 128], rhs=Vpan[:, k, :],
                        start=(k == 0), stop=(k == KC - 1)), reads=[b_hT[b], b_Vpan], writes=[b_ps[pb]], inc=(k == KC - 1))
                s_ = b % 2
                P.tag = 'vevac'
                P.op("dve", lambda e, b=b, pb=pb: e.tensor_copy(
                    out=Vext[:, b, :, 0:128], in_=psum[pb][:, 0:256].rearrange("p (h d) -> p h d", d=128)),
                    reads=[b_ps[pb]], writes=[b_Vext[b]])
                P.tag = 'vst'
                P.op("dve", lambda e, s_=s_, pb=pb: e.tensor_copy(
                    out=vst[s_], in_=psum[pb][:, 0:256].rearrange("p (h d) -> p h d", d=128)),
                    reads=[b_ps[pb]], writes=[b_vst[s_]])
                seg, pos0 = b // 2, (b % 2) * 128
                P.tag = 'nvdma'
                P.dma("sp", f"vst{s_}", lambda e, s_=s_, seg=seg, pos0=pos0, hg=hg: e.dma_start(
                    out=nv_d[seg, 2 * hg:2 * hg + 2, pos0:pos0 + 128, :].rearrange("h p d -> p h d"), in_=vst[s_]),
                    reads=[b_vst[s_]])
                P.tag = 'vproj'
            P.tag = 'cvdma'
            for hl in range(2):
                for bb in range(2):
                    P.dma("pool", f"cv{hl}{bb}", lambda e, hl=hl, bb=bb, hg=hg: e.dma_start(
                        out=Vext[:, 16 + bb, hl, 0:128], in_=cv_d[2 * hg + hl, bb * 128:(bb + 1) * 128, :]),
                        writes=[b_Vext[16 + bb]])
            P.tag = 'att'
            for hl in range(2):
                if ATT_STOP == 'v':
                    break
                h = hg * 2 + hl
                Kp, Qp = KQpan[2 * hl], KQpan[2 * hl + 1]
                bKp, bQp = b_KQpan[2 * hl], b_KQpan[2 * hl + 1]
                ck0 = OFF_KB + h * 128
                cq0 = OFF_QB + h * 128
                P.dma("pool", f"KQ{2 * hl}", lambda e, Kp=Kp, ck0=ck0: e.dma_start(out=Kp, in_=w_in_r[:, :, ck0:ck0 + 128]),
                      writes=[bKp])
                P.dma("pool", f"KQ{2 * hl + 1}", lambda e, Qp=Qp, cq0=cq0: e.dma_start(out=Qp, in_=w_in_r[:, :, cq0:cq0 + 128]),
                      writes=[bQp])
                for tt in range(4):
                    pb = 4 + tt % 2
                    for k in range(KC):
                        P.op("pe", lambda e, tt=tt, k=k, pb=pb, Kp=Kp: e.matmul(
                            psum[pb][:], lhsT=Kp[:, k, :], rhs=hT[:, k, tt * 512:(tt + 1) * 512],
                            start=(k == 0), stop=(k == KC - 1)),
                            reads=[bKp] + b_hT[4 * tt:4 * tt + 4], writes=[b_ps[pb]], inc=(k == KC - 1))

                    def kwriter(s_, tt=tt, h=h):
                        P.op("dve", lambda e: e.tensor_tensor(out=KT[:, tt * 512:(tt + 1) * 512], in0=t1, in1=t2, op=ALU.add),
                             reads=[b_t1, b_t2], writes=[b_KT[tt]])
                        for j in range(4):
                            P.op("pe", lambda e, j=j: e.transpose(out=psum[7][:, j * 128:(j + 1) * 128],
                                                                  in_=xf[s_][:, j * 128:(j + 1) * 128], identity=ident_f[:]),
                                 reads=[b_xf[s_], b_ident], writes=[b_ps[7]], inc=(j == 3))
                        ks = tt % 2
                        P.op("dve", lambda e: e.tensor_copy(out=kst[ks], in_=psum[7][:].rearrange("p (j d) -> p j d", d=128)),
                             reads=[b_ps[7]], writes=[b_kst[ks]])
                        for sg in range(2):
                            seg = 2 * tt + sg
                            P.dma("sp", f"kst{ks}", lambda e, sg=sg, seg=seg: e.dma_start(
                                out=nk_d[seg, h, :, :].rearrange("(b p) d -> p b d", p=128),
                                in_=kst[ks][:, 2 * sg:2 * sg + 2, :]), reads=[b_kst[ks]])
                    rope(pb, tt, kwriter)
                P.dma("pool", "ckf", lambda e, h=h: e.dma_start(out=ckf, in_=ck_d[h].rearrange("(b p) d -> p b d", p=128)),
                      writes=[b_ckf])
                for bb in range(2):
                    P.op("pe", lambda e, bb=bb: e.transpose(out=psum[7][:, bb * 128:(bb + 1) * 128], in_=ckf[:, bb, :],
                                                            identity=ident_f[:]),
                         reads=[b_ckf, b_ident], writes=[b_ps[7]], inc=(bb == 1))
                P.op("dve", lambda e: e.tensor_copy(out=KT[:, 2048:2304], in_=psum[7][:, 0:256]),
                     reads=[b_ps[7]], writes=[b_KT[4]])

                if ATT_STOP == 'k':
                    continue
                def qproj(tt, h=h, Qp=Qp, bQp=bQp):
                    pb = 4
                    for k in range(KC):
                        P.op("pe", lambda e, k=k: e.matmul(
                            psum[pb][:], lhsT=Qp[:, k, :], rhs=hT[:, k, tt * 512:(tt + 1) * 512],
                            start=(k == 0), stop=(k == KC - 1)),
                            reads=[bQp] + b_hT[4 * tt:4 * tt + 4], writes=[b_ps[pb]], inc=(k == KC - 1))

                    def qwriter(s_):
                        for sg in range(2):
                            q = (tt % 2) * 2 + sg
                            P.op("dve", lambda e, q=q, sg=sg: e.tensor_tensor(
                                out=Qpad[q][0:64, 0:256], in0=t1[0:64, sg * 256:(sg + 1) * 256],
                                in1=t2[0:64, sg * 256:(sg + 1) * 256], op=ALU.add),
                                reads=[b_t1, b_t2], writes=[b_Qpad[q]])
                            P.op("dve", lambda e, q=q, sg=sg: e.tensor_tensor(
                                out=Qpad[q][64:128, 256:512], in0=t1[64:128, sg * 256:(sg + 1) * 256],
                                in1=t2[64:128, sg * 256:(sg + 1) * 256], op=ALU.add),
                                reads=[b_t1, b_t2], writes=[b_Qpad[q]])
                    rope(pb, tt, qwriter)

                SB = (0, 1, 5, 6)

                def smm(seg, kb, q):
                    sbk = SB[kb % 4]
                    P.op("pe", lambda e: e.matmul(psum[sbk][:], lhsT=KT[:, kb * 128:(kb + 1) * 128], rhs=Qpad[q],
                                                  start=True, stop=True),
                         reads=[b_KT[min(kb // 4, 4)], b_Qpad[q]], writes=[b_ps[sbk]])

                def attend(seg, mid_hook=None, h=h, hl=hl):
                    tt, sg = seg // 2, seg % 2
                    q = (tt % 2) * 2 + sg
                    smm(seg, 0, q)
                    smm(seg, 1, q)
                    smm(seg, 2, q)
                    for kb in range(18):
                        sbk = SB[kb % 4]
                        pj = kb % 4
                        P.op("act", lambda e, kb=kb, sbk=sbk, pj=pj: e.activation(
                            out=P12[pj], in_=psum[sbk][:], func=AF.Exp, scale=0.125,
                            bias=abias[:, seg * 18 + kb:seg * 18 + kb + 1]),
                            reads=[b_ps[sbk], b_const], writes=[b_P12[pj]])
                        for i in range(2):
                            for qb in range(2):
                                P.op("pe", lambda e, kb=kb, pj=pj, i=i, qb=qb: e.matmul(
                                    psum[2 + i][:, qb * 129:(qb + 1) * 129],
                                    lhsT=P12[pj][:, i * 256 + qb * 128:i * 256 + (qb + 1) * 128],
                                    rhs=Vext[:, kb, hl, 0:129], start=(kb == 0 and qb == 0), stop=(kb == 17),
                                    skip_group_check=True),
                                    reads=[b_P12[pj], b_Vext[kb], b_Vone], writes=[b_ps[2 + i]],
                                    inc=(i == 1 and qb == 1))
                        if kb + 3 < 18:
                            smm(seg, kb + 3, q)
                        if kb == 5 and mid_hook is not None:
                            mid_hook()
                    par = seg % 2
                    P.op("dve", lambda e, par=par: e.tensor_copy(out=obuf[par][:, 0, :], in_=psum[2][:, 0:258]),
                         reads=[b_ps[2]], writes=[b_obuf[par]])
                    P.op("dve", lambda e, par=par: e.tensor_copy(out=obuf[par][:, 1, :], in_=psum[3][:, 0:258]),
                         reads=[b_ps[3]], writes=[b_obuf[par]])

                def finalize(seg, h=h):
                    par = seg % 2
                    for qb in range(2):
                        f = qb % 2
                        O1 = obuf[par][:, 0, qb * 129:(qb + 1) * 129]
                        O2 = obuf[par][:, 1, qb * 129:(qb + 1) * 129]
                        smf = sm[f]
                        P.op("dve", lambda e, O1=O1, smf=smf: e.reciprocal(out=smf[:, 0:1], in_=O1[:, 128:129]),
                             reads=[b_obuf[par]], writes=[b_sm[f]])
                        P.op("dve", lambda e, O2=O2, smf=smf: e.reciprocal(out=smf[:, 1:2], in_=O2[:, 128:129]),
                             reads=[b_obuf[par]], writes=[b_sm[f]])
                        P.op("dve", lambda e, smf=smf: e.tensor_tensor(out=smf[:, 2:3], in0=smf[:, 1:2], in1=neg_lam, op=ALU.mult),
                             reads=[b_sm[f], b_const], writes=[b_sm[f]])
                        P.op("dve", lambda e, O1=O1, smf=smf, f=f: e.tensor_scalar(out=osb[f], in0=O1[:, 0:128], scalar1=smf[:, 0:1],
                                                                                    scalar2=None, op0=ALU.mult),
                             reads=[b_obuf[par], b_sm[f]], writes=[b_osb[f]])
                        P.op("dve", lambda e, O2=O2, smf=smf, f=f: e.scalar_tensor_tensor(
                            out=osb[f], in0=O2[:, 0:128], scalar=smf[:, 2:3], in1=osb[f], op0=ALU.mult, op1=ALU.add),
                            reads=[b_obuf[par], b_sm[f], b_osb[f]], writes=[b_osb[f]])
                        P.op("act", lambda e, smf=smf, f=f: e.activation(out=ajunk, in_=osb[f], func=AF.Square,
                                                                         accum_out=smf[:, 3:4]),
                             reads=[b_osb[f]], writes=[b_ajunk, b_sm[f]])
                        P.op("act", lambda e, smf=smf: e.activation(out=smf[:, 4:5], in_=smf[:, 3:4], func=AF.Ln,
                                                                    scale=1.0 / 128, bias=EPS),
                             reads=[b_sm[f]], writes=[b_sm[f]])
                        P.op("act", lambda e, smf=smf: e.activation(out=smf[:, 5:6], in_=smf[:, 4:5], func=AF.Exp, scale=-0.5),
                             reads=[b_sm[f]], writes=[b_sm[f]])
                        P.op("dve", lambda e, smf=smf, f=f: e.scalar_tensor_tensor(
                            out=ybt[f], in0=osb[f], scalar=smf[:, 5:6], in1=dn8, op0=ALU.mult, op1=ALU.mult),
                            reads=[b_osb[f], b_sm[f], b_const], writes=[b_ybt[f]])
                        P.op("pe", lambda e, f=f: e.transpose(out=psum_b[7][:, f * 128:(f + 1) * 128], in_=ybt[f],
                                                              identity=ident_b[:]),
                             reads=[b_ybt[f], b_ident], writes=[b_ps[7]])
                        blk = seg * 2 + qb
                        P.op("dve", lambda e, f=f, blk=blk: e.tensor_copy(
                            out=yT[:, 8 + h, blk * 128:(blk + 1) * 128], in_=psum_b[7][:, f * 128:(f + 1) * 128]),
                            reads=[b_ps[7]], writes=[b_yT[8 + h][blk]])

                qproj(0)
                for tt in range(4):
                    if tt + 1 < 4:
                        qproj(tt + 1)
                    if ATT_STOP == 'q':
                        continue
                    for seg in (2 * tt, 2 * tt + 1):
                        attend(seg, (lambda seg=seg: finalize(seg - 1)) if seg > 0 else None)
                finalize(7)
        P.barrier()
    if upto >= 3:
        P.tag = 'mlstm'
        W.reset()
        Gpan = W.alloc([KC, 32], BF16)
        Gt = W.alloc([NB, 32], F32)
        LF = W.alloc([NB, 2, 8], F32)
        IV = W.alloc([NB, 2, 8], F32)
        CS = Gt
        Et = W.alloc([NB, 16], F32)
        Ut = W.alloc([NB, 16], F32)
        EBt = W.alloc([NB, 16], F32)
        At = W.alloc([NB, 16], F32)
        bg_bc = W.alloc([32], F32)
        Umask = W.alloc([128], F32)
        Lmask = W.alloc([128], F32)
        ones_f = W.alloc([128], F32)
        SCL = W.alloc([NSEG, 16], F32)
        EM0 = W.alloc([16], F32)
        keepc = W.alloc([1], F32)
        mln_bc = W.alloc([256], F32)
        amaxc = W.alloc([2], F32)
        mrow = W.alloc([16 * 16 + 16 * 16 + 16 + 16 + NSEG * 16], F32)
        VOpan = W.alloc([KC, 256], BF16)
        QaT = W.alloc([T], BF16)
        KaTpad = [W.alloc([T], BF16) for _ in range(2)]
        Katok = W.alloc([NB, 128], BF16)
        Vaext = W.alloc([NB, 2, 130], BF16)
        Hbuf = W.alloc([NB, 2, 128], BF16)
        _hb = Hbuf.rearrange('p b h d -> p (b h d)')
        sgoT = W.alloc([2, T], BF16)
        QKpan = [_hb[:, i * 2048:(i + 1) * 2048].rearrange('p (k c) -> p k c', c=128) for i in range(2)]
        Cst2 = [W.alloc([130], F32) for _ in range(2)]
        EBp = W.alloc([NB, 2], F32)
        SCLp = W.alloc([NSEG, 2], F32)
        smd = [[W.alloc([4], F32) for _ in range(2)] for _ in range(2)]
        Cb = [[W.alloc([130], BF16) for _ in range(2)] for _ in range(2)]
        Pm8 = [[W.alloc([128], BF16) for _ in range(4)] for _ in range(2)]
        Kupad = [[W.alloc([128], BF16) for _ in range(2)] for _ in range(2)]
        stg = [W.alloc([130], F32) for _ in range(2)]
        sgtmp = W.alloc([512], F32)
        yat4 = [W.alloc([128], BF16) for _ in range(4)]
        mtmp4 = [W.alloc([128], F32) for _ in range(4)]
        msm = [W.alloc([8], F32) for _ in range(4)]
        msm8 = [[W.alloc([2], F32) for _ in range(4)] for _ in range(2)]
        mjunk = W.alloc([128], BF16)

        b_g = B("gates")
        b_mc = B("mconst")
        b_mrow = B("mrow")
        b_scl = B("scl")
        b_QKpan = P.bufs(2, "QKpan")
        b_VOpan = B("VOpan")
        b_QaT = B("QaT")
        b_KaT = P.bufs(2, "KaT")
        b_Katok = B("Katok")
        b_Va = B("Va")
        b_Hb = [[B(f"Hb{b}_{hl}") for hl in range(2)] for b in range(NB)]
        b_Cst2 = [B(f"Cst2{d_}") for d_ in range(2)]
        b_EBp = B("EBp")
        b_smd = [[B(f"smd{a_}{d_}") for d_ in range(2)] for a_ in range(2)]
        b_sgoT = B("sgoT")
        b_sgtmp = B("sgtmp")
        b_Cb = [[B(f"Cb{d_}{hl}") for hl in range(2)] for d_ in range(2)]
        b_Pm8 = [P.bufs(4, "PmA"), P.bufs(4, "PmB")]
        b_msm8 = [P.bufs(4, "msmA"), P.bufs(4, "msmB")]
        b_Ku = [[B(f"Ku{d_}{hl}") for hl in range(2)] for d_ in range(2)]
        b_stg = P.bufs(2, "stg")
        b_yat4 = P.bufs(4, "yat")
        b_mtmp4 = P.bufs(4, "mtmp")
        b_msm = P.bufs(4, "msm")
        b_mjunk = B("mjunk")

        P.dma("sp", "mc0", lambda e: e.dma_start(out=bg_bc, in_=bgates_d.partition_broadcast(128)), writes=[b_mc])
        P.dma("sp", "mc1", lambda e: e.dma_start(out=Umask, in_=umask_d), writes=[b_mc])
        P.dma("sp", "mc2", lambda e: e.dma_start(out=Lmask, in_=lmask_d), writes=[b_mc])
        P.dma("sp", "mc3", lambda e: e.dma_start(out=EM0, in_=m0_d.rearrange("a b -> (a b)").partition_broadcast(128)),
              writes=[b_mc])
        P.dma("sp", "mc4", lambda e: e.dma_start(out=keepc, in_=keep_d.partition_broadcast(128)), writes=[b_mc])
        P.op("pool", lambda e: e.memset(ones_f, 1.0), writes=[b_mc])
        P.op("pool", lambda e: e.memset(Vaext[:, :, :, 128:129], 1.0), writes=[b_Va])
        for hl in range(2):
            P.op("pool", lambda e, hl=hl: e.memset(KaTpad[hl], 0.0), writes=[b_KaT[hl]])
            for d_ in range(2):
                P.op("pool", lambda e, hl=hl, d_=d_: e.memset(Kupad[d_][hl], 0.0), writes=[b_Ku[d_][hl]])
        MR_A, MR_T, MR_M, MR_X, MR_O = 0, 256, 512, 528, 544
        P.op("pool", lambda e: e.tensor_copy(out=mrow[0:1, MR_M:MR_M + 16], in_=EM0[0:1, :]), reads=[b_mc], writes=[b_mrow])
        P.op("act", lambda e: e.activation(out=EM0, in_=EM0, func=AF.Exp), reads=[b_mc, b_mrow], writes=[b_mc])

        P.dma("pool", "Gpan", lambda e: e.dma_start(out=Gpan, in_=w_in_r[:, :, OFF_G:OFF_G + 32]), writes=[b_g])
        for b in range(NB):
            for k in range(KC):
                P.op("pe", lambda e, b=b, k=k: e.matmul(psum[4][:, b * 32:(b + 1) * 32], lhsT=hT[:, k, b * 128:(b + 1) * 128],
                                                        rhs=Gpan[:, k, :], start=(k == 0), stop=(k == KC - 1)),
                     reads=[b_hT[b], b_g], writes=[b_ps[4]], inc=(k == KC - 1))
        P.op("dve", lambda e: e.tensor_tensor(out=Gt, in0=psum[4][:].rearrange("p (b c) -> p b c", c=32),
                                              in1=bg_bc.unsqueeze(1).to_broadcast([128, NB, 32]), op=ALU.add),
             reads=[b_ps[4], b_mc], writes=[b_g])
        P.op("act", lambda e: e.activation(out=Gt, in_=Gt, func=AF.Tanh, scale=1.0 / GATE_CAP), reads=[b_g], writes=[b_g])
        G4 = Gt.rearrange("p b (t h) -> p b t h", h=8)
        P.op("dve", lambda e: e.tensor_scalar(out=IV, in0=G4[:, :, 0::2, :], scalar1=GATE_CAP, scalar2=None, op0=ALU.mult),
             reads=[b_g], writes=[b_g])
        P.op("act", lambda e: e.activation(out=LF, in_=G4[:, :, 1::2, :], func=AF.Exp, scale=-GATE_CAP), reads=[b_g], writes=[b_g])
        P.op("act", lambda e: e.activation(out=LF, in_=LF, func=AF.Ln, bias=1.0), reads=[b_g], writes=[b_g])
        P.op("dve", lambda e: e.tensor_scalar(out=LF, in0=LF, scalar1=-1.0, scalar2=None, op0=ALU.mult), reads=[b_g], writes=[b_g])
        for b in range(NB):
            P.op("pe", lambda e, b=b: e.matmul(psum[5][:, b * 32:b * 32 + 8], lhsT=Umask, rhs=LF[:, b, 0, :], start=True, stop=True),
                 reads=[b_g, b_mc], writes=[b_ps[5]], inc=False)
            P.op("pe", lambda e, b=b: e.matmul(psum[5][:, b * 32 + 8:b * 32 + 16], lhsT=Lmask, rhs=LF[:, b, 1, :], start=True, stop=True),
                 reads=[b_g, b_mc], writes=[b_ps[5]], inc=False)
            P.op("pe", lambda e, b=b: e.matmul(psum[5][:, b * 32 + 16:b * 32 + 32], lhsT=ones_f,
                                               rhs=LF[:, b, :, :].rearrange("p t h -> p (t h)"), start=True, stop=True),
                 reads=[b_g, b_mc], writes=[b_ps[5]], inc=True)
        P.op("dve", lambda e: e.tensor_copy(out=CS, in_=psum[5][:].rearrange("p (b c) -> p b c", c=32)),
             reads=[b_ps[5]], writes=[b_g])
        IV16 = IV.rearrange("p b t h -> p b (t h)")
        P.op("act", lambda e: e.activation(out=Et, in_=CS[:, :, 0:16], func=AF.Exp), reads=[b_g], writes=[b_g])
        P.op("act", lambda e: e.activation(out=EBt, in_=CS[:, :, 16:32], func=AF.Exp), reads=[b_g], writes=[b_g])
        P.op("dve", lambda e: e.tensor_tensor(out=At, in0=IV16, in1=CS[:, :, 0:16], op=ALU.subtract), reads=[b_g], writes=[b_g])
        P.op("act", lambda e: e.activation(out=Ut, in_=At, func=AF.Exp), reads=[b_g], writes=[b_g])

        Aflat = At.rearrange("p b c -> p (b c)")
        for half in range(2):
            P.op("pe", lambda e, half=half: e.transpose(out=psum[6][:, 0:128], in_=Aflat[:, half * 128:(half + 1) * 128],
                                                        identity=ident_f[:]), reads=[b_g, b_ident], writes=[b_ps[6]])
            P.op("dve", lambda e, half=half: e.tensor_reduce(out=amaxc[:, half:half + 1], in_=psum[6][:, 0:128], axis=AX.X, op=ALU.max),
                 reads=[b_ps[6]], writes=[b_mc])
        for half in range(2):
            P.op("pe", lambda e, half=half: e.transpose(out=psum[6][0:1, half * 128:(half + 1) * 128], in_=amaxc[:, half:half + 1],
                                                        identity=ident_f[:]), reads=[b_mc, b_ident], writes=[b_ps[6]])
        P.op("dve", lambda e: e.tensor_copy(out=mrow[0:1, MR_A:MR_A + 256], in_=psum[6][0:1, 0:256]), reads=[b_ps[6]], writes=[b_mrow])
        P.op("dve", lambda e: e.tensor_copy(out=mrow[0:1, MR_T:MR_T + 256].rearrange("p (b c) -> p b c", c=16),
                                            in_=CS[0:1, :, 16:32]), reads=[b_g], writes=[b_mrow])

        def mr(off, n=8):
            return mrow[0:1, off:off + n]
        for d_ in range(2):
            order = list(range(NB)) if d_ == 0 else list(range(NB - 1, -1, -1))
            mcur = mr(MR_M + d_ * 8)
            for idx, blk in enumerate(order):
                seg = blk // 2
                first_of_seg = (blk % 2 == 0) if d_ == 0 else (blk % 2 == 1)
                if first_of_seg and idx > 0:
                    P.op("dve", lambda e, mcur=mcur: e.tensor_scalar(out=mcur, in0=mcur, scalar1=keepc[0:1, 0:1], scalar2=None,
                                                                      op0=ALU.mult), reads=[b_mrow, b_mc], writes=[b_mrow])
                am = mr(MR_A + blk * 16 + d_ * 8)
                tt_ = mr(MR_T + blk * 16 + d_ * 8)
                P.op("dve", lambda e, mcur=mcur, am=am: e.tensor_tensor(out=mcur, in0=mcur, in1=am, op=ALU.max),
                     reads=[b_mrow], writes=[b_mrow])
                P.op("dve", lambda e, mcur=mcur, tt_=tt_: e.tensor_tensor(out=mcur, in0=mcur, in1=tt_, op=ALU.add),
                     reads=[b_mrow], writes=[b_mrow])
                if not first_of_seg:
                    mo = mr(MR_O + seg * 16 + d_ * 8)
                    P.op("dve", lambda e, mcur=mcur, mo=mo: e.tensor_copy(out=mo, in_=mcur), reads=[b_mrow], writes=[b_mrow])
        P.dma("sp", "nm", lambda e: e.dma_start(out=nm_d.rearrange("s a h -> (s a h)").rearrange("(o n) -> o n", o=1),
                                                in_=mrow[0:1, MR_O:MR_O + NSEG * 16]), reads=[b_mrow])
        P.op("pe", lambda e: e.matmul(psum[6][:, 0:128], lhsT=ones_f[0:1, :], rhs=mrow[0:1, MR_O:MR_O + 128], start=True, stop=True),
             reads=[b_mrow, b_mc], writes=[b_ps[6]])
        P.op("act", lambda e: e.activation(out=SCL.rearrange("p s c -> p (s c)"), in_=psum[6][:, 0:128], func=AF.Exp, scale=-1.0),
             reads=[b_ps[6]], writes=[b_scl])

        import os
        NHP = int(os.environ.get('M_HP', '4'))
        for hp in range(NHP):
            P.barrier()
            P.dma("pool", "mc5", lambda e, hp=hp: e.dma_start(out=mln_bc, in_=mlnorm_d[hp * 256:(hp + 1) * 256].partition_broadcast(128)),
                  writes=[b_mc])
            P.dma("pool", "QKpan0", lambda e, hp=hp: e.dma_start(out=QKpan[0], in_=w_in_r[:, :, OFF_QA + hp * 128:OFF_QA + (hp + 1) * 128]),
                  writes=[b_QKpan[0]])
            P.dma("pool", "QKpan1", lambda e, hp=hp: e.dma_start(out=QKpan[1], in_=w_in_r[:, :, OFF_KA + hp * 128:OFF_KA + (hp + 1) * 128]),
                  writes=[b_QKpan[1]])
            P.dma("pool", "VOpan", lambda e, hp=hp: e.dma_start(out=VOpan, in_=w_in_r[:, :, OFF_VA + hp * 256:OFF_VA + (hp + 1) * 256]),
                  writes=[b_VOpan])
            for which in range(2):
                for tt in range(4):
                    pb = 4 + tt % 2
                    for k in range(KC):
                        P.op("pe", lambda e, which=which, tt=tt, k=k, pb=pb: e.matmul(
                            psum[pb][:], lhsT=QKpan[which][:, k, :], rhs=hT[:, k, tt * 512:(tt + 1) * 512],
                            start=(k == 0), stop=(k == KC - 1)),
                            reads=[b_QKpan[which]] + b_hT[4 * tt:4 * tt + 4], writes=[b_ps[pb]], inc=(k == KC - 1))
                    if which == 0:
                        P.op("act", lambda e, tt=tt, pb=pb: e.activation(out=QaT[:, tt * 512:(tt + 1) * 512], in_=psum[pb][:],
                                                                         func=AF.Copy, scale=0.125),
                             reads=[b_ps[pb]], writes=[b_QaT])
                    else:
                        P.op("act", lambda e, tt=tt, pb=pb: e.copy(out=KaTpad[0][0:64, tt * 512:(tt + 1) * 512], in_=psum[pb][0:64, :]),
                             reads=[b_ps[pb]], writes=[b_KaT[0]])
                        P.op("dve", lambda e, tt=tt, pb=pb: e.tensor_copy(out=KaTpad[1][64:128, tt * 512:(tt + 1) * 512],
                                                                          in_=psum[pb][64:128, :]),
                             reads=[b_ps[pb]], writes=[b_KaT[1]])
            for b in range(NB):
                for hl in range(2):
                    P.op("pe", lambda e, b=b, hl=hl: e.transpose(out=psum_b[7][:, hl * 128:(hl + 1) * 128],
                                                                 in_=KaTpad[hl][:, b * 128:(b + 1) * 128], identity=ident_b[:]),
                         reads=[b_KaT[hl], b_ident], writes=[b_ps[7]], inc=(hl == 1))
                for hl in range(2):
                    P.op("dve", lambda e, b=b, hl=hl: e.tensor_copy(out=Katok[:, b, hl * 64:(hl + 1) * 64],
                                                                    in_=psum_b[7][:, hl * 128 + hl * 64:hl * 128 + (hl + 1) * 64]),
                         reads=[b_ps[7]], writes=[b_Katok])
            for b in range(NB):
                pb = 4 + b % 2
                for k in range(KC):
                    P.op("pe", lambda e, b=b, k=k, pb=pb: e.matmul(psum[pb][:, 0:256], lhsT=hT[:, k, b * 128:(b + 1) * 128],
                                                                   rhs=VOpan[:, k, :], start=(k == 0), stop=(k == KC - 1)),
                         reads=[b_hT[b], b_VOpan], writes=[b_ps[pb]], inc=(k == KC - 1))
                P.op("act", lambda e, b=b, pb=pb: e.copy(out=Vaext[:, b, :, 0:128],
                                                         in_=psum[pb][:, 0:256].rearrange("p (h d) -> p h d", d=128)),
                     reads=[b_ps[pb]], writes=[b_Va])
            P.barrier()
            P.dma("pool", "VOpan", lambda e, hp=hp: e.dma_start(out=VOpan, in_=w_in_r[:, :, OFF_OA + hp * 256:OFF_OA + (hp + 1) * 256]),
                  writes=[b_VOpan])
            for cc in range(2):
                for tt in range(4):
                    pb = 4 + tt % 2
                    for k in range(KC):
                        P.op("pe", lambda e, cc=cc, tt=tt, k=k, pb=pb: e.matmul(
                            psum[pb][:], lhsT=VOpan[:, k, cc * 128:(cc + 1) * 128], rhs=hT[:, k, tt * 512:(tt + 1) * 512],
                            start=(k == 0), stop=(k == KC - 1)),
                            reads=[b_VOpan] + b_hT[4 * tt:4 * tt + 4], writes=[b_ps[pb]], inc=(k == KC - 1))
                    P.op("act", lambda e, pb=pb: e.activation(out=sgtmp, in_=psum[pb][:], func=AF.Exp, scale=-1.0),
                         reads=[b_ps[pb]], writes=[b_sgtmp])
                    P.op("dve", lambda e: e.tensor_scalar(out=sgtmp, in0=sgtmp, scalar1=1.0, scalar2=None, op0=ALU.add),
                         reads=[b_sgtmp], writes=[b_sgtmp])
                    P.op("dve", lambda e: e.reciprocal(out=sgtmp, in_=sgtmp), reads=[b_sgtmp], writes=[b_sgtmp])
                    P.op("act", lambda e, cc=cc, tt=tt: e.copy(out=sgoT[:, cc, tt * 512:(tt + 1) * 512], in_=sgtmp),
                         reads=[b_sgtmp], writes=[b_sgoT])
            for d_ in range(2):
                for hl in range(2):
                    rows = slice(hl * 64, (hl + 1) * 64)
                    dh = d_ * 8 + 2 * hp + hl
                    P.op("dve", lambda e, d_=d_, rows=rows, dh=dh: e.tensor_copy(out=EBp[rows, :, d_:d_ + 1], in_=EBt[rows, :, dh:dh + 1]),
                         reads=[b_g], writes=[b_EBp])
                    P.op("dve", lambda e, d_=d_, rows=rows, dh=dh: e.tensor_copy(out=SCLp[rows, :, d_:d_ + 1], in_=SCL[rows, :, dh:dh + 1]),
                         reads=[b_scl], writes=[b_EBp])
            for d_ in range(2):
                P.op("pool", lambda e, d_=d_: e.memset(Cst2[d_], 0.0), writes=[b_Cst2[d_]])
                for hl in range(2):
                    h = 2 * hp + hl
                    dh = d_ * 8 + h
                    rows = slice(hl * 64, (hl + 1) * 64)
                    P.op("pool", lambda e, d_=d_, hl=hl: e.memset(Cb[d_][hl], 0.0), writes=[b_Cb[d_][hl]])
                    P.dma("pool", f"c0{d_}{hl}", lambda e, d_=d_, h=h, rows=rows: e.dma_start(
                        out=Cst2[d_][rows, 0:128], in_=c0_d[d_, h]), writes=[b_Cst2[d_]])
                    P.dma("pool", f"n0{d_}{hl}", lambda e, d_=d_, h=h, rows=rows: e.dma_start(
                        out=Cst2[d_][rows, 128:129], in_=n0_d[d_, h].rearrange("(p o) -> p o", o=1)), writes=[b_Cst2[d_]])
                    P.op("dve", lambda e, d_=d_, rows=rows, dh=dh: e.tensor_scalar(
                        out=Cst2[d_][rows, 0:129], in0=Cst2[d_][rows, 0:129], scalar1=EM0[rows, dh:dh + 1], scalar2=None,
                        op0=ALU.mult), reads=[b_Cst2[d_], b_mc], writes=[b_Cst2[d_]])
                    P.op("act", lambda e, d_=d_, hl=hl, rows=rows: e.copy(out=Cb[d_][hl][rows, 0:129], in_=Cst2[d_][rows, 0:129]),
                         reads=[b_Cst2[d_]], writes=[b_Cb[d_][hl]])
            for j in range(NB):
                units = []
                for d_ in range(2):
                    blk = j if d_ == 0 else NB - 1 - j
                    for hl in range(2):
                        units.append((d_ * 2 + hl, d_, hl, blk))
                jp = j % 2
                sbk = jp
                nbks = (2, 3) if jp == 0 else (4, 5)
                gbks = (6, 7)
                last = (j == NB - 1)

                def geo(u, d_, hl, blk):
                    h = 2 * hp + hl
                    dh = d_ * 8 + h
                    rows = slice(hl * 64, (hl + 1) * 64)
                    cols = slice(blk * 128, (blk + 1) * 128)
                    return h, dh, rows, cols
                for (u, d_, hl, blk) in units:
                    h, dh, rows, cols = geo(u, d_, hl, blk)
                    P.op("pe", lambda e, u=u, hl=hl, cols=cols, sbk=sbk: e.matmul(
                        psum[sbk][:, u * 128:(u + 1) * 128], lhsT=KaTpad[hl][:, cols], rhs=QaT[:, cols], start=True, stop=True),
                        reads=[b_KaT[hl], b_QaT], writes=[b_ps[sbk]], inc=(u == 3))
                for (u, d_, hl, blk) in units:
                    h, dh, rows, cols = geo(u, d_, hl, blk)
                    ucol = Ut[:, blk, dh:dh + 1]
                    P.op("act", lambda e, d_=d_, hl=hl, blk=blk, ucol=ucol: e.activation(
                        out=Kupad[d_][hl][:, hl * 64:(hl + 1) * 64], in_=Katok[:, blk, hl * 64:(hl + 1) * 64],
                        func=AF.Copy, scale=ucol), reads=[b_Katok, b_g], writes=[b_Ku[d_][hl]])
                for (u, d_, hl, blk) in units:
                    h, dh, rows, cols = geo(u, d_, hl, blk)
                    ucol = Ut[:, blk, dh:dh + 1]
                    mask = Umask if d_ == 0 else Lmask
                    pm = Pm8[jp][u]
                    P.op("dve", lambda e, u=u, sbk=sbk, pm=pm, ucol=ucol, mask=mask: e.scalar_tensor_tensor(
                        out=pm, in0=psum[sbk][:, u * 128:(u + 1) * 128], scalar=ucol, in1=mask, op0=ALU.mult, op1=ALU.mult),
                        reads=[b_ps[sbk], b_g, b_mc], writes=[b_Pm8[jp][u]])
                for (u, d_, hl, blk) in units:
                    h, dh, rows, cols = geo(u, d_, hl, blk)
                    gbk = gbks[u // 2]
                    P.op("pe", lambda e, gbk=gbk, d_=d_, hl=hl, blk=blk, u=u: e.matmul(
                        psum[gbk][:, 0:129], lhsT=Kupad[d_][hl], rhs=Vaext[:, blk, hl, 0:129], start=(u % 2 == 0), stop=(u % 2 == 1),
                        skip_group_check=True),
                        reads=[b_Ku[d_][hl], b_Va], writes=[b_ps[gbk]], inc=(u % 2 == 1))
                for d_ in range(2):
                    blk = j if d_ == 0 else NB - 1 - j
                    gbk = gbks[d_]
                    Cs = Cst2[d_]
                    P.op("dve", lambda e, Cs=Cs, gbk=gbk: e.tensor_tensor(out=Cs[:, 0:129], in0=psum[gbk][:, 0:129], in1=Cs[:, 0:129], op=ALU.add),
                         reads=[b_ps[gbk], b_Cst2[d_]], writes=[b_Cst2[d_]])
                for d_ in range(2):
                    blk = j if d_ == 0 else NB - 1 - j
                    Cs = Cst2[d_]
                    P.op("dve", lambda e, Cs=Cs, blk=blk, d_=d_: e.tensor_scalar(
                        out=Cs[:, 0:129], in0=Cs[:, 0:129], scalar1=EBp[:, blk, d_:d_ + 1], scalar2=None, op0=ALU.mult),
                        reads=[b_Cst2[d_], b_EBp], writes=[b_Cst2[d_]])
                for d_ in range(2):
                    blk = j if d_ == 0 else NB - 1 - j
                    seg = blk // 2
                    seg_end = (blk % 2 == 1) if d_ == 0 else (blk % 2 == 0)
                    Cs = Cst2[d_]
                    if seg_end:
                        st = stg[d_]
                        P.op("dve", lambda e, Cs=Cs, st=st, seg=seg, d_=d_: e.tensor_scalar(
                            out=st[:, 0:129], in0=Cs[:, 0:129], scalar1=SCLp[:, seg, d_:d_ + 1], scalar2=None, op0=ALU.mult),
                            reads=[b_Cst2[d_], b_EBp], writes=[b_stg[d_]])
                        P.dma("sp", f"stg{d_}", lambda e, st=st, seg=seg, d_=d_, hp=hp: e.dma_start(
                            out=nC_d[seg, d_, 2 * hp:2 * hp + 2].rearrange("h p c -> (h p) c"), in_=st[:, 0:128]), reads=[b_stg[d_]])
                        P.dma("sp", f"stg{d_}", lambda e, st=st, seg=seg, d_=d_, hp=hp: e.dma_start(
                            out=nn_d[seg, d_, 2 * hp:2 * hp + 2].rearrange("h (p o) -> (h p) o", o=1), in_=st[:, 128:129]), reads=[b_stg[d_]])
                        if not last:
                            P.op("dve", lambda e, Cs=Cs: e.tensor_scalar(
                                out=Cs[:, 0:129], in0=Cs[:, 0:129], scalar1=keepc[:, 0:1], scalar2=None, op0=ALU.mult),
                                reads=[b_Cst2[d_], b_mc], writes=[b_Cst2[d_]])
                for (u, d_, hl, blk) in units:
                    h, dh, rows, cols = geo(u, d_, hl, blk)
                    nbk = nbks[u // 2]
                    c0 = (u % 2) * 129
                    pm = Pm8[jp][u]
                    P.op("pe", lambda e, nbk=nbk, c0=c0, pm=pm, blk=blk, hl=hl, u=u: e.matmul(
                        psum[nbk][:, c0:c0 + 129], lhsT=pm, rhs=Vaext[:, blk, hl, 0:129], start=(u % 2 == 0), stop=False,
                        skip_group_check=True),
                        reads=[b_Pm8[jp][u], b_Va], writes=[b_ps[nbk]], inc=False)
                    P.op("pe", lambda e, nbk=nbk, c0=c0, cols=cols, d_=d_, hl=hl: e.matmul(
                        psum[nbk][:, c0:c0 + 129], lhsT=QaT[:, cols], rhs=Cb[d_][hl][:, 0:129], start=False, stop=True,
                        skip_group_check=True),
                        reads=[b_QaT, b_Cb[d_][hl]], writes=[b_ps[nbk]], inc=(u % 2 == 1))
                if not last:
                    for (u, d_, hl, blk) in units:
                        h, dh, rows, cols = geo(u, d_, hl, blk)
                        Cs = Cst2[d_]
                        P.op("act", lambda e, Cs=Cs, rows=rows, d_=d_, hl=hl: e.copy(out=Cb[d_][hl][rows, 0:129], in_=Cs[rows, 0:129]),
                             reads=[b_Cst2[d_]], writes=[b_Cb[d_][hl]])
                for d_ in range(2):
                    blk = j if d_ == 0 else NB - 1 - j
                    nbk = nbks[d_]
                    sd = smd[jp][d_]
                    P.op("act", lambda e, nbk=nbk, sd=sd: e.activation(out=sd[:, 0:2], in_=psum[nbk][:, 128:258:129], func=AF.Abs),
                         reads=[b_ps[nbk]], writes=[b_smd[jp][d_]])
                for which in range(4):
                    for d_ in range(2):
                        blk = j if d_ == 0 else NB - 1 - j
                        dh0 = d_ * 8 + 2 * hp
                        E2 = Et[:, blk, dh0:dh0 + 2]
                        sd = smd[jp][d_]
                        bsd = b_smd[jp][d_]
                        if which == 0:
                            P.op("dve", lambda e, sd=sd, E2=E2: e.tensor_tensor(out=sd[:, 0:2], in0=sd[:, 0:2], in1=E2, op=ALU.mult),
                                 reads=[bsd, b_g], writes=[bsd])
                        elif which == 1:
                            P.op("dve", lambda e, sd=sd: e.tensor_scalar(out=sd[:, 0:2], in0=sd[:, 0:2], scalar1=1.0, scalar2=None, op0=ALU.max),
                                 reads=[bsd], writes=[bsd])
                        elif which == 2:
                            P.op("dve", lambda e, sd=sd: e.reciprocal(out=sd[:, 0:2], in_=sd[:, 0:2]), reads=[bsd], writes=[bsd])
                        else:
                            P.op("dve", lambda e, sd=sd, E2=E2: e.tensor_tensor(out=sd[:, 2:4], in0=sd[:, 0:2], in1=E2, op=ALU.mult),
                                 reads=[bsd, b_g], writes=[bsd])
                for (u, d_, hl, blk) in units:
                    h, dh, rows, cols = geo(u, d_, hl, blk)
                    nbk = nbks[u // 2]
                    c0 = (u % 2) * 129
                    sd = smd[jp][d_]
                    bsd = b_smd[jp][d_]
                    rcol = sd[:, 2 + hl:3 + hl]
                    if j < NB // 2:
                        P.op("act", lambda e, nbk=nbk, c0=c0, rcol=rcol, blk=blk, hl=hl: e.activation(
                            out=Hbuf[:, blk, hl, :], in_=psum[nbk][:, c0:c0 + 128], func=AF.Copy, scale=rcol),
                            reads=[b_ps[nbk], bsd], writes=[b_Hb[blk][hl]])
                    else:
                        P.op("dve", lambda e, nbk=nbk, c0=c0, rcol=rcol, blk=blk, hl=hl: e.scalar_tensor_tensor(
                            out=Hbuf[:, blk, hl, :], in0=psum[nbk][:, c0:c0 + 128], scalar=rcol, in1=Hbuf[:, blk, hl, :],
                            op0=ALU.mult, op1=ALU.add),
                            reads=[b_ps[nbk], bsd, b_Hb[blk][hl]], writes=[b_Hb[blk][hl]])
            for b in range(NB):
                bp = b % 2
                pbt = 6 + bp
                for hl in range(2):
                    h = 2 * hp + hl
                    q_ = bp * 2 + hl
                    sm_ = msm[q_]
                    bsm = b_msm[q_]
                    hs = Hbuf[:, b, hl, :]
                    ya_ = yat4[q_]
                    P.op("act", lambda e, hs=hs, sm_=sm_: e.activation(out=mjunk, in_=hs, func=AF.Square, accum_out=sm_[:, 2:3]),
                         reads=[b_Hb[b][hl]], writes=[b_mjunk, bsm])
                    P.op("act", lambda e, sm_=sm_: e.activation(out=sm_[:, 3:4], in_=sm_[:, 2:3], func=AF.Ln, scale=1.0 / 128, bias=EPS),
                         reads=[bsm], writes=[bsm])
                    P.op("act", lambda e, sm_=sm_: e.activation(out=sm_[:, 4:5], in_=sm_[:, 3:4], func=AF.Exp, scale=-0.5),
                         reads=[bsm], writes=[bsm])
                    P.op("dve", lambda e, hs=hs, sm_=sm_, hl=hl, ya_=ya_: e.scalar_tensor_tensor(
                        out=ya_, in0=hs, scalar=sm_[:, 4:5], in1=mln_bc[:, hl * 128:(hl + 1) * 128], op0=ALU.mult, op1=ALU.mult),
                        reads=[b_Hb[b][hl], bsm, b_mc], writes=[b_yat4[q_]])
                    P.op("pe", lambda e, hl=hl, ya_=ya_, pbt=pbt: e.transpose(out=psum_b[pbt][:, hl * 128:(hl + 1) * 128], in_=ya_,
                                                                              identity=ident_b[:]),
                         reads=[b_yat4[q_], b_ident], writes=[b_ps[pbt]])
                    P.op("dve", lambda e, hl=hl, h=h, b=b, pbt=pbt: e.tensor_tensor(
                        out=yT[:, h, b * 128:(b + 1) * 128], in0=psum_b[pbt][:, hl * 128:(hl + 1) * 128],
                        in1=sgoT[:, hl, b * 128:(b + 1) * 128], op=ALU.mult),
                        reads=[b_ps[pbt], b_sgoT], writes=[b_yT[h][b]])
        P.barrier()
    mT_d = nc.dram_tensor("mT_scr", [KC, 128, T], BF16).ap()
    x1_d = nc.dram_tensor("x1_scr", [T, D], F32).ap()
    w_pa_r = w_pa_d.rearrange("(k p) c -> p k c", p=128)
    w_pb_r = w_pb_d.rearrange("(k p) c -> p k c", p=128)
    w_out_r = w_out_d.rearrange("(k p) c -> p k c", p=128)
    w_up_r = w_up_d.rearrange("(k p) c -> p k c", p=128)
    w_dn_r = w_down_d.rearrange("(k p) c -> p k c", p=128)
    if upto >= 4:
        P.tag = 'ph3a'
        W.reset()
        pa_pan = [W.alloc([8, 128], BF16) for _ in range(2)]
        pb_pan = [W.alloc([8, 128], BF16) for _ in range(2)]
        ga_pan = [W.alloc([KC, 128], BF16) for _ in range(2)]
        gb_pan = [W.alloc([KC, 128], BF16) for _ in range(2)]
        sga = [W.alloc([512], F32) for _ in range(2)]
        sgb = [W.alloc([512], F32) for _ in range(2)]
        m1 = [W.alloc([512], F32) for _ in range(2)]
        m2 = [W.alloc([512], F32) for _ in range(2)]
        mTc = [W.alloc([T], BF16) for _ in range(2)]
        b_pan3 = [P.bufs(2, "pa"), P.bufs(2, "pb"), P.bufs(2, "ga"), P.bufs(2, "gb")]
        b_sga = P.bufs(2, "sga")
        b_sgb = P.bufs(2, "sgb")
        b_m1 = P.bufs(2, "m1")
        b_m2 = P.bufs(2, "m2")
        b_mTc = P.bufs(2, "mTc")
        all_hT = list(b_hT)
        for c in range(KC):
            s_ = c % 2
            cc = slice(c * 128, (c + 1) * 128)
            P.dma("pool", f"pa{s_}", lambda e, s_=s_, cc=cc: e.dma_start(out=pa_pan[s_], in_=w_pa_r[:, :, cc]), writes=[b_pan3[0][s_]])
            P.dma("pool", f"pb{s_}", lambda e, s_=s_, cc=cc: e.dma_start(out=pb_pan[s_], in_=w_pb_r[:, :, cc]), writes=[b_pan3[1][s_]])
            P.dma("pool", f"ga{s_}", lambda e, s_=s_, c=c: e.dma_start(
                out=ga_pan[s_], in_=w_in_r[:, :, OFF_GA + c * 128:OFF_GA + (c + 1) * 128]), writes=[b_pan3[2][s_]])
            P.dma("pool", f"gb{s_}", lambda e, s_=s_, c=c: e.dma_start(
                out=gb_pan[s_], in_=w_in_r[:, :, OFF_GB + c * 128:OFF_GB + (c + 1) * 128]), writes=[b_pan3[3][s_]])
            for tt in range(4):
                t_ = tt % 2
                tc_ = slice(tt * 512, (tt + 1) * 512)
                bk = [0 + t_, 2 + t_, 4 + t_, 6 + t_]
                yb_a = [b_yT[k][4 * tt + j] for k in range(8) for j in range(4)]
                yb_b = [b_yT[8 + k][4 * tt + j] for k in range(8) for j in range(4)]
                for k in range(8):
                    P.op("pe", lambda e, k=k, s_=s_, tc_=tc_, bk=bk: e.matmul(psum[bk[0]][:], lhsT=pa_pan[s_][:, k, :], rhs=yT[:, k, tc_],
                                                                             start=(k == 0), stop=(k == 7)),
                         reads=[b_pan3[0][s_]] + yb_a, writes=[b_ps[bk[0]]], inc=(k == 7))
                for k in range(KC):
                    P.op("pe", lambda e, k=k, s_=s_, tc_=tc_, bk=bk: e.matmul(psum[bk[1]][:], lhsT=ga_pan[s_][:, k, :], rhs=hT[:, k, tc_],
                                                                             start=(k == 0), stop=(k == KC - 1)),
                         reads=[b_pan3[2][s_]] + all_hT[4 * tt:4 * tt + 4], writes=[b_ps[bk[1]]], inc=(k == KC - 1))
                for k in range(8):
                    P.op("pe", lambda e, k=k, s_=s_, tc_=tc_, bk=bk: e.matmul(psum[bk[2]][:], lhsT=pb_pan[s_][:, k, :], rhs=yT[:, 8 + k, tc_],
                                                                             start=(k == 0), stop=(k == 7)),
                         reads=[b_pan3[1][s_]] + yb_b, writes=[b_ps[bk[2]]], inc=(k == 7))
                for k in range(KC):
                    P.op("pe", lambda e, k=k, s_=s_, tc_=tc_, bk=bk: e.matmul(psum[bk[3]][:], lhsT=gb_pan[s_][:, k, :], rhs=hT[:, k, tc_],
                                                                             start=(k == 0), stop=(k == KC - 1)),
                         reads=[b_pan3[3][s_]] + all_hT[4 * tt:4 * tt + 4], writes=[b_ps[bk[3]]], inc=(k == KC - 1))
                P.op("act", lambda e, t_=t_, bk=bk: e.activation(out=sga[t_], in_=psum[bk[1]][:], func=AF.Sigmoid),
                     reads=[b_ps[bk[1]]], writes=[b_sga[t_]])
                P.op("dve", lambda e, t_=t_, bk=bk: e.tensor_tensor(out=m1[t_], in0=psum[bk[0]][:], in1=sga[t_], op=ALU.mult),
                     reads=[b_ps[bk[0]], b_sga[t_]], writes=[b_m1[t_]])
                P.op("act", lambda e, t_=t_, bk=bk: e.activation(out=sgb[t_], in_=psum[bk[3]][:], func=AF.Sigmoid),
                     reads=[b_ps[bk[3]]], writes=[b_sgb[t_]])
                P.op("dve", lambda e, t_=t_, bk=bk: e.tensor_tensor(out=m2[t_], in0=psum[bk[2]][:], in1=sgb[t_], op=ALU.mult),
                     reads=[b_ps[bk[2]], b_sgb[t_]], writes=[b_m2[t_]])
                P.op("dve", lambda e, t_=t_, s_=s_, tc_=tc_: e.tensor_tensor(out=mTc[s_][:, tc_], in0=m1[t_], in1=m2[t_], op=ALU.add),
                     reads=[b_m1[t_], b_m2[t_]], writes=[b_mTc[s_]])
            P.dma("sp", f"mTc{s_}", lambda e, s_=s_, c=c: e.dma_start(out=mT_d[c], in_=mTc[s_]), reads=[b_mTc[s_]])
        P.barrier()

        P.tag = 'ph3b'
        A_.reset(); B_.reset(); W.reset()
        g1_bc = W.alloc([D], F32)
        wpan2 = [W.alloc([KC, 512], BF16) for _ in range(2)]
        b_wpan2 = P.bufs(2, "wpan2")
        bmb = W.alloc([512], F32)
        b_bmb = B("bmb")
        sc_bc = W.alloc([KC, 128], BF16)
        b_scbc = B("scbc")
        b_g1 = B("g1")
        mod_bcast(2, g1_bc, b_g1, wpan2, b_wpan2, bmb, b_bmb, (0, 1), sc_bc, b_scbc)
        mTt = A_.alloc([KC, 1024], BF16)
        b_mTt = B("mTt")
        xt = B_.alloc([8, D], F32)
        b_xt = P.bufs(8, "xt")
        x1s = [W.alloc([512], F32) for _ in range(2)]
        b_x1s = P.bufs(2, "x1s")
        ctr = 0
        for tile in range(2):
            t0 = tile * 1024
            P.dma("sp", "mTt", lambda e, t0=t0: e.dma_start(out=mTt, in_=mT_d[:, :, t0:t0 + 1024].rearrange("k p t -> p k t")),
                  writes=[b_mTt])
            for tb in range(8):
                P.dma("sp", f"xt{tb}", lambda e, t0=t0, tb=tb: e.dma_start(out=xt[:, tb, :], in_=x_d[t0 + tb * 128:t0 + (tb + 1) * 128, :]),
                      writes=[b_xt[tb]])
            for ct in range(4):
                s_ = ct % 2
                P.dma("pool", f"wpan{s_}", lambda e, s_=s_, ct=ct: e.dma_start(out=wpan2[s_], in_=w_out_r[:, :, ct * 512:(ct + 1) * 512]),
                      writes=[b_wpan2[s_]])
                for tb in range(8):
                    pb = ctr % 4
                    r2 = ctr % 2
                    ctr += 1
                    cols = slice(ct * 512, (ct + 1) * 512)
                    for k in range(KC):
                        P.op("pe", lambda e, k=k, pb=pb, tb=tb, s_=s_: e.matmul(
                            psum[pb][:], lhsT=mTt[:, k, tb * 128:(tb + 1) * 128], rhs=wpan2[s_][:, k, :],
                            start=(k == 0), stop=(k == KC - 1)), reads=[b_mTt, b_wpan2[s_]], writes=[b_ps[pb]], inc=(k == KC - 1))
                    P.op("dve", lambda e, pb=pb, r2=r2, cols=cols: e.tensor_tensor(out=x1s[r2], in0=psum[pb][:], in1=g1_bc[:, cols], op=ALU.mult),
                         reads=[b_ps[pb], b_g1], writes=[b_x1s[r2]])
                    P.op("dve", lambda e, r2=r2, tb=tb, cols=cols: e.tensor_tensor(out=xt[:, tb, cols], in0=xt[:, tb, cols], in1=x1s[r2], op=ALU.add),
                         reads=[b_x1s[r2], b_xt[tb]], writes=[b_xt[tb]])
                    if ct == 3:
                        P.dma("sp", f"x1o{tb}", lambda e, t0=t0, tb=tb: e.dma_start(out=x1_d[t0 + tb * 128:t0 + (tb + 1) * 128, :], in_=xt[:, tb, :]),
                              reads=[b_xt[tb]])
        P.barrier()
    if upto >= 5:
        P.tag = 'ph4'
        TL = 1024
        NW = 342
        ALL = Arena(big, 0, RA + RB + RW)
        g2_d = nc.dram_tensor("g2_scr", [128, D], F32).ap()
        cwT = ALL.alloc([3, 88], F32)
        cbT = ALL.alloc([88], F32)
        nk0T = ALL.alloc([88], F32)
        nk2T = ALL.alloc([88], F32)
        keepc4 = ALL.alloc([1], F32)
        tmp4 = [ALL.alloc([256], F32) for _ in range(2)]
        fsm = [ALL.alloc([4], F32) for _ in range(2)]
        gT_all = ALL.alloc([NFC, TL], BF16)
        U0 = ALL.off
        b_g2 = B("g2")
        b_c4 = B("c4")
        b_wd = B("wd")
        b_tmp4 = P.bufs(2, "tmp4")
        b_fsm = P.bufs(2, "fsm")
        g2_bc = ALL.alloc([D], F32)
        cw_sb = ALL.alloc([4, 128], F32)
        wpan3 = [ALL.alloc([KC, 512], BF16) for _ in range(2)]
        b_wpan3 = P.bufs(2, "wpan3")
        bmb3 = ALL.alloc([512], F32)
        b_bmb3 = B("bmb3")
        sc_bc3 = ALL.alloc([KC, 128], BF16)
        b_scbc3 = B("scbc3")
        mod_bcast(5, g2_bc, b_g2, wpan3, b_wpan3, bmb3, b_bmb3, (0, 1), sc_bc3, b_scbc3)
        P.dma("sp", "g2st", lambda e, src=g2_bc: e.dma_start(out=g2_d, in_=src), reads=[b_g2])
        P.dma("sp", "c41", lambda e: e.dma_start(out=cw_sb[0:88, 0:3, :], in_=convw_d.rearrange("t (c p) -> c t p", p=128)), writes=[b_c4])
        P.dma("sp", "c42", lambda e: e.dma_start(out=cw_sb[0:88, 3, :], in_=convb_d.rearrange("(c p) -> c p", p=128)), writes=[b_c4])
        P.dma("sp", "c43", lambda e: e.dma_start(out=keepc4, in_=keep_d.partition_broadcast(128)), writes=[b_c4])
        for t_ in range(4):
            P.op("pe", lambda e, t_=t_: e.transpose(out=psum[7][:, t_ * 88:(t_ + 1) * 88], in_=cw_sb[0:88, t_, :], identity=ident_f[0:88, 0:88]),
                 reads=[b_c4, b_ident], writes=[b_ps[7]], inc=(t_ == 3))
        P.op("dve", lambda e: e.tensor_copy(out=cwT, in_=psum[7][:, 0:264].rearrange("p (t c) -> p t c", c=88)), reads=[b_ps[7]], writes=[b_c4])
        P.op("dve", lambda e: e.tensor_copy(out=cbT, in_=psum[7][:, 264:352]), reads=[b_ps[7]], writes=[b_c4])
        P.op("dve", lambda e: e.tensor_scalar(out=keepc4, in0=keepc4, scalar1=-1.0, scalar2=None, op0=ALU.add), reads=[b_c4], writes=[b_c4])
        P.op("dve", lambda e: e.tensor_scalar(out=nk0T, in0=cwT[:, 0, :], scalar1=keepc4[:, 0:1], scalar2=None, op0=ALU.mult),
             reads=[b_c4], writes=[b_c4])
        P.op("dve", lambda e: e.tensor_scalar(out=nk2T, in0=cwT[:, 2, :], scalar1=keepc4[:, 0:1], scalar2=None, op0=ALU.mult),
             reads=[b_c4], writes=[b_c4])
        P.barrier()

        for tile in range(2):
            t0 = tile * TL
            ALL.off = U0
            h2T = ALL.alloc([KC, TL + 2], BF16)
            upan = [[ALL.alloc([KC, 512], BF16) for _ in range(2)] for _ in range(2)]
            ua = ALL.alloc([TL + 2], F32)
            acc = [ALL.alloc([TL], F32) for _ in range(2)]
            sil = ALL.alloc([TL], BF16)
            xb = [upan[1][0].rearrange("p k c -> p (k c)").bitcast(F32)[:, 0:D]]
            jk = upan[1][1].rearrange("p k c -> p (k c)")[:, 0:D]
            b_h2T = [B(f"h2T{tile}_{b}") for b in range(9)]
            b_upan = P.bufs(2, "upan")
            b_xb = [b_upan[1]]
            b_jk = b_upan[1]
            b_gT = [B(f"gT{tile}_{i}") for i in range(NFC)]
            b_ua = B("ua")
            b_acc = P.bufs(2, "acc")
            b_sil = B("sil")

            def gT(i):
                return gT_all[:, i, :]

            P.op("pool", lambda e: e.memset(h2T, 0.0), writes=b_h2T)
            P.op("pool", lambda e: e.memset(xb[0], 0.0), writes=[b_xb[0]])
            if tile == 1:
                P.dma("sp", "xb40", lambda e, t0=t0: e.dma_start(out=xb[0][0:1, :], in_=x1_d[t0 - 1:t0, :]), writes=[b_xb[0]])
            else:
                P.dma("sp", "xb40", lambda e, t0=t0: e.dma_start(out=xb[0][1:2, :], in_=x1_d[t0 + TL:t0 + TL + 1, :]), writes=[b_xb[0]])

            def norm_blk(src_fn, dsts, hb):
                if src_fn is not None:
                    P.dma("sp", "xb40", lambda e: e.dma_start(out=xb[0], in_=src_fn()), writes=[b_xb[0]])
                ssq = small[:, 40:41]
                P.op("act", lambda e: e.activation(out=jk, in_=xb[0], func=AF.Square, accum_out=ssq),
                     reads=[b_xb[0]], writes=[b_jk, b_small])
                P.op("act", lambda e: e.activation(out=ssq, in_=ssq, func=AF.Ln, scale=1.0 / D, bias=EPS), reads=[b_small], writes=[b_small])
                P.op("act", lambda e: e.activation(out=ssq, in_=ssq, func=AF.Exp, scale=-0.5), reads=[b_small], writes=[b_small])
                P.op("dve", lambda e: e.tensor_scalar(out=xb[0], in0=xb[0], scalar1=ssq, scalar2=None, op0=ALU.mult),
                     reads=[b_small, b_xb[0]], writes=[b_xb[0]])
                for k4 in range(4):
                    pb = 4 + k4
                    for kk in range(4):
                        k = k4 * 4 + kk
                        P.op("pe", lambda e, k=k, kk=kk, pb=pb: e.transpose(
                            out=psum[pb][:, kk * 128:(kk + 1) * 128], in_=xb[0][:, k * 128:(k + 1) * 128], identity=ident_f[:]),
                            reads=[b_xb[0], b_ident], writes=[b_ps[pb]], inc=(kk == 3))
                    for kk in range(4):
                        k = k4 * 4 + kk
                        for (pc0, n, dc0) in dsts:
                            if k4 % 2 == 0:
                                P.op("act", lambda e, k=k, kk=kk, pb=pb, pc0=pc0, n=n, dc0=dc0: e.activation(
                                    out=h2T[:, k, dc0:dc0 + n], in_=psum[pb][:, kk * 128 + pc0:kk * 128 + pc0 + n],
                                    func=AF.Identity, scale=s2T[:, k:k + 1], bias=modT[:, 48 + k:48 + k + 1]),
                                    reads=[b_ps[pb], b_s, b_modT], writes=[hb])
                            else:
                                P.op("dve", lambda e, k=k, kk=kk, pb=pb, pc0=pc0, n=n, dc0=dc0: e.tensor_scalar(
                                    out=h2T[:, k, dc0:dc0 + n], in0=psum[pb][:, kk * 128 + pc0:kk * 128 + pc0 + n],
                                    scalar1=s2T[:, k:k + 1], scalar2=modT[:, 48 + k:48 + k + 1], op0=ALU.mult, op1=ALU.add),
                                    reads=[b_ps[pb], b_s, b_modT], writes=[hb])
            if tile == 1:
                norm_blk(None, [(0, 1, 0)], b_h2T[8])
            else:
                norm_blk(None, [(1, 1, TL + 1)], b_h2T[8])
            for b in range(8):
                norm_blk(lambda b=b, t0=t0: x1_d[t0 + b * 128:t0 + (b + 1) * 128, :], [(0, 128, 1 + b * 128)], b_h2T[b])

            for i in range(NFC):
                grp, gi = divmod(i, 4)
                s_ = grp % 2
                if gi == 0:
                    P.dma("pool", f"upan{s_}a", lambda e, s_=s_, grp=grp: e.dma_start(out=upan[s_][0], in_=w_up_r[:, :, grp * 512:(grp + 1) * 512]),
                          writes=[b_upan[s_]])
                    P.dma("pool", f"upan{s_}b", lambda e, s_=s_, grp=grp: e.dma_start(
                        out=upan[s_][1], in_=w_up_r[:, :, D_FF + grp * 512:D_FF + (grp + 1) * 512]), writes=[b_upan[s_]])
                for half in range(2):
                    c = i if half == 0 else NFC + i
                    for n3 in range(3):
                        pb = half * 3 + n3
                        for k in range(KC):
                            P.op("pe", lambda e, k=k, pb=pb, half=half, n3=n3, s_=s_, gi=gi: e.matmul(
                                psum[pb][:, 0:NW], lhsT=upan[s_][half][:, k, gi * 128:(gi + 1) * 128],
                                rhs=h2T[:, k, n3 * NW:(n3 + 1) * NW], start=(k == 0), stop=(k == KC - 1)),
                                reads=[b_upan[s_]] + b_h2T, writes=[b_ps[pb]], inc=(k == KC - 1))
                    ac = acc[half]
                    bac = b_acc[half]
                    for n3 in range(3):
                        pb = half * 3 + n3
                        P.op("act", lambda e, pb=pb, n3=n3: e.copy(out=ua[:, n3 * NW:(n3 + 1) * NW], in_=psum[pb][:, 0:NW]),
                             reads=[b_ps[pb]], writes=[b_ua])
                    eng = "dve"
                    P.op("act", lambda e, ac=ac, c=c: e.activation(out=ac, in_=ua[:, 1:TL + 1], func=AF.Identity,
                                                                   scale=cwT[:, 1, c:c + 1], bias=cbT[:, c:c + 1]),
                         reads=[b_ua, b_c4], writes=[bac])
                    P.op(eng, lambda e, ac=ac, c=c: e.scalar_tensor_tensor(out=ac, in0=ua[:, 0:TL], scalar=cwT[:, 0, c:c + 1], in1=ac,
                                                                           op0=ALU.mult, op1=ALU.add),
                         reads=[b_ua, b_c4, bac], writes=[bac])
                    P.op(eng, lambda e, ac=ac, c=c: e.scalar_tensor_tensor(out=ac, in0=ua[:, 2:TL + 2], scalar=cwT[:, 2, c:c + 1], in1=ac,
                                                                           op0=ALU.mult, op1=ALU.add),
                         reads=[b_ua, b_c4, bac], writes=[bac])
                    acv = ac.rearrange("p (s t) -> p s t", t=256)
                    ul = ua[:, 0:TL].rearrange("p (s t) -> p s t", t=256)
                    ur = ua[:, 2:TL + 2].rearrange("p (s t) -> p s t", t=256)
                    P.op(eng, lambda e, acv=acv, ul=ul, c=c: e.scalar_tensor_tensor(
                        out=acv[:, :, 0:1], in0=ul[:, :, 0:1], scalar=nk0T[:, c:c + 1], in1=acv[:, :, 0:1], op0=ALU.mult, op1=ALU.add),
                        reads=[b_ua, b_c4, bac], writes=[bac])
                    P.op(eng, lambda e, acv=acv, ur=ur, c=c: e.scalar_tensor_tensor(
                        out=acv[:, :, 255:256], in0=ur[:, :, 255:256], scalar=nk2T[:, c:c + 1], in1=acv[:, :, 255:256],
                        op0=ALU.mult, op1=ALU.add), reads=[b_ua, b_c4, bac], writes=[bac])
                P.op("act", lambda e: e.activation(out=sil, in_=acc[0], func=AF.Silu), reads=[b_acc[0]], writes=[b_sil])
                P.op("dve", lambda e, i=i: e.tensor_tensor(out=gT(i), in0=sil, in1=acc[1], op=ALU.mult),
                     reads=[b_sil, b_acc[1]], writes=[b_gT[i]])
            P.barrier()

            ALL.off = U0
            x2 = ALL.alloc([8, D], F32)
            wd = [ALL.alloc([NFC, 256], BF16) for _ in range(2)]
            g2s = [ALL.alloc([256], F32) for _ in range(2)]
            fn_bc = wd[0].rearrange("p k c -> p (k c)").bitcast(F32)[:, 0:D]
            fjunk = wd[1].rearrange("p k c -> p (k c)")[:, 0:TL]
            b_wdd = P.bufs(2, "wdd")
            b_g2s = P.bufs(2, "g2s")
            b_x2 = [B(f"x2{tile}_{b}") for b in range(8)]
            P.dma("sp", "x2ld", lambda e, t0=t0: e.dma_start(out=x2, in_=x1_d[t0:t0 + TL, :].rearrange("(b p) c -> p b c", p=128)),
                  writes=b_x2)
            ctr = 0
            for ct in range(8):
                cols = slice(ct * 256, (ct + 1) * 256)
                w_ = ct % 2
                P.dma("pool", f"wdpan{w_}", lambda e, cols=cols, w_=w_: e.dma_start(out=wd[w_], in_=w_dn_r[:, :, cols]), writes=[b_wdd[w_]])
                P.dma("sp", f"g2s{w_}", lambda e, cols=cols, w_=w_: e.dma_start(out=g2s[w_], in_=g2_d[:, cols]), writes=[b_g2s[w_]])
                for tb in range(8):
                    pb = ctr % 4
                    tq = ctr % 2
                    ctr += 1
                    for kc in range(NFC):
                        P.op("pe", lambda e, kc=kc, pb=pb, tb=tb, w_=w_: e.matmul(
                            psum[pb][:, 0:256], lhsT=gT(kc)[:, tb * 128:(tb + 1) * 128], rhs=wd[w_][:, kc, :],
                            start=(kc == 0), stop=(kc == NFC - 1)), reads=[b_gT[kc], b_wdd[w_]], writes=[b_ps[pb]], inc=(kc == NFC - 1))
                    P.op("dve", lambda e, pb=pb, tq=tq, w_=w_: e.tensor_tensor(out=tmp4[tq], in0=psum[pb][:, 0:256], in1=g2s[w_],
                                                                              op=ALU.mult), reads=[b_ps[pb], b_g2s[w_]], writes=[b_tmp4[tq]])
                    P.op("dve", lambda e, tq=tq, tb=tb, cols=cols: e.tensor_tensor(out=x2[:, tb, cols], in0=x2[:, tb, cols], in1=tmp4[tq],
                                                                                  op=ALU.add), reads=[b_tmp4[tq], b_x2[tb]], writes=[b_x2[tb]])
            P.dma("sp", "c40", lambda e: e.dma_start(out=fn_bc, in_=fnorm_d.partition_broadcast(128)), writes=[b_wdd[0]])
            b_fj = b_wdd[1]
            jk2 = upan_alias = None
            for tb in range(8):
                f = tb % 2
                P.op("act", lambda e, tb=tb, f=f: e.activation(out=fjunk, in_=x2[:, tb, 0:TL], func=AF.Square, accum_out=fsm[f][:, 0:1]),
                     reads=[b_x2[tb]], writes=[b_fj, b_fsm[f]])
                P.op("act", lambda e, tb=tb, f=f: e.activation(out=fjunk, in_=x2[:, tb, TL:D], func=AF.Square, accum_out=fsm[f][:, 1:2]),
                     reads=[b_x2[tb]], writes=[b_fj, b_fsm[f]])
                P.op("dve", lambda e, f=f: e.tensor_tensor(out=fsm[f][:, 2:3], in0=fsm[f][:, 0:1], in1=fsm[f][:, 1:2], op=ALU.add),
                     reads=[b_fsm[f]], writes=[b_fsm[f]])
                P.op("act", lambda e, f=f: e.activation(out=fsm[f][:, 2:3], in_=fsm[f][:, 2:3], func=AF.Ln, scale=1.0 / D, bias=EPS),
                     reads=[b_fsm[f]], writes=[b_fsm[f]])
                P.op("act", lambda e, f=f: e.activation(out=fsm[f][:, 3:4], in_=fsm[f][:, 2:3], func=AF.Exp, scale=-0.5),
                     reads=[b_fsm[f]], writes=[b_fsm[f]])
                P.op("dve", lambda e, tb=tb, f=f: e.scalar_tensor_tensor(out=x2[:, tb, :], in0=x2[:, tb, :], scalar=fsm[f][:, 3:4], in1=fn_bc,
                                                                         op0=ALU.mult, op1=ALU.mult),
                     reads=[b_x2[tb], b_fsm[f], b_wdd[0]], writes=[b_x2[tb]])
                P.dma("sp", f"yout{f}", lambda e, tb=tb, t0=t0: e.dma_start(out=y_d[t0 + tb * 128:t0 + (tb + 1) * 128, :], in_=x2[:, tb, :]),
                      reads=[b_x2[tb]])
            P.barrier()
    if dbg:
        if "hT" in dbg_d:
            P.dma("sp", "dbg", lambda e: e.dma_start(out=dbg_d["hT"], in_=hT), reads=b_hT)
        if "yT" in dbg_d:
            P.dma("sp", "dbg", lambda e: e.dma_start(out=dbg_d["yT"], in_=yT), reads=[b for l in b_yT for b in l])
        if "modT" in dbg_d:
            P.dma("sp", "dbg", lambda e: e.dma_start(out=dbg_d["modT"], in_=modT[:]), reads=[b_modT])

    P.barrier()
    ok, stuck, val = simulate(P)
    print('SIM', ok, stuck if not ok else '', {k: len(v) for k, v in P.ops.items()}, 'nsem', len(P.sem_keys()))
    assert ok

    with ExitStack() as es2:
        sems = {k: es2.enter_context(nc.semaphore("s_" + k.replace(":", "_"))) for k in P.sem_keys()}
        with nc.Block() as block:
            @block.tensor
            def _(e):
                P.emit("pe", e, sems)

            @block.scalar
            def _(e):
                P.emit("act", e, sems)

            @block.vector
            def _(e):
                P.emit("dve", e, sems)

            @block.gpsimd
            def _(e):
                P.emit("pool", e, sems)

            @block.sync
            def _(e):
                P.emit("sp", e, sems)
    es.close()
    return nc


def rope_tables():
    quarter = 16
    t = np.arange(T)
    row = (t // 64).astype(np.float32)
    col = (t % 64).astype(np.float32)
    inv_freq = np.power(np.float32(10000.0), -np.arange(quarter, dtype=np.float32) / np.float32(quarter)).astype(np.float32)
    ang_r = row[:, None] * inv_freq
    ang_c = col[:, None] * inv_freq
    ang = np.concatenate([ang_r, ang_r, ang_c, ang_c], axis=-1).astype(np.float32)
    cos = np.cos(ang).astype(np.float32).T
    sin = np.sin(ang).astype(np.float32).T
    return (np.ascontiguousarray(np.concatenate([cos, cos], axis=0)),
            np.ascontiguousarray(np.concatenate([sin, sin], axis=0)))


def rot_matrix():
    r = np.zeros((128, 128), np.float32)
    for base in (0, 64):
        for m in range(64):
            blk = m // 16
            if blk == 0:
                r[base + m + 16, base + m] = -1.0
            elif blk == 1:
                r[base + m - 16, base + m] = 1.0
            elif blk == 2:
                r[base + m + 16, base + m] = -1.0
            else:
                r[base + m - 16, base + m] = 1.0
    return r


def make_in_maps(inp):
    maps = []
    cosT, sinT = rope_tables()
    shared = {
        "w_mod": np.ascontiguousarray(inp["w_mod"][0]),
        "b_mod": np.ascontiguousarray(inp["b_mod"][0]),
        "norm1": np.ascontiguousarray(inp["norm1"][0]),
        "norm2": np.ascontiguousarray(inp["norm2"][0]),
        "final_norm": np.ascontiguousarray(inp["final_norm"]),
        "ident": np.eye(128, dtype=np.float32),
        "w_in": np.ascontiguousarray(inp["w_in"][0]),
        "rrot": rot_matrix(),
        "lamv": np.ascontiguousarray(np.stack([inp["lam_q1"][0], inp["lam_k1"][0], inp["lam_q2"][0], inp["lam_k2"][0]])),
        "diff_norm": np.ascontiguousarray(inp["diff_norm"][0]),
        "b_gates": np.ascontiguousarray(inp["b_gates"][0]),
        "w_pa": np.ascontiguousarray(inp["w_pa"][0]),
        "w_pb": np.ascontiguousarray(inp["w_pb"][0]),
        "w_out": np.ascontiguousarray(inp["w_out"][0]),
        "w_up": np.ascontiguousarray(inp["w_up"][0]),
        "w_down": np.ascontiguousarray(inp["w_down"][0]),
        "conv_w": np.ascontiguousarray(inp["conv_w"][0]),
        "conv_b": np.ascontiguousarray(inp["conv_b"][0]),
        "umask": np.ascontiguousarray(np.triu(np.ones((128, 128), np.float32))),
        "lmask": np.ascontiguousarray(np.tril(np.ones((128, 128), np.float32))),
        "mlstm_norm": np.ascontiguousarray(inp["mlstm_norm"][0]),
    }
    for core in range(8):
        m = dict(shared)
        if core < 4:
            m["x"] = np.ascontiguousarray(inp["x_sample"][core])
            m["cvec"] = np.ascontiguousarray(inp["c"][core])
            m["ck"] = np.ascontiguousarray(inp["cache_k"][core, 0])
            m["cv"] = np.ascontiguousarray(inp["cache_v"][core, 0])
            m["cosT"] = cosT
            m["sinT"] = sinT
            m["abias"] = np.zeros((128, NSEG * 18), np.float32)
            m["c0"] = np.ascontiguousarray(inp["state_C"][core, 0])
            m["n0"] = np.ascontiguousarray(inp["state_n"][core, 0])
            m["m0"] = np.ascontiguousarray(inp["state_m"][core, 0])
            m["keep"] = np.ones((1,), np.float32)
        else:
            j = core - 4
            m["x"] = np.ascontiguousarray(inp["x_prompt"][8 * j:8 * j + 8].reshape(T, D))
            m["cvec"] = np.ascontiguousarray(inp["c_ctx"])
            m["ck"] = np.zeros((H, 256, 128), np.float32)
            m["cv"] = np.zeros((H, 256, 128), np.float32)
            m["cosT"] = np.ones((128, T), np.float32)
            m["sinT"] = np.zeros((128, T), np.float32)
            m["c0"] = np.zeros((2, H, 64, 128), np.float32)
            m["n0"] = np.zeros((2, H, 64), np.float32)
            m["m0"] = np.zeros((2, H), np.float32)
            m["keep"] = np.zeros((1,), np.float32)
            ab = np.full((NSEG, 18), NEG, np.float32)
            for s_ in range(NSEG):
                ab[s_, 2 * s_:2 * s_ + 2] = 0.0
            m["abias"] = np.ascontiguousarray(np.broadcast_to(ab.reshape(1, -1), (128, NSEG * 18)))
        maps.append(m)
    return maps


def kernel(**inputs):
    inp = {k: np.asarray(v, dtype=np.float32) for k, v in inputs.items()}
    nc = build_program()
    maps = make_in_maps(inp)
    res = run_bass_kernel_spmd(nc, maps, core_ids=list(range(8)))
    r = res.results
    y_sample = np.stack([np.asarray(r[c]["y"], np.float32) for c in range(4)], axis=0)
    y_prompt = np.concatenate([np.asarray(r[c]["y"], np.float32).reshape(8, SEG, D) for c in range(4, 8)], axis=0)
    nk = np.concatenate([np.asarray(r[c]["nk"], np.float32) for c in range(4, 8)], axis=0)[:, None]
    nv = np.concatenate([np.asarray(r[c]["nv"], np.float32) for c in range(4, 8)], axis=0)[:, None]
    nC = np.concatenate([np.asarray(r[c]["nC"], np.float32) for c in range(4, 8)], axis=0)[:, None]
    nn = np.concatenate([np.asarray(r[c]["nn"], np.float32) for c in range(4, 8)], axis=0)[:, None]
    nm = np.concatenate([np.asarray(r[c]["nm"], np.float32) for c in range(4, 8)], axis=0)[:, None]
    return (y_prompt, y_sample, nk, nv, nC, nn, nm)
```

```python
import math
import numpy as np
import concourse.bass as bass
import concourse.mybir as mybir
from concourse.bass_utils import run_bass_kernel_spmd

F32 = mybir.dt.float32
BF16 = mybir.dt.bfloat16
AF = mybir.ActivationFunctionType
ALU = mybir.AluOpType
AX = mybir.AxisListType

D = 2048
T = 2048
NB = 16
KC = 16
SEG = 256
NSEG = 8
H = 8
N_IN = 10272
D_FF = 5632
NFC = D_FF // 128
EPS = 1e-6
GATE_CAP = 15.0
LAMBDA_INIT = 0.8 - 0.6 * math.exp(0.0)
NEG = -30000.0

OFF_QA = 0
OFF_KA = 512
OFF_VA = 1024
OFF_OA = 2048
OFF_G = 3072
OFF_QB = 3104
OFF_KB = 4128
OFF_VB = 5152
OFF_GA = 6176
OFF_GB = 8224


class Buf:
    __slots__ = ("name", "w", "r", "excl")

    def __init__(self, name):
        self.name = name
        self.excl = False
        self.w = None
        self.r = []


class Prog:
    ENGS = ("pe", "act", "dve", "pool", "sp")

    def __init__(self, nc):
        self.nc = nc
        self.ops = {e: [] for e in self.ENGS}
        self.cnt = {e: 0 for e in self.ENGS}
        self.pending = {e: False for e in self.ENGS}
        self.waited = {e: {} for e in self.ENGS}
        self.sems = {}
        self.dma_cnt = {}
        self.nbuf = 0
        self.tag = ''
        import os
        self.skip = set(x for x in os.environ.get('SKIP', '').split(',') if x)

    def buf(self, name=None):
        self.nbuf += 1
        return Buf(name or f"b{self.nbuf}")

    def bufs(self, n, name="b"):
        return [self.buf(f"{name}{i}") for i in range(n)]

    def _wait(self, eng, key, val):
        if val <= 0:
            return
        if self.waited[eng].get(key, 0) >= val:
            return
        self.waited[eng][key] = val
        self.ops[eng].append(("wait", key, val))

    def _deps(self, eng, reads, writes, skip_self):
        need = {}

        def add(dep):
            if skip_self and dep[0] == eng:
                return
            if need.get(dep[0], 0) < dep[1]:
                need[dep[0]] = dep[1]
        for b in reads:
            if b.w is not None:
                add(b.w)
        for b in writes:
            if b.w is not None:
                add(b.w)
            for rr in b.r:
                add(rr)
        for key, val in need.items():
            self._wait(eng, key, val)

    def op(self, eng, fn, reads=(), writes=(), inc=True):
        if self.tag in self.skip:
            return
        skip_self = eng == "pe"
        xr = [b for b in reads if b.excl]
        if xr:
            reads = [b for b in reads if not b.excl]
            writes = list(writes) + [b for b in xr if b not in writes]
        self._deps(eng, reads, writes, skip_self)
        idx = self.cnt[eng] + 1
        for b in reads:
            b.r.append((eng, idx))
        for b in writes:
            b.w = (eng, idx)
            b.r = []
        if inc:
            self.cnt[eng] = idx
            self.pending[eng] = False
        else:
            self.pending[eng] = True
        self.ops[eng].append(("op", fn, inc))

    def dma(self, q, slot, fn, reads=(), writes=()):
        if self.tag in self.skip:
            return
        self._deps(q, reads, writes, False)
        key = "d:" + slot
        n = self.dma_cnt.get(key, 0) + 16
        self.dma_cnt[key] = n
        for b in reads:
            b.r.append((key, n))
        for b in writes:
            b.w = (key, n)
            b.r = []
        self.ops[q].append(("dma", fn, key))

    def barrier(self):
        for e in self.ENGS:
            assert not self.pending[e]
        for e in self.ENGS:
            for e2 in self.ENGS:
                if e2 != e:
                    self._wait(e, e2, self.cnt[e2])
            for key, n in self.dma_cnt.items():
                self._wait(e, key, n)

    def sem_keys(self):
        return list(self.ENGS) + list(self.dma_cnt.keys())

    def emit(self, eng, engine_obj, sems):
        for o in self.ops[eng]:
            if o[0] == "wait":
                engine_obj.wait_ge(sems[o[1]], o[2])
            elif o[0] == "op":
                ins = o[1](engine_obj)
                if o[2]:
                    ins.then_inc(sems[eng], 1)
            else:
                o[1](engine_obj).then_inc(sems[o[2]], 16)


def simulate(P):
    ptr = {e: 0 for e in P.ENGS}
    val = {}
    prog = True
    while prog:
        prog = False
        for e in P.ENGS:
            ops = P.ops[e]
            while ptr[e] < len(ops):
                o = ops[ptr[e]]
                if o[0] == "wait":
                    if val.get(o[1], 0) < o[2]:
                        break
                elif o[0] == "op":
                    if o[2]:
                        val[e] = val.get(e, 0) + 1
                else:
                    val[o[2]] = val.get(o[2], 0) + 16
                ptr[e] += 1
                prog = True
    stuck = {e: (ptr[e], len(P.ops[e]), P.ops[e][ptr[e]][:3] if ptr[e] < len(P.ops[e]) else None) for e in P.ENGS}
    ok = all(ptr[e] == len(P.ops[e]) for e in P.ENGS)
    return ok, stuck, val
class Arena:
    def __init__(self, tens, base_bytes, nbytes):
        self.t = tens
        self.base = base_bytes
        self.size = nbytes
        self.off = 0

    def reset(self):
        self.off = 0

    def alloc(self, shape, dt):
        n = 1
        for s_ in shape:
            n *= s_
        nb = n * (4 if dt == F32 else 2)
        nb_al = (nb + 31) // 32 * 32
        assert self.off + nb_al <= self.size, (self.off, nb_al, self.size)
        o = (self.base + self.off) // 4
        ap = self.t[:, o:o + nb_al // 4]
        self.off += nb_al
        if dt != F32:
            ap = ap.bitcast(dt)
        ap = ap[:, 0:n]
        if len(shape) == 2:
            ap = ap.rearrange("p (a b) -> p a b", b=shape[1])
        elif len(shape) == 3:
            ap = ap.rearrange("p (a b c) -> p a b c", b=shape[1], c=shape[2])
        return ap


def build_program(dbg=None, upto=99):
    from contextlib import ExitStack
    nc = bass.Bass("TRN2", target_bir_lowering=False)
    P = Prog(nc)

    def din(name, shape, dt=F32):
        return nc.dram_tensor(name, list(shape), dt, kind="ExternalInput").ap()

    def dout(name, shape, dt=F32):
        return nc.dram_tensor(name, list(shape), dt, kind="ExternalOutput").ap()

    x_d = din("x", [T, D])
    cvec_d = din("cvec", [D])
    w_mod_d = din("w_mod", [D, 6 * D])
    b_mod_d = din("b_mod", [6 * D])
    norm1_d = din("norm1", [D])
    norm2_d = din("norm2", [D])
    fnorm_d = din("final_norm", [D])
    ident_d = din("ident", [128, 128])
    w_in_d = din("w_in", [D, N_IN])
    ck_d = din("ck", [H, 256, 128])
    cv_d = din("cv", [H, 256, 128])
    cosT_d = din("cosT", [128, T])
    sinT_d = din("sinT", [128, T])
    rrot_d = din("rrot", [128, 128])
    abias_d = din("abias", [128, NSEG * 18])
    lamv_d = din("lamv", [4, 64])
    dnorm_d = din("diff_norm", [128])
    bgates_d = din("b_gates", [32])
    umask_d = din("umask", [128, 128])
    lmask_d = din("lmask", [128, 128])
    m0_d = din("m0", [2, H])
    keep_d = din("keep", [1])
    mlnorm_d = din("mlstm_norm", [1024])
    c0_d = din("c0", [2, H, 64, 128])
    w_pa_d = din("w_pa", [1024, D])
    w_pb_d = din("w_pb", [1024, D])
    w_out_d = din("w_out", [D, D])
    w_up_d = din("w_up", [D, 2 * D_FF])
    w_down_d = din("w_down", [D_FF, D])
    convw_d = din("conv_w", [3, 2 * D_FF])
    convb_d = din("conv_b", [2 * D_FF])
    n0_d = din("n0", [2, H, 64])

    y_d = dout("y", [T, D])
    nk_d = dout("nk", [NSEG, H, SEG, 128])
    nv_d = dout("nv", [NSEG, H, SEG, 128])
    nC_d = dout("nC", [NSEG, 2, H, 64, 128])
    nn_d = dout("nn", [NSEG, 2, H, 64])
    nm_d = dout("nm", [NSEG, 2, H])

    dbg_d = {}
    if dbg:
        for nm, (shape, dt) in dbg.items():
            dbg_d[nm] = dout("dbg_" + nm, shape, dt)

    es = ExitStack()

    def sb(name, shape, dt):
        return es.enter_context(nc.sbuf_tensor(name, list(shape), dt))

    ident_f = sb("ident_f", [128, 128], F32)
    ident_b = sb("ident_b", [128, 128], BF16)
    modT = sb("modT", [128, 96], F32)
    s1T = sb("s1T", [128, KC], F32)
    s2T = sb("s2T", [128, KC], F32)
    nrmT = sb("nrmT", [128, 2, KC], F32)
    small = sb("small", [128, 64], F32)
    sc_col = sb("sc_col", [128, KC], BF16)
    RA, RB, RW = 64 * 1024, 64 * 1024, 78 * 1024
    big = sb("big", [128, (RA + RB + RW) // 4], F32)
    A_ = Arena(big, 0, RA)
    B_ = Arena(big, RA, RB)
    W = Arena(big, RA + RB, RW)

    psum = [es.enter_context(nc.psum_tensor(f"ps{i}", [128, 512], F32)) for i in range(8)]
    psum_b = [p[:].bitcast(BF16) for p in psum]

    B = P.buf
    b_ident = B("ident")
    b_modT = B("modT")
    b_s = B("s12")
    b_nrm = B("nrm")
    b_cv = B("cv")
    b_sc = B("sc")
    b_bmT = B("bmT")
    b_ps = P.bufs(8, "ps")
    for b_ in b_ps:
        b_.excl = True
    b_small = B("small")

    w_in_r = w_in_d.rearrange("(k p) c -> p k c", p=128)

    P.dma("sp", "misc0", lambda e: e.dma_start(out=ident_f[:], in_=ident_d), writes=[b_ident])
    P.op("pool", lambda e: e.tensor_copy(out=ident_b[:], in_=ident_f[:]), reads=[b_ident], writes=[b_ident])

    W.reset()
    bmT = W.alloc([96], F32)
    cvT = W.alloc([KC], F32)

    def fm_load(slot, dst, src, wb):
        P.dma("sp", slot, lambda e: e.dma_start(out=dst, in_=src.rearrange("(k p) -> p k", p=128),
                                                allow_slow_non_contiguous=True), writes=[wb])
    fm_load("misc", cvT, cvec_d, b_cv)
    fm_load("misc2", bmT, b_mod_d, b_bmT)
    fm_load("misc3", nrmT[:, 0, :], norm1_d, b_nrm)
    fm_load("misc5", nrmT[:, 1, :], norm2_d, b_nrm)

    P.op("act", lambda e: e.activation(out=small[:, 0:KC], in_=cvT, func=AF.Sigmoid), reads=[b_cv], writes=[b_small])
    P.op("dve", lambda e: e.tensor_tensor(out=sc_col[:], in0=small[:, 0:KC], in1=cvT, op=ALU.mult),
         reads=[b_small, b_cv], writes=[b_sc])

    wmr = w_mod_d.rearrange("(k p) c -> p k c", p=128)
    wpan = [W.alloc([KC, 512], BF16) for _ in range(2)]
    b_wpan = P.bufs(2, "wpan")
    ps_mod = psum[0]
    def mod_panel(pi):
        if True:
            grp = (0, 1, 3, 4)[pi // 4]
            sub = pi % 4
            pn = grp * 4 + sub
            slot = pi % 2
            c0 = pn * 512
            P.dma("pool", f"wpan{slot}",
                  lambda e, slot=slot, c0=c0: e.dma_start(out=wpan[slot], in_=wmr[:, :, c0:c0 + 512]),
                  writes=[b_wpan[slot]])
            for jj in range(4):
                j = pn * 4 + jj
                for k in range(KC):
                    P.op("pe", lambda e, slot=slot, jj=jj, j=j, k=k: e.matmul(
                        ps_mod[:, j:j + 1], lhsT=wpan[slot][:, k, jj * 128:(jj + 1) * 128],
                        rhs=sc_col[:, k:k + 1], start=(k == 0), stop=(k == KC - 1)),
                        reads=[b_wpan[slot], b_sc], writes=[b_ps[0]], inc=(k == KC - 1))
    def mod_finish(c0_, c1_):
        P.op("dve", lambda e: e.tensor_tensor(out=modT[:, c0_:c1_], in0=ps_mod[:, c0_:c1_], in1=bmT[:, c0_:c1_], op=ALU.add),
             reads=[b_ps[0], b_bmT], writes=[b_modT])

    for pi in range(8):
        mod_panel(pi)
    mod_finish(0, 32)
    P.op("dve", lambda e: e.scalar_tensor_tensor(out=s1T[:], in0=modT[:, 16:32], scalar=1.0, in1=nrmT[:, 0, :],
                                                 op0=ALU.add, op1=ALU.mult), reads=[b_modT, b_nrm], writes=[b_s])

    def mod_bcast(grp, dst, b_dst, pan, b_pan, bmb, b_bmb, pbanks, sc_bc, b_scbc):
        P.op("dve", lambda e: e.tensor_copy(out=sc_bc, in_=sc_col[:].unsqueeze(2).to_broadcast([128, KC, 128])),
             reads=[b_sc], writes=[b_scbc])
        for sub in range(4):
            pn = grp * 4 + sub
            slot = sub % 2
            c0 = pn * 512
            pb = pbanks[sub % 2]
            P.dma("pool", f"wpan{slot}",
                  lambda e, slot=slot, c0=c0: e.dma_start(out=pan[slot], in_=wmr[:, :, c0:c0 + 512]),
                  writes=[b_pan[slot]])
            for k in range(KC):
                P.op("pe", lambda e, slot=slot, k=k, pb=pb: e.matmul(
                    psum[pb][:], lhsT=sc_bc[:, k, :], rhs=pan[slot][:, k, :],
                    start=(k == 0), stop=(k == KC - 1)),
                    reads=[b_pan[slot], b_scbc], writes=[b_ps[pb]], inc=(k == KC - 1))
            P.dma("sp", "bmbc", lambda e, c0=c0: e.dma_start(
                out=bmb, in_=b_mod_d[c0:c0 + 512].partition_broadcast(128)), writes=[b_bmb])
            P.op("dve", lambda e, pb=pb, sub=sub: e.tensor_tensor(
                out=dst[:, sub * 512:(sub + 1) * 512], in0=psum[pb][:], in1=bmb, op=ALU.add),
                reads=[b_ps[pb], b_bmb], writes=[b_dst])

    def norm_to_featmajor(src_fn, nblk, sT, shT_cols, dst, dst_col0, dst_bufs, xblk, b_xblk, junk, b_junk, hook=None):
        for b in range(nblk):
            if hook is not None:
                hook(b)
            slot = b % 2
            P.dma("sp", f"xblk{slot}", lambda e, slot=slot, b=b: e.dma_start(out=xblk[slot], in_=src_fn(b)),
                  writes=[b_xblk[slot]])
            ssq = small[:, 32 + slot:33 + slot]
            P.op("act", lambda e, slot=slot, ssq=ssq: e.activation(out=junk, in_=xblk[slot], func=AF.Square,
                                                                    accum_out=ssq),
                 reads=[b_xblk[slot]], writes=[b_junk, b_small])
            P.op("act", lambda e, ssq=ssq: e.activation(out=ssq, in_=ssq, func=AF.Ln, scale=1.0 / D, bias=EPS),
                 reads=[b_small], writes=[b_small])
            P.op("act", lambda e, ssq=ssq: e.activation(out=ssq, in_=ssq, func=AF.Exp, scale=-0.5),
                 reads=[b_small], writes=[b_small])
            P.op("dve", lambda e, slot=slot, ssq=ssq: e.tensor_scalar(out=xblk[slot], in0=xblk[slot], scalar1=ssq,
                                                                       scalar2=None, op0=ALU.mult),
                 reads=[b_small, b_xblk[slot]], writes=[b_xblk[slot]])
            for k4 in range(4):
                pb = 4 + (k4 % 4)
                for kk in range(4):
                    k = k4 * 4 + kk
                    P.op("pe", lambda e, slot=slot, k=k, kk=kk, pb=pb: e.transpose(
                        out=psum[pb][:, kk * 128:(kk + 1) * 128], in_=xblk[slot][:, k * 128:(k + 1) * 128],
                        identity=ident_f[:]), reads=[b_xblk[slot], b_ident], writes=[b_ps[pb]], inc=(kk == 3))
                for kk in range(4):
                    k = k4 * 4 + kk
                    c0 = dst_col0 + b * 128
                    if k4 % 2 == 0:
                        P.op("act", lambda e, k=k, kk=kk, pb=pb, c0=c0: e.activation(
                            out=dst[:, k, c0:c0 + 128], in_=psum[pb][:, kk * 128:(kk + 1) * 128],
                            func=AF.Identity, scale=sT[:, k:k + 1], bias=modT[:, shT_cols + k:shT_cols + k + 1]),
                            reads=[b_ps[pb], b_s, b_modT], writes=[dst_bufs[b]])
                    else:
                        P.op("dve", lambda e, k=k, kk=kk, pb=pb, c0=c0: e.tensor_scalar(
                            out=dst[:, k, c0:c0 + 128], in0=psum[pb][:, kk * 128:(kk + 1) * 128],
                            scalar1=sT[:, k:k + 1], scalar2=modT[:, shT_cols + k:shT_cols + k + 1],
                            op0=ALU.mult, op1=ALU.add),
                            reads=[b_ps[pb], b_s, b_modT], writes=[dst_bufs[b]])

    A_.reset()
    B_.reset()
    hT = A_.alloc([KC, T], BF16)
    b_hT = [B(f"hT{b}") for b in range(NB)]
    xblk = [B_.alloc([D], F32) for _ in range(2)]
    b_xblk = P.bufs(2, "xblk")
    junk = B_.alloc([D], BF16)
    b_junk = B("junk")

    def mod_hook(b):
        if b % 2 == 0 and b > 0:
            mod_panel(8 + b // 2 - 1)
    norm_to_featmajor(lambda b: x_d[b * 128:(b + 1) * 128, :], NB, s1T, 0, hT, 0, b_hT, xblk, b_xblk, junk, b_junk, hook=mod_hook)
    mod_panel(15)
    mod_finish(48, 80)
    P.op("dve", lambda e: e.scalar_tensor_tensor(out=s2T[:], in0=modT[:, 64:80], scalar=1.0, in1=nrmT[:, 1, :],
                                                 op0=ALU.add, op1=ALU.mult), reads=[b_modT, b_nrm], writes=[b_s])
    P.barrier()
    B_.reset()
    yT = B_.alloc([16, T], BF16)
    b_yT = [[B(f"yT{c}_{b}") for b in range(NB)] for c in range(16)]
    if upto >= 2:
        W.reset()
        Vpan = W.alloc([KC, 256], BF16)
        KQpan = [W.alloc([KC, 128], BF16) for _ in range(4)]
        Vext = W.alloc([18, 2, 130], BF16)
        KT = W.alloc([2304], BF16)
        cs = [W.alloc([2, 512], F32) for _ in range(2)]
        xf = [W.alloc([512], F32) for _ in range(2)]
        t1 = W.alloc([512], F32)
        t2 = W.alloc([512], F32)
        P12 = [W.alloc([512], BF16) for _ in range(4)]
        Qpad = [W.alloc([512], BF16) for _ in range(4)]
        kst = [W.alloc([4, 128], F32) for _ in range(2)]
        vst = [W.alloc([2, 128], F32) for _ in range(2)]
        osb = [W.alloc([128], F32) for _ in range(2)]
        ybt = [W.alloc([128], BF16) for _ in range(2)]
        sm = [W.alloc([8], F32) for _ in range(2)]
        ajunk = W.alloc([128], BF16)
        obuf = [W.alloc([2, 258], F32) for _ in range(2)]
        ckf = W.alloc([2, 128], F32)
        abias = W.alloc([NSEG * 18], F32)
        lamv = W.alloc([4, 64], F32)
        lamt = W.alloc([2, 64], F32)
        lams = W.alloc([4], F32)
        dn8 = W.alloc([128], F32)
        rrot = W.alloc([128], F32)

        b_Vpan = B("Vpan")
        b_KQpan = P.bufs(4, "KQpan")
        b_Vext = [B(f"Vext{b}") for b in range(18)]
        b_Vone = B("Vone")
        b_KT = [B(f"KT{i}") for i in range(5)]
        b_cs = P.bufs(2, "cs")
        b_xf = P.bufs(2, "xf")
        b_t1 = B("t1")
        b_t2 = B("t2")
        b_P12 = P.bufs(4, "P12")
        b_Qpad = P.bufs(4, "Qpad")
        b_kst = P.bufs(2, "kst")
        b_vst = P.bufs(2, "vst")
        b_osb = P.bufs(2, "osb")
        b_ybt = P.bufs(2, "ybt")
        b_sm = P.bufs(2, "sm")
        b_ajunk = B("ajunk")
        b_obuf = P.bufs(2, "obuf")
        b_ckf = B("ckf")
        b_const = B("aconst")

        P.tag = 'aconst'
        P.dma("sp", "ac0", lambda e: e.dma_start(out=abias, in_=abias_d), writes=[b_const])
        P.dma("sp", "ac1", lambda e: e.dma_start(out=lamv, in_=lamv_d.partition_broadcast(128)), writes=[b_const])
        P.dma("sp", "ac2", lambda e: e.dma_start(out=dn8, in_=dnorm_d.partition_broadcast(128)), writes=[b_const])
        P.dma("sp", "ac3", lambda e: e.dma_start(out=rrot, in_=rrot_d), writes=[b_const])
        P.op("dve", lambda e: e.tensor_scalar(out=dn8, in0=dn8, scalar1=1.0 - LAMBDA_INIT, scalar2=None, op0=ALU.mult),
             reads=[b_const], writes=[b_const])
        P.tag = 'lam'
        P.op("dve", lambda e: e.tensor_tensor(out=lamt[:, 0, :], in0=lamv[:, 0, :], in1=lamv[:, 1, :], op=ALU.mult),
             reads=[b_const], writes=[b_const])
        P.op("dve", lambda e: e.tensor_tensor(out=lamt[:, 1, :], in0=lamv[:, 2, :], in1=lamv[:, 3, :], op=ALU.mult),
             reads=[b_const], writes=[b_const])
        P.op("dve", lambda e: e.tensor_reduce(out=lams[:, 0:2], in_=lamt, axis=AX.X, op=ALU.add),
             reads=[b_const], writes=[b_const])
        P.op("act", lambda e: e.activation(out=lams[:, 0:2], in_=lams[:, 0:2], func=AF.Exp),
             reads=[b_const], writes=[b_const])
        P.op("dve", lambda e: e.tensor_tensor(out=lams[:, 2:3], in0=lams[:, 1:2], in1=lams[:, 0:1], op=ALU.subtract),
             reads=[b_const], writes=[b_const])
        P.op("dve", lambda e: e.tensor_scalar(out=lams[:, 3:4], in0=lams[:, 2:3], scalar1=-LAMBDA_INIT, scalar2=None,
                                              op0=ALU.add), reads=[b_const], writes=[b_const])
        neg_lam = lams[:, 3:4]
        P.tag = 'amemset'
        P.op("pool", lambda e: e.memset(Vext[:, :, :, 128:129], 1.0), writes=[b_Vone])
        for q in range(4):
            P.op("pool", lambda e, q=q: e.memset(Qpad[q], 0.0), writes=[b_Qpad[q]])

        rope_ctr = [0]

        def rope(ps_idx, tt, writer):
            i = rope_ctr[0]
            rope_ctr[0] += 1
            s_ = i % 2
            P.dma("pool", f"cs{s_}", lambda e: e.dma_start(out=cs[s_][:, 0, :], in_=cosT_d[:, tt * 512:(tt + 1) * 512]),
                  writes=[b_cs[s_]])
            P.dma("pool", f"cs{s_}", lambda e: e.dma_start(out=cs[s_][:, 1, :], in_=sinT_d[:, tt * 512:(tt + 1) * 512]),
                  writes=[b_cs[s_]])
            P.op("dve", lambda e: e.tensor_copy(out=xf[s_], in_=psum[ps_idx][:]), reads=[b_ps[ps_idx]], writes=[b_xf[s_]])
            P.op("pe", lambda e: e.matmul(psum[7][:], lhsT=rrot, rhs=xf[s_], start=True, stop=True),
                 reads=[b_xf[s_], b_const], writes=[b_ps[7]])
            P.op("dve", lambda e: e.tensor_tensor(out=t1, in0=xf[s_], in1=cs[s_][:, 0, :], op=ALU.mult),
                 reads=[b_xf[s_], b_cs[s_]], writes=[b_t1])
            P.op("dve", lambda e: e.tensor_tensor(out=t2, in0=psum[7][:], in1=cs[s_][:, 1, :], op=ALU.mult),
                 reads=[b_ps[7], b_cs[s_]], writes=[b_t2])
            writer(s_)

        import os
        ATT_HG = int(os.environ.get('ATT_HG', '4'))
        ATT_STOP = os.environ.get('ATT_STOP', 'full')
        for hg in range(ATT_HG):
            c0 = OFF_VB + hg * 256
            P.tag = 'vproj'
            P.dma("pool", "Vpan", lambda e, c0=c0: e.dma_start(out=Vpan, in_=w_in_r[:, :, c0:c0 + 256]), writes=[b_Vpan])
            for b in range(NB):
                pb = 4 + b % 2
                for k in range(KC):
                    P.op("pe", lambda e, b=b, k=k, pb=pb: e.matmul(
                        psum[pb][:, 0:256], lhsT=hT[:, k, b * 128:(b + 1) * 128], rhs=Vpan[:, k, :],
                        start=(k == 0), stop=(k == KC - 1)), reads=[b_hT[b], b_Vpan], writes=[b_ps[pb]], inc=(k == KC - 1))
                s_ = b % 2
                P.tag = 'vevac'
                P.op("dve", lambda e, b=b, pb=pb: e.tensor_copy(
                    out=Vext[:, b, :, 0:128], in_=psum[pb][:, 0:256].rearrange("p (h d) -> p h d", d=128)),
                    reads=[b_ps[pb]], writes=[b_Vext[b]])
                P.tag = 'vst'
                P.op("dve", lambda e, s_=s_, pb=pb: e.tensor_copy(
                    out=vst[s_], in_=psum[pb][:, 0:256].rearrange("p (h d) -> p h d", d=128)),
                    reads=[b_ps[pb]], writes=[b_vst[s_]])
                seg, pos0 = b // 2, (b % 2) * 128
                P.tag = 'nvdma'
                P.dma("sp", f"vst{s_}", lambda e, s_=s_, seg=seg, pos0=pos0, hg=hg: e.dma_start(
                    out=nv_d[seg, 2 * hg:2 * hg + 2, pos0:pos0 + 128, :].rearrange("h p d -> p h d"), in_=vst[s_]),
                    reads=[b_vst[s_]])
                P.tag = 'vproj'
            P.tag = 'cvdma'
            for hl in range(2):
                for bb in range(2):
                    P.dma("pool", f"cv{hl}{bb}", lambda e, hl=hl, bb=bb, hg=hg: e.dma_start(
                        out=Vext[:, 16 + bb, hl, 0:128], in_=cv_d[2 * hg + hl, bb * 128:(bb + 1) * 128, :]),
                        writes=[b_Vext[16 + bb]])
            P.tag = 'att'
            for hl in range(2):
                if ATT_STOP == 'v':
                    break
                h = hg * 2 + hl
                Kp, Qp = KQpan[2 * hl], KQpan[2 * hl + 1]
                bKp, bQp = b_KQpan[2 * hl], b_KQpan[2 * hl + 1]
                ck0 = OFF_KB + h * 128
                cq0 = OFF_QB + h * 128
                P.dma("pool", f"KQ{2 * hl}", lambda e, Kp=Kp, ck0=ck0: e.dma_start(out=Kp, in_=w_in_r[:, :, ck0:ck0 + 128]),
                      writes=[bKp])
                P.dma("pool", f"KQ{2 * hl + 1}", lambda e, Qp=Qp, cq0=cq0: e.dma_start(out=Qp, in_=w_in_r[:, :, cq0:cq0 + 128]),
                      writes=[bQp])
                for tt in range(4):
                    pb = 4 + tt % 2
                    for k in range(KC):
                        P.op("pe", lambda e, tt=tt, k=k, pb=pb, Kp=Kp: e.matmul(
                            psum[pb][:], lhsT=Kp[:, k, :], rhs=hT[:, k, tt * 512:(tt + 1) * 512],
                            start=(k == 0), stop=(k == KC - 1)),
                            reads=[bKp] + b_hT[4 * tt:4 * tt + 4], writes=[b_ps[pb]], inc=(k == KC - 1))

                    def kwriter(s_, tt=tt, h=h):
                        P.op("dve", lambda e: e.tensor_tensor(out=KT[:, tt * 512:(tt + 1) * 512], in0=t1, in1=t2, op=ALU.add),
                             reads=[b_t1, b_t2], writes=[b_KT[tt]])
                        for j in range(4):
                            P.op("pe", lambda e, j=j: e.transpose(out=psum[7][:, j * 128:(j + 1) * 128],
                                                                  in_=xf[s_][:, j * 128:(j + 1) * 128], identity=ident_f[:]),
                                 reads=[b_xf[s_], b_ident], writes=[b_ps[7]], inc=(j == 3))
                        ks = tt % 2
                        P.op("dve", lambda e: e.tensor_copy(out=kst[ks], in_=psum[7][:].rearrange("p (j d) -> p j d", d=128)),
                             reads=[b_ps[7]], writes=[b_kst[ks]])
                        for sg in range(2):
                            seg = 2 * tt + sg
                            P.dma("sp", f"kst{ks}", lambda e, sg=sg, seg=seg: e.dma_start(
                                out=nk_d[seg, h, :, :].rearrange("(b p) d -> p b d", p=128),
                                in_=kst[ks][:, 2 * sg:2 * sg + 2, :]), reads=[b_kst[ks]])
                    rope(pb, tt, kwriter)
                P.dma("pool", "ckf", lambda e, h=h: e.dma_start(out=ckf, in_=ck_d[h].rearrange("(b p) d -> p b d", p=128)),
                      writes=[b_ckf])
                for bb in range(2):
                    P.op("pe", lambda e, bb=bb: e.transpose(out=psum[7][:, bb * 128:(bb + 1) * 128], in_=ckf[:, bb, :],
                                                            identity=ident_f[:]),
                         reads=[b_ckf, b_ident], writes=[b_ps[7]], inc=(bb == 1))
                P.op("dve", lambda e: e.tensor_copy(out=KT[:, 2048:2304], in_=psum[7][:, 0:256]),
                     reads=[b_ps[7]], writes=[b_KT[4]])

                if ATT_STOP == 'k':
                    continue
                def qproj(tt, h=h, Qp=Qp, bQp=bQp):
                    pb = 4
                    for k in range(KC):
                        P.op("pe", lambda e, k=k: e.matmul(
                            psum[pb][:], lhsT=Qp[:, k, :], rhs=hT[:, k, tt * 512:(tt + 1) * 512],
                            start=(k == 0), stop=(k == KC - 1)),
                            reads=[bQp] + b_hT[4 * tt:4 * tt + 4], writes=[b_ps[pb]], inc=(k == KC - 1))

                    def qwriter(s_):
                        for sg in range(2):
                            q = (tt % 2) * 2 + sg
                            P.op("dve", lambda e, q=q, sg=sg: e.tensor_tensor(
                                out=Qpad[q][0:64, 0:256], in0=t1[0:64, sg * 256:(sg + 1) * 256],
                                in1=t2[0:64, sg * 256:(sg + 1) * 256], op=ALU.add),
                                reads=[b_t1, b_t2], writes=[b_Qpad[q]])
                            P.op("dve", lambda e, q=q, sg=sg: e.tensor_tensor(
                                out=Qpad[q][64:128, 256:512], in0=t1[64:128, sg * 256:(sg + 1) * 256],
                                in1=t2[64:128, sg * 256:(sg + 1) * 256], op=ALU.add),
                                reads=[b_t1, b_t2], writes=[b_Qpad[q]])
                    rope(pb, tt, qwriter)

                SB = (0, 1, 5, 6)

                def smm(seg, kb, q):
                    sbk = SB[kb % 4]
                    P.op("pe", lambda e: e.matmul(psum[sbk][:], lhsT=KT[:, kb * 128:(kb + 1) * 128], rhs=Qpad[q],
                                                  start=True, stop=True),
                         reads=[b_KT[min(kb // 4, 4)], b_Qpad[q]], writes=[b_ps[sbk]])

                def attend(seg, mid_hook=None, h=h, hl=hl):
                    tt, sg = seg // 2, seg % 2
                    q = (tt % 2) * 2 + sg
                    smm(seg, 0, q)
                    smm(seg, 1, q)
                    smm(seg, 2, q)
                    for kb in range(18):
                        sbk = SB[kb % 4]
                        pj = kb % 4
                        P.op("act", lambda e, kb=kb, sbk=sbk, pj=pj: e.activation(
                            out=P12[pj], in_=psum[sbk][:], func=AF.Exp, scale=0.125,
                            bias=abias[:, seg * 18 + kb:seg * 18 + kb + 1]),
                            reads=[b_ps[sbk], b_const], writes=[b_P12[pj]])
                        for i in range(2):
                            for qb in range(2):
                                P.op("pe", lambda e, kb=kb, pj=pj, i=i, qb=qb: e.matmul(
                                    psum[2 + i][:, qb * 129:(qb + 1) * 129],
                                    lhsT=P12[pj][:, i * 256 + qb * 128:i * 256 + (qb + 1) * 128],
                                    rhs=Vext[:, kb, hl, 0:129], start=(kb == 0 and qb == 0), stop=(kb == 17),
                                    skip_group_check=True),
                                    reads=[b_P12[pj], b_Vext[kb], b_Vone], writes=[b_ps[2 + i]],
                                    inc=(i == 1 and qb == 1))
                        if kb + 3 < 18:
                            smm(seg, kb + 3, q)
                        if kb == 5 and mid_hook is not None:
                            mid_hook()
                    par = seg % 2
                    P.op("dve", lambda e, par=par: e.tensor_copy(out=obuf[par][:, 0, :], in_=psum[2][:, 0:258]),
                         reads=[b_ps[2]], writes=[b_obuf[par]])
                    P.op("dve", lambda e, par=par: e.tensor_copy(out=obuf[par][:, 1, :], in_=psum[3][:, 0:258]),
                         reads=[b_ps[3]], writes=[b_obuf[par]])

                def finalize(seg, h=h):
                    par = seg % 2
                    for qb in range(2):
                        f = qb % 2
                        O1 = obuf[par][:, 0, qb * 129:(qb + 1) * 129]
                        O2 = obuf[par][:, 1, qb * 129:(qb + 1) * 129]
                        smf = sm[f]
                        P.op("dve", lambda e, O1=O1, smf=smf: e.reciprocal(out=smf[:, 0:1], in_=O1[:, 128:129]),
                             reads=[b_obuf[par]], writes=[b_sm[f]])
                        P.op("dve", lambda e, O2=O2, smf=smf: e.reciprocal(out=smf[:, 1:2], in_=O2[:, 128:129]),
                             reads=[b_obuf[par]], writes=[b_sm[f]])
                        P.op("dve", lambda e, smf=smf: e.tensor_tensor(out=smf[:, 2:3], in0=smf[:, 1:2], in1=neg_lam, op=ALU.mult),
                             reads=[b_sm[f], b_const], writes=[b_sm[f]])
                        P.op("dve", lambda e, O1=O1, smf=smf, f=f: e.tensor_scalar(out=osb[f], in0=O1[:, 0:128], scalar1=smf[:, 0:1],
                                                                                    scalar2=None, op0=ALU.mult),
                             reads=[b_obuf[par], b_sm[f]], writes=[b_osb[f]])
                        P.op("dve", lambda e, O2=O2, smf=smf, f=f: e.scalar_tensor_tensor(
                            out=osb[f], in0=O2[:, 0:128], scalar=smf[:, 2:3], in1=osb[f], op0=ALU.mult, op1=ALU.add),
                            reads=[b_obuf[par], b_sm[f], b_osb[f]], writes=[b_osb[f]])
                        P.op("act", lambda e, smf=smf, f=f: e.activation(out=ajunk, in_=osb[f], func=AF.Square,
                                                                         accum_out=smf[:, 3:4]),
                             reads=[b_osb[f]], writes=[b_ajunk, b_sm[f]])
                        P.op("act", lambda e, smf=smf: e.activation(out=smf[:, 4:5], in_=smf[:, 3:4], func=AF.Ln,
                                                                    scale=1.0 / 128, bias=EPS),
                             reads=[b_sm[f]], writes=[b_sm[f]])
                        P.op("act", lambda e, smf=smf: e.activation(out=smf[:, 5:6], in_=smf[:, 4:5], func=AF.Exp, scale=-0.5),
                             reads=[b_sm[f]], writes=[b_sm[f]])
                        P.op("dve", lambda e, smf=smf, f=f: e.scalar_tensor_tensor(
                            out=ybt[f], in0=osb[f], scalar=smf[:, 5:6], in1=dn8, op0=ALU.mult, op1=ALU.mult),
                            reads=[b_osb[f], b_sm[f], b_const], writes=[b_ybt[f]])
                        P.op("pe", lambda e, f=f: e.transpose(out=psum_b[7][:, f * 128:(f + 1) * 128], in_=ybt[f],
                                                              identity=ident_b[:]),
                             reads=[b_ybt[f], b_ident], writes=[b_ps[7]])
                        blk = seg * 2 + qb
                        P.op("dve", lambda e, f=f, blk=blk: e.tensor_copy(
                            out=yT[:, 8 + h, blk * 128:(blk + 1) * 128], in_=psum_b[7][:, f * 128:(f + 1) * 128]),
                            reads=[b_ps[7]], writes=[b_yT[8 + h][blk]])

                qproj(0)
                for tt in range(4):
                    if tt + 1 < 4:
                        qproj(tt + 1)
                    if ATT_STOP == 'q':
                        continue
                    for seg in (2 * tt, 2 * tt + 1):
                        attend(seg, (lambda seg=seg: finalize(seg - 1)) if seg > 0 else None)
                finalize(7)
        P.barrier()
    if upto >= 3:
        P.tag = 'mlstm'
        W.reset()
        Gpan = W.alloc([KC, 32], BF16)
        Gt = W.alloc([NB, 32], F32)
        LF = W.alloc([NB, 2, 8], F32)
        IV = W.alloc([NB, 2, 8], F32)
        CS = Gt
        Et = W.alloc([NB, 16], F32)
        Ut = W.alloc([NB, 16], F32)
        EBt = W.alloc([NB, 16], F32)
        At = W.alloc([NB, 16], F32)
        bg_bc = W.alloc([32], F32)
        Umask = W.alloc([128], F32)
        Lmask = W.alloc([128], F32)
        ones_f = W.alloc([128], F32)
        SCL = W.alloc([NSEG, 16], F32)
        EM0 = W.alloc([16], F32)
        keepc = W.alloc([1], F32)
        mln_bc = W.alloc([256], F32)
        amaxc = W.alloc([2], F32)
        mrow = W.alloc([16 * 16 + 16 * 16 + 16 + 16 + NSEG * 16], F32)
        VOpan = W.alloc([KC, 256], BF16)
        QaT = W.alloc([T], BF16)
        KaTpad = [W.alloc([T], BF16) for _ in range(2)]
        Katok = W.alloc([NB, 128], BF16)
        Vaext = W.alloc([NB, 2, 130], BF16)
        Hbuf = W.alloc([NB, 2, 128], BF16)
        _hb = Hbuf.rearrange('p b h d -> p (b h d)')
        sgoT = W.alloc([2, T], BF16)
        QKpan = [_hb[:, i * 2048:(i + 1) * 2048].rearrange('p (k c) -> p k c', c=128) for i in range(2)]
        Cst2 = [W.alloc([130], F32) for _ in range(2)]
        EBp = W.alloc([NB, 2], F32)
        SCLp = W.alloc([NSEG, 2], F32)
        smd = [[W.alloc([4], F32) for _ in range(2)] for _ in range(2)]
        Cb = [[W.alloc([130], BF16) for _ in range(2)] for _ in range(2)]
        Pm8 = [[W.alloc([128], BF16) for _ in range(4)] for _ in range(2)]
        Kupad = [[W.alloc([128], BF16) for _ in range(2)] for _ in range(2)]
        stg = [W.alloc([130], F32) for _ in range(2)]
        sgtmp = W.alloc([512], F32)
        yat4 = [W.alloc([128], BF16) for _ in range(4)]
        mtmp4 = [W.alloc([128], F32) for _ in range(4)]
        msm = [W.alloc([8], F32) for _ in range(4)]
        msm8 = [[W.alloc([2], F32) for _ in range(4)] for _ in range(2)]
        mjunk = W.alloc([128], BF16)

        b_g = B("gates")
        b_mc = B("mconst")
        b_mrow = B("mrow")
        b_scl = B("scl")
        b_QKpan = P.bufs(2, "QKpan")
        b_VOpan = B("VOpan")
        b_QaT = B("QaT")
        b_KaT = P.bufs(2, "KaT")
        b_Katok = B("Katok")
        b_Va = B("Va")
        b_Hb = [[B(f"Hb{b}_{hl}") for hl in range(2)] for b in range(NB)]
        b_Cst2 = [B(f"Cst2{d_}") for d_ in range(2)]
        b_EBp = B("EBp")
        b_smd = [[B(f"smd{a_}{d_}") for d_ in range(2)] for a_ in range(2)]
        b_sgoT = B("sgoT")
        b_sgtmp = B("sgtmp")
        b_Cb = [[B(f"Cb{d_}{hl}") for hl in range(2)] for d_ in range(2)]
        b_Pm8 = [P.bufs(4, "PmA"), P.bufs(4, "PmB")]
        b_msm8 = [P.bufs(4, "msmA"), P.bufs(4, "msmB")]
        b_Ku = [[B(f"Ku{d_}{hl}") for hl in range(2)] for d_ in range(2)]
        b_stg = P.bufs(2, "stg")
        b_yat4 = P.bufs(4, "yat")
        b_mtmp4 = P.bufs(4, "mtmp")
        b_msm = P.bufs(4, "msm")
        b_mjunk = B("mjunk")

        P.dma("sp", "mc0", lambda e: e.dma_start(out=bg_bc, in_=bgates_d.partition_broadcast(128)), writes=[b_mc])
        P.dma("sp", "mc1", lambda e: e.dma_start(out=Umask, in_=umask_d), writes=[b_mc])
        P.dma("sp", "mc2", lambda e: e.dma_start(out=Lmask, in_=lmask_d), writes=[b_mc])
        P.dma("sp", "mc3", lambda e: e.dma_start(out=EM0, in_=m0_d.rearrange("a b -> (a b)").partition_broadcast(128)),
              writes=[b_mc])
        P.dma("sp", "mc4", lambda e: e.dma_start(out=keepc, in_=keep_d.partition_broadcast(128)), writes=[b_mc])
        P.op("pool", lambda e: e.memset(ones_f, 1.0), writes=[b_mc])
        P.op("pool", lambda e: e.memset(Vaext[:, :, :, 128:129], 1.0), writes=[b_Va])
        for hl in range(2):
            P.op("pool", lambda e, hl=hl: e.memset(KaTpad[hl], 0.0), writes=[b_KaT[hl]])
            for d_ in range(2):
                P.op("pool", lambda e, hl=hl, d_=d_: e.memset(Kupad[d_][hl], 0.0), writes=[b_Ku[d_][hl]])
        MR_A, MR_T, MR_M, MR_X, MR_O = 0, 256, 512, 528, 544
        P.op("pool", lambda e: e.tensor_copy(out=mrow[0:1, MR_M:MR_M + 16], in_=EM0[0:1, :]), reads=[b_mc], writes=[b_mrow])
        P.op("act", lambda e: e.activation(out=EM0, in_=EM0, func=AF.Exp), reads=[b_mc, b_mrow], writes=[b_mc])

        P.dma("pool", "Gpan", lambda e: e.dma_start(out=Gpan, in_=w_in_r[:, :, OFF_G:OFF_G + 32]), writes=[b_g])
        for b in range(NB):
            for k in range(KC):
                P.op("pe", lambda e, b=b, k=k: e.matmul(psum[4][:, b * 32:(b + 1) * 32], lhsT=hT[:, k, b * 128:(b + 1) * 128],
                                                        rhs=Gpan[:, k, :], start=(k == 0), stop=(k == KC - 1)),
                     reads=[b_hT[b], b_g], writes=[b_ps[4]], inc=(k == KC - 1))
        P.op("dve", lambda e: e.tensor_tensor(out=Gt, in0=psum[4][:].rearrange("p (b c) -> p b c", c=32),
                                              in1=bg_bc.unsqueeze(1).to_broadcast([128, NB, 32]), op=ALU.add),
             reads=[b_ps[4], b_mc], writes=[b_g])
        P.op("act", lambda e: e.activation(out=Gt, in_=Gt, func=AF.Tanh, scale=1.0 / GATE_CAP), reads=[b_g], writes=[b_g])
        G4 = Gt.rearrange("p b (t h) -> p b t h", h=8)
        P.op("dve", lambda e: e.tensor_scalar(out=IV, in0=G4[:, :, 0::2, :], scalar1=GATE_CAP, scalar2=None, op0=ALU.mult),
             reads=[b_g], writes=[b_g])
        P.op("act", lambda e: e.activation(out=LF, in_=G4[:, :, 1::2, :], func=AF.Exp, scale=-GATE_CAP), reads=[b_g], writes=[b_g])
        P.op("act", lambda e: e.activation(out=LF, in_=LF, func=AF.Ln, bias=1.0), reads=[b_g], writes=[b_g])
        P.op("dve", lambda e: e.tensor_scalar(out=LF, in0=LF, scalar1=-1.0, scalar2=None, op0=ALU.mult), reads=[b_g], writes=[b_g])
        for b in range(NB):
            P.op("pe", lambda e, b=b: e.matmul(psum[5][:, b * 32:b * 32 + 8], lhsT=Umask, rhs=LF[:, b, 0, :], start=True, stop=True),
                 reads=[b_g, b_mc], writes=[b_ps[5]], inc=False)
            P.op("pe", lambda e, b=b: e.matmul(psum[5][:, b * 32 + 8:b * 32 + 16], lhsT=Lmask, rhs=LF[:, b, 1, :], start=True, stop=True),
                 reads=[b_g, b_mc], writes=[b_ps[5]], inc=False)
            P.op("pe", lambda e, b=b: e.matmul(psum[5][:, b * 32 + 16:b * 32 + 32], lhsT=ones_f,
                                               rhs=LF[:, b, :, :].rearrange("p t h -> p (t h)"), start=True, stop=True),
                 reads=[b_g, b_mc], writes=[b_ps[5]], inc=True)
        P.op("dve", lambda e: e.tensor_copy(out=CS, in_=psum[5][:].rearrange("p (b c) -> p b c", c=32)),
             reads=[b_ps[5]], writes=[b_g])
        IV16 = IV.rearrange("p b t h -> p b (t h)")
        P.op("act", lambda e: e.activation(out=Et, in_=CS[:, :, 0:16], func=AF.Exp), reads=[b_g], writes=[b_g])
        P.op("act", lambda e: e.activation(out=EBt, in_=CS[:, :, 16:32], func=AF.Exp), reads=[b_g], writes=[b_g])
        P.op("dve", lambda e: e.tensor_tensor(out=At, in0=IV16, in1=CS[:, :, 0:16], op=ALU.subtract), reads=[b_g], writes=[b_g])
        P.op("act", lambda e: e.activation(out=Ut, in_=At, func=AF.Exp), reads=[b_g], writes=[b_g])

        Aflat = At.rearrange("p b c -> p (b c)")
        for half in range(2):
            P.op("pe", lambda e, half=half: e.transpose(out=psum[6][:, 0:128], in_=Aflat[:, half * 128:(half + 1) * 128],
                                                        identity=ident_f[:]), reads=[b_g, b_ident], writes=[b_ps[6]])
            P.op("dve", lambda e, half=half: e.tensor_reduce(out=amaxc[:, half:half + 1], in_=psum[6][:, 0:128], axis=AX.X, op=ALU.max),
                 reads=[b_ps[6]], writes=[b_mc])
        for half in range(2):
            P.op("pe", lambda e, half=half: e.transpose(out=psum[6][0:1, half * 128:(half + 1) * 128], in_=amaxc[:, half:half + 1],
                                                        identity=ident_f[:]), reads=[b_mc, b_ident], writes=[b_ps[6]])
        P.op("dve", lambda e: e.tensor_copy(out=mrow[0:1, MR_A:MR_A + 256], in_=psum[6][0:1, 0:256]), reads=[b_ps[6]], writes=[b_mrow])
        P.op("dve", lambda e: e.tensor_copy(out=mrow[0:1, MR_T:MR_T + 256].rearrange("p (b c) -> p b c", c=16),
                                            in_=CS[0:1, :, 16:32]), reads=[b_g], writes=[b_mrow])

        def mr(off, n=8):
            return mrow[0:1, off:off + n]
        for d_ in range(2):
            order = list(range(NB)) if d_ == 0 else list(range(NB - 1, -1, -1))
            mcur = mr(MR_M + d_ * 8)
            for idx, blk in enumerate(order):
                seg = blk // 2
                first_of_seg = (blk % 2 == 0) if d_ == 0 else (blk % 2 == 1)
                if first_of_seg and idx > 0:
                    P.op("dve", lambda e, mcur=mcur: e.tensor_scalar(out=mcur, in0=mcur, scalar1=keepc[0:1, 0:1], scalar2=None,
                                                                      op0=ALU.mult), reads=[b_mrow, b_mc], writes=[b_mrow])
                am = mr(MR_A + blk * 16 + d_ * 8)
                tt_ = mr(MR_T + blk * 16 + d_ * 8)
                P.op("dve", lambda e, mcur=mcur, am=am: e.tensor_tensor(out=mcur, in0=mcur, in1=am, op=ALU.max),
                     reads=[b_mrow], writes=[b_mrow])
                P.op("dve", lambda e, mcur=mcur, tt_=tt_: e.tensor_tensor(out=mcur, in0=mcur, in1=tt_, op=ALU.add),
                     reads=[b_mrow], writes=[b_mrow])
                if not first_of_seg:
                    mo = mr(MR_O + seg * 16 + d_ * 8)
                    P.op("dve", lambda e, mcur=mcur, mo=mo: e.tensor_copy(out=mo, in_=mcur), reads=[b_mrow], writes=[b_mrow])
        P.dma("sp", "nm", lambda e: e.dma_start(out=nm_d.rearrange("s a h -> (s a h)").rearrange("(o n) -> o n", o=1),
                                                in_=mrow[0:1, MR_O:MR_O + NSEG * 16]), reads=[b_mrow])
        P.op("pe", lambda e: e.matmul(psum[6][:, 0:128], lhsT=ones_f[0:1, :], rhs=mrow[0:1, MR_O:MR_O + 128], start=True, stop=True),
             reads=[b_mrow, b_mc], writes=[b_ps[6]])
        P.op("act", lambda e: e.activation(out=SCL.rearrange("p s c -> p (s c)"), in_=psum[6][:, 0:128], func=AF.Exp, scale=-1.0),
             reads=[b_ps[6]], writes=[b_scl])

        import os
        NHP = int(os.environ.get('M_HP', '4'))
        for hp in range(NHP):
            P.barrier()
            P.dma("pool", "mc5", lambda e, hp=hp: e.dma_start(out=mln_bc, in_=mlnorm_d[hp * 256:(hp + 1) * 256].partition_broadcast(128)),
                  writes=[b_mc])
            P.dma("pool", "QKpan0", lambda e, hp=hp: e.dma_start(out=QKpan[0], in_=w_in_r[:, :, OFF_QA + hp * 128:OFF_QA + (hp + 1) * 128]),
                  writes=[b_QKpan[0]])
            P.dma("pool", "QKpan1", lambda e, hp=hp: e.dma_start(out=QKpan[1], in_=w_in_r[:, :, OFF_KA + hp * 128:OFF_KA + (hp + 1) * 128]),
                  writes=[b_QKpan[1]])
            P.dma("pool", "VOpan", lambda e, hp=hp: e.dma_start(out=VOpan, in_=w_in_r[:, :, OFF_VA + hp * 256:OFF_VA + (hp + 1) * 256]),
                  writes=[b_VOpan])
            for which in range(2):
                for tt in range(4):
                    pb = 4 + tt % 2
                    for k in range(KC):
                        P.op("pe", lambda e, which=which, tt=tt, k=k, pb=pb: e.matmul(
                            psum[pb][:], lhsT=QKpan[which][:, k, :], rhs=hT[:, k, tt * 512:(tt + 1) * 512],
                            start=(k == 0), stop=(k == KC - 1)),
                            reads=[b_QKpan[which]] + b_hT[4 * tt:4 * tt + 4], writes=[b_ps[pb]], inc=(k == KC - 1))
                    if which == 0:
                        P.op("act", lambda e, tt=tt, pb=pb: e.activation(out=QaT[:, tt * 512:(tt + 1) * 512], in_=psum[pb][:],
                                                                         func=AF.Copy, scale=0.125),
                             reads=[b_ps[pb]], writes=[b_QaT])
                    else:
                        P.op("act", lambda e, tt=tt, pb=pb: e.copy(out=KaTpad[0][0:64, tt * 512:(tt + 1) * 512], in_=psum[pb][0:64, :]),
                             reads=[b_ps[pb]], writes=[b_KaT[0]])
                        P.op("dve", lambda e, tt=tt, pb=pb: e.tensor_copy(out=KaTpad[1][64:128, tt * 512:(tt + 1) * 512],
                                                                          in_=psum[pb][64:128, :]),
                             reads=[b_ps[pb]], writes=[b_KaT[1]])
            for b in range(NB):
                for hl in range(2):
                    P.op("pe", lambda e, b=b, hl=hl: e.transpose(out=psum_b[7][:, hl * 128:(hl + 1) * 128],
                                                                 in_=KaTpad[hl][:, b * 128:(b + 1) * 128], identity=ident_b[:]),
                         reads=[b_KaT[hl], b_ident], writes=[b_ps[7]], inc=(hl == 1))
                for hl in range(2):
                    P.op("dve", lambda e, b=b, hl=hl: e.tensor_copy(out=Katok[:, b, hl * 64:(hl + 1) * 64],
                                                                    in_=psum_b[7][:, hl * 128 + hl * 64:hl * 128 + (hl + 1) * 64]),
                         reads=[b_ps[7]], writes=[b_Katok])
            for b in range(NB):
                pb = 4 + b % 2
                for k in range(KC):
                    P.op("pe", lambda e, b=b, k=k, pb=pb: e.matmul(psum[pb][:, 0:256], lhsT=hT[:, k, b * 128:(b + 1) * 128],
                                                                   rhs=VOpan[:, k, :], start=(k == 0), stop=(k == KC - 1)),
                         reads=[b_hT[b], b_VOpan], writes=[b_ps[pb]], inc=(k == KC - 1))
                P.op("act", lambda e, b=b, pb=pb: e.copy(out=Vaext[:, b, :, 0:128],
                                                         in_=psum[pb][:, 0:256].rearrange("p (h d) -> p h d", d=128)),
                     reads=[b_ps[pb]], writes=[b_Va])
            P.barrier()
            P.dma("pool", "VOpan", lambda e, hp=hp: e.dma_start(out=VOpan, in_=w_in_r[:, :, OFF_OA + hp * 256:OFF_OA + (hp + 1) * 256]),
                  writes=[b_VOpan])
            for cc in range(2):
                for tt in range(4):
                    pb = 4 + tt % 2
                    for k in range(KC):
                        P.op("pe", lambda e, cc=cc, tt=tt, k=k, pb=pb: e.matmul(
                            psum[pb][:], lhsT=VOpan[:, k, cc * 128:(cc + 1) * 128], rhs=hT[:, k, tt * 512:(tt + 1) * 512],
                            start=(k == 0), stop=(k == KC - 1)),
                            reads=[b_VOpan] + b_hT[4 * tt:4 * tt + 4], writes=[b_ps[pb]], inc=(k == KC - 1))
                    P.op("act", lambda e, pb=pb: e.activation(out=sgtmp, in_=psum[pb][:], func=AF.Exp, scale=-1.0),
                         reads=[b_ps[pb]], writes=[b_sgtmp])
                    P.op("dve", lambda e: e.tensor_scalar(out=sgtmp, in0=sgtmp, scalar1=1.0, scalar2=None, op0=ALU.add),
                         reads=[b_sgtmp], writes=[b_sgtmp])
                    P.op("dve", lambda e: e.reciprocal(out=sgtmp, in_=sgtmp), reads=[b_sgtmp], writes=[b_sgtmp])
                    P.op("act", lambda e, cc=cc, tt=tt: e.copy(out=sgoT[:, cc, tt * 512:(tt + 1) * 512], in_=sgtmp),
                         reads=[b_sgtmp], writes=[b_sgoT])
            for d_ in range(2):
                for hl in range(2):
                    rows = slice(hl * 64, (hl + 1) * 64)
                    dh = d_ * 8 + 2 * hp + hl
                    P.op("dve", lambda e, d_=d_, rows=rows, dh=dh: e.tensor_copy(out=EBp[rows, :, d_:d_ + 1], in_=EBt[rows, :, dh:dh + 1]),
                         reads=[b_g], writes=[b_EBp])
                    P.op("dve", lambda e, d_=d_, rows=rows, dh=dh: e.tensor_copy(out=SCLp[rows, :, d_:d_ + 1], in_=SCL[rows, :, dh:dh + 1]),
                         reads=[b_scl], writes=[b_EBp])
            for d_ in range(2):
                P.op("pool", lambda e, d_=d_: e.memset(Cst2[d_], 0.0), writes=[b_Cst2[d_]])
                for hl in range(2):
                    h = 2 * hp + hl
                    dh = d_ * 8 + h
                    rows = slice(hl * 64, (hl + 1) * 64)
                    P.op("pool", lambda e, d_=d_, hl=hl: e.memset(Cb[d_][hl], 0.0), writes=[b_Cb[d_][hl]])
                    P.dma("pool", f"c0{d_}{hl}", lambda e, d_=d_, h=h, rows=rows: e.dma_start(
                        out=Cst2[d_][rows, 0:128], in_=c0_d[d_, h]), writes=[b_Cst2[d_]])
                    P.dma("pool", f"n0{d_}{hl}", lambda e, d_=d_, h=h, rows=rows: e.dma_start(
                        out=Cst2[d_][rows, 128:129], in_=n0_d[d_, h].rearrange("(p o) -> p o", o=1)), writes=[b_Cst2[d_]])
                    P.op("dve", lambda e, d_=d_, rows=rows, dh=dh: e.tensor_scalar(
                        out=Cst2[d_][rows, 0:129], in0=Cst2[d_][rows, 0:129], scalar1=EM0[rows, dh:dh + 1], scalar2=None,
                        op0=ALU.mult), reads=[b_Cst2[d_], b_mc], writes=[b_Cst2[d_]])
                    P.op("act", lambda e, d_=d_, hl=hl, rows=rows: e.copy(out=Cb[d_][hl][rows, 0:129], in_=Cst2[d_][rows, 0:129]),
                         reads=[b_Cst2[d_]], writes=[b_Cb[d_][hl]])
            for j in range(NB):
                units = []
                for d_ in range(2):
                    blk = j if d_ == 0 else NB - 1 - j
                    for hl in range(2):
                        units.append((d_ * 2 + hl, d_, hl, blk))
                jp = j % 2
                sbk = jp
                nbks = (2, 3) if jp == 0 else (4, 5)
                gbks = (6, 7)
                last = (j == NB - 1)

                def geo(u, d_, hl, blk):
                    h = 2 * hp + hl
                    dh = d_ * 8 + h
                    rows = slice(hl * 64, (hl + 1) * 64)
                    cols = slice(blk * 128, (blk + 1) * 128)
                    return h, dh, rows, cols
                for (u, d_, hl, blk) in units:
                    h, dh, rows, cols = geo(u, d_, hl, blk)
                    P.op("pe", lambda e, u=u, hl=hl, cols=cols, sbk=sbk: e.matmul(
                        psum[sbk][:, u * 128:(u + 1) * 128], lhsT=KaTpad[hl][:, cols], rhs=QaT[:, cols], start=True, stop=True),
                        reads=[b_KaT[hl], b_QaT], writes=[b_ps[sbk]], inc=(u == 3))
                for (u, d_, hl, blk) in units:
                    h, dh, rows, cols = geo(u, d_, hl, blk)
                    ucol = Ut[:, blk, dh:dh + 1]
                    P.op("act", lambda e, d_=d_, hl=hl, blk=blk, ucol=ucol: e.activation(
                        out=Kupad[d_][hl][:, hl * 64:(hl + 1) * 64], in_=Katok[:, blk, hl * 64:(hl + 1) * 64],
                        func=AF.Copy, scale=ucol), reads=[b_Katok, b_g], writes=[b_Ku[d_][hl]])
                for (u, d_, hl, blk) in units:
                    h, dh, rows, cols = geo(u, d_, hl, blk)
                    ucol = Ut[:, blk, dh:dh + 1]
                    mask = Umask if d_ == 0 else Lmask
                    pm = Pm8[jp][u]
                    P.op("dve", lambda e, u=u, sbk=sbk, pm=pm, ucol=ucol, mask=mask: e.scalar_tensor_tensor(
                        out=pm, in0=psum[sbk][:, u * 128:(u + 1) * 128], scalar=ucol, in1=mask, op0=ALU.mult, op1=ALU.mult),
                        reads=[b_ps[sbk], b_g, b_mc], writes=[b_Pm8[jp][u]])
                for (u, d_, hl, blk) in units:
                    h, dh, rows, cols = geo(u, d_, hl, blk)
                    gbk = gbks[u // 2]
                    P.op("pe", lambda e, gbk=gbk, d_=d_, hl=hl, blk=blk, u=u: e.matmul(
                        psum[gbk][:, 0:129], lhsT=Kupad[d_][hl], rhs=Vaext[:, blk, hl, 0:129], start=(u % 2 == 0), stop=(u % 2 == 1),
                        skip_group_check=True),
                        reads=[b_Ku[d_][hl], b_Va], writes=[b_ps[gbk]], inc=(u % 2 == 1))
                for d_ in range(2):
                    blk = j if d_ == 0 else NB - 1 - j
                    gbk = gbks[d_]
                    Cs = Cst2[d_]
                    P.op("dve", lambda e, Cs=Cs, gbk=gbk: e.tensor_tensor(out=Cs[:, 0:129], in0=psum[gbk][:, 0:129], in1=Cs[:, 0:129], op=ALU.add),
                         reads=[b_ps[gbk], b_Cst2[d_]], writes=[b_Cst2[d_]])
                for d_ in range(2):
                    blk = j if d_ == 0 else NB - 1 - j
                    Cs = Cst2[d_]
                    P.op("dve", lambda e, Cs=Cs, blk=blk, d_=d_: e.tensor_scalar(
                        out=Cs[:, 0:129], in0=Cs[:, 0:129], scalar1=EBp[:, blk, d_:d_ + 1], scalar2=None, op0=ALU.mult),
                        reads=[b_Cst2[d_], b_EBp], writes=[b_Cst2[d_]])
                for d_ in range(2):
                    blk = j if d_ == 0 else NB - 1 - j
                    seg = blk // 2
                    seg_end = (blk % 2 == 1) if d_ == 0 else (blk % 2 == 0)
                    Cs = Cst2[d_]
                    if seg_end:
                        st = stg[d_]
                        P.op("dve", lambda e, Cs=Cs, st=st, seg=seg, d_=d_: e.tensor_scalar(
                            out=st[:, 0:129], in0=Cs[:, 0:129], scalar1=SCLp[:, seg, d_:d_ + 1], scalar2=None, op0=ALU.mult),
                            reads=[b_Cst2[d_], b_EBp], writes=[b_stg[d_]])
                        P.dma("sp", f"stg{d_}", lambda e, st=st, seg=seg, d_=d_, hp=hp: e.dma_start(
                            out=nC_d[seg, d_, 2 * hp:2 * hp + 2].rearrange("h p c -> (h p) c"), in_=st[:, 0:128]), reads=[b_stg[d_]])
                        P.dma("sp", f"stg{d_}", lambda e, st=st, seg=seg, d_=d_, hp=hp: e.dma_start(
                            out=nn_d[seg, d_, 2 * hp:2 * hp + 2].rearrange("h (p o) -> (h p) o", o=1), in_=st[:, 128:129]), reads=[b_stg[d_]])
                        if not last:
                            P.op("dve", lambda e, Cs=Cs: e.tensor_scalar(
                                out=Cs[:, 0:129], in0=Cs[:, 0:129], scalar1=keepc[:, 0:1], scalar2=None, op0=ALU.mult),
                                reads=[b_Cst2[d_], b_mc], writes=[b_Cst2[d_]])
                for (u, d_, hl, blk) in units:
                    h, dh, rows, cols = geo(u, d_, hl, blk)
                    nbk = nbks[u // 2]
                    c0 = (u % 2) * 129
                    pm = Pm8[jp][u]
                    P.op("pe", lambda e, nbk=nbk, c0=c0, pm=pm, blk=blk, hl=hl, u=u: e.matmul(
                        psum[nbk][:, c0:c0 + 129], lhsT=pm, rhs=Vaext[:, blk, hl, 0:129], start=(u % 2 == 0), stop=False,
                        skip_group_check=True),
                        reads=[b_Pm8[jp][u], b_Va], writes=[b_ps[nbk]], inc=False)
                    P.op("pe", lambda e, nbk=nbk, c0=c0, cols=cols, d_=d_, hl=hl: e.matmul(
                        psum[nbk][:, c0:c0 + 129], lhsT=QaT[:, cols], rhs=Cb[d_][hl][:, 0:129], start=False, stop=True,
                        skip_group_check=True),
                        reads=[b_QaT, b_Cb[d_][hl]], writes=[b_ps[nbk]], inc=(u % 2 == 1))
                if not last:
                    for (u, d_, hl, blk) in units:
                        h, dh, rows, cols = geo(u, d_, hl, blk)
                        Cs = Cst2[d_]
                        P.op("act", lambda e, Cs=Cs, rows=rows, d_=d_, hl=hl: e.copy(out=Cb[d_][hl][rows, 0:129], in_=Cs[rows, 0:129]),
                             reads=[b_Cst2[d_]], writes=[b_Cb[d_][hl]])
                for d_ in range(2):
                    blk = j if d_ == 0 else NB - 1 - j
                    nbk = nbks[d_]
                    sd = smd[jp][d_]
                    P.op("act", lambda e, nbk=nbk, sd=sd: e.activation(out=sd[:, 0:2], in_=psum[nbk][:, 128:258:129], func=AF.Abs),
                         reads=[b_ps[nbk]], writes=[b_smd[jp][d_]])
                for which in range(4):
                    for d_ in range(2):
                        blk = j if d_ == 0 else NB - 1 - j
                        dh0 = d_ * 8 + 2 * hp
                        E2 = Et[:, blk, dh0:dh0 + 2]
                        sd = smd[jp][d_]
                        bsd = b_smd[jp][d_]
                        if which == 0:
                            P.op("dve", lambda e, sd=sd, E2=E2: e.tensor_tensor(out=sd[:, 0:2], in0=sd[:, 0:2], in1=E2, op=ALU.mult),
                                 reads=[bsd, b_g], writes=[bsd])
                        elif which == 1:
                            P.op("dve", lambda e, sd=sd: e.tensor_scalar(out=sd[:, 0:2], in0=sd[:, 0:2], scalar1=1.0, scalar2=None, op0=ALU.max),
                                 reads=[bsd], writes=[bsd])
                        elif which == 2:
                            P.op("dve", lambda e, sd=sd: e.reciprocal(out=sd[:, 0:2], in_=sd[:, 0:2]), reads=[bsd], writes=[bsd])
                        else:
                            P.op("dve", lambda e, sd=sd, E2=E2: e.tensor_tensor(out=sd[:, 2:4], in0=sd[:, 0:2], in1=E2, op=ALU.mult),
                                 reads=[bsd, b_g], writes=[bsd])
                for (u, d_, hl, blk) in units:
                    h, dh, rows, cols = geo(u, d_, hl, blk)
                    nbk = nbks[u // 2]
                    c0 = (u % 2) * 129
                    sd = smd[jp][d_]
                    bsd = b_smd[jp][d_]
                    rcol = sd[:, 2 + hl:3 + hl]
                    if j < NB // 2:
                        P.op("act", lambda e, nbk=nbk, c0=c0, rcol=rcol, blk=blk, hl=hl: e.activation(
                            out=Hbuf[:, blk, hl, :], in_=psum[nbk][:, c0:c0 + 128], func=AF.Copy, scale=rcol),
                            reads=[b_ps[nbk], bsd], writes=[b_Hb[blk][hl]])
                    else:
                        P.op("dve", lambda e, nbk=nbk, c0=c0, rcol=rcol, blk=blk, hl=hl: e.scalar_tensor_tensor(
                            out=Hbuf[:, blk, hl, :], in0=psum[nbk][:, c0:c0 + 128], scalar=rcol, in1=Hbuf[:, blk, hl, :],
                            op0=ALU.mult, op1=ALU.add),
                            reads=[b_ps[nbk], bsd, b_Hb[blk][hl]], writes=[b_Hb[blk][hl]])
            for b in range(NB):
                bp = b % 2
                pbt = 6 + bp
                for hl in range(2):
                    h = 2 * hp + hl
                    q_ = bp * 2 + hl
                    sm_ = msm[q_]
                    bsm = b_msm[q_]
                    hs = Hbuf[:, b, hl, :]
                    ya_ = yat4[q_]
                    P.op("act", lambda e, hs=hs, sm_=sm_: e.activation(out=mjunk, in_=hs, func=AF.Square, accum_out=sm_[:, 2:3]),
                         reads=[b_Hb[b][hl]], writes=[b_mjunk, bsm])
                    P.op("act", lambda e, sm_=sm_: e.activation(out=sm_[:, 3:4], in_=sm_[:, 2:3], func=AF.Ln, scale=1.0 / 128, bias=EPS),
                         reads=[bsm], writes=[bsm])
                    P.op("act", lambda e, sm_=sm_: e.activation(out=sm_[:, 4:5], in_=sm_[:, 3:4], func=AF.Exp, scale=-0.5),
                         reads=[bsm], writes=[bsm])
                    P.op("dve", lambda e, hs=hs, sm_=sm_, hl=hl, ya_=ya_: e.scalar_tensor_tensor(
                        out=ya_, in0=hs, scalar=sm_[:, 4:5], in1=mln_bc[:, hl * 128:(hl + 1) * 128], op0=ALU.mult, op1=ALU.mult),
                        reads=[b_Hb[b][hl], bsm, b_mc], writes=[b_yat4[q_]])
                    P.op("pe", lambda e, hl=hl, ya_=ya_, pbt=pbt: e.transpose(out=psum_b[pbt][:, hl * 128:(hl + 1) * 128], in_=ya_,
                                                                              identity=ident_b[:]),
                         reads=[b_yat4[q_], b_ident], writes=[b_ps[pbt]])
                    P.op("dve", lambda e, hl=hl, h=h, b=b, pbt=pbt: e.tensor_tensor(
                        out=yT[:, h, b * 128:(b + 1) * 128], in0=psum_b[pbt][:, hl * 128:(hl + 1) * 128],
                        in1=sgoT[:, hl, b * 128:(b + 1) * 128], op=ALU.mult),
                        reads=[b_ps[pbt], b_sgoT], writes=[b_yT[h][b]])
        P.barrier()
    mT_d = nc.dram_tensor("mT_scr", [KC, 128, T], BF16).ap()
    x1_d = nc.dram_tensor("x1_scr", [T, D], F32).ap()
    w_pa_r = w_pa_d.rearrange("(k p) c -> p k c", p=128)
    w_pb_r = w_pb_d.rearrange("(k p) c -> p k c", p=128)
    w_out_r = w_out_d.rearrange("(k p) c -> p k c", p=128)
    w_up_r = w_up_d.rearrange("(k p) c -> p k c", p=128)
    w_dn_r = w_down_d.rearrange("(k p) c -> p k c", p=128)
    if upto >= 4:
        P.tag = 'ph3a'
        W.reset()
        pa_pan = [W.alloc([8, 128], BF16) for _ in range(2)]
        pb_pan = [W.alloc([8, 128], BF16) for _ in range(2)]
        ga_pan = [W.alloc([KC, 128], BF16) for _ in range(2)]
        gb_pan = [W.alloc([KC, 128], BF16) for _ in range(2)]
        sga = [W.alloc([512], F32) for _ in range(2)]
        sgb = [W.alloc([512], F32) for _ in range(2)]
        m1 = [W.alloc([512], F32) for _ in range(2)]
        m2 = [W.alloc([512], F32) for _ in range(2)]
        mTc = [W.alloc([T], BF16) for _ in range(2)]
        b_pan3 = [P.bufs(2, "pa"), P.bufs(2, "pb"), P.bufs(2, "ga"), P.bufs(2, "gb")]
        b_sga = P.bufs(2, "sga")
        b_sgb = P.bufs(2, "sgb")
        b_m1 = P.bufs(2, "m1")
        b_m2 = P.bufs(2, "m2")
        b_mTc = P.bufs(2, "mTc")
        all_hT = list(b_hT)
        for c in range(KC):
            s_ = c % 2
            cc = slice(c * 128, (c + 1) * 128)
            P.dma("pool", f"pa{s_}", lambda e, s_=s_, cc=cc: e.dma_start(out=pa_pan[s_], in_=w_pa_r[:, :, cc]), writes=[b_pan3[0][s_]])
            P.dma("pool", f"pb{s_}", lambda e, s_=s_, cc=cc: e.dma_start(out=pb_pan[s_], in_=w_pb_r[:, :, cc]), writes=[b_pan3[1][s_]])
            P.dma("pool", f"ga{s_}", lambda e, s_=s_, c=c: e.dma_start(
                out=ga_pan[s_], in_=w_in_r[:, :, OFF_GA + c * 128:OFF_GA + (c + 1) * 128]), writes=[b_pan3[2][s_]])
            P.dma("pool", f"gb{s_}", lambda e, s_=s_, c=c: e.dma_start(
                out=gb_pan[s_], in_=w_in_r[:, :, OFF_GB + c * 128:OFF_GB + (c + 1) * 128]), writes=[b_pan3[3][s_]])
            for tt in range(4):
                t_ = tt % 2
                tc_ = slice(tt * 512, (tt + 1) * 512)
                bk = [0 + t_, 2 + t_, 4 + t_, 6 + t_]
                yb_a = [b_yT[k][4 * tt + j] for k in range(8) for j in range(4)]
                yb_b = [b_yT[8 + k][4 * tt + j] for k in range(8) for j in range(4)]
                for k in range(8):
                    P.op("pe", lambda e, k=k, s_=s_, tc_=tc_, bk=bk: e.matmul(psum[bk[0]][:], lhsT=pa_pan[s_][:, k, :], rhs=yT[:, k, tc_],
                                                                             start=(k == 0), stop=(k == 7)),
                         reads=[b_pan3[0][s_]] + yb_a, writes=[b_ps[bk[0]]], inc=(k == 7))
                for k in range(KC):
                    P.op("pe", lambda e, k=k, s_=s_, tc_=tc_, bk=bk: e.matmul(psum[bk[1]][:], lhsT=ga_pan[s_][:, k, :], rhs=hT[:, k, tc_],
                                                                             start=(k == 0), stop=(k == KC - 1)),
                         reads=[b_pan3[2][s_]] + all_hT[4 * tt:4 * tt + 4], writes=[b_ps[bk[1]]], inc=(k == KC - 1))
                for k in range(8):
                    P.op("pe", lambda e, k=k, s_=s_, tc_=tc_, bk=bk: e.matmul(psum[bk[2]][:], lhsT=pb_pan[s_][:, k, :], rhs=yT[:, 8 + k, tc_],
                                                                             start=(k == 0), stop=(k == 7)),
                         reads=[b_pan3[1][s_]] + yb_b, writes=[b_ps[bk[2]]], inc=(k == 7))
                for k in range(KC):
                    P.op("pe", lambda e, k=k, s_=s_, tc_=tc_, bk=bk: e.matmul(psum[bk[3]][:], lhsT=gb_pan[s_][:, k, :], rhs=hT[:, k, tc_],
                                                                             start=(k == 0), stop=(k == KC - 1)),
                         reads=[b_pan3[3][s_]] + all_hT[4 * tt:4 * tt + 4], writes=[b_ps[bk[3]]], inc=(k == KC - 1))
                P.op("act", lambda e, t_=t_, bk=bk: e.activation(out=sga[t_], in_=psum[bk[1]][:], func=AF.Sigmoid),
                     reads=[b_ps[bk[1]]], writes=[b_sga[t_]])
                P.op("dve", lambda e, t_=t_, bk=bk: e.tensor_tensor(out=m1[t_], in0=psum[bk[0]][:], in1=sga[t_], op=ALU.mult),
                     reads=[b_ps[bk[0]], b_sga[t_]], writes=[b_m1[t_]])
                P.op("act", lambda e, t_=t_, bk=bk: e.activation(out=sgb[t_], in_=psum[bk[3]][:], func=AF.Sigmoid),
                     reads=[b_ps[bk[3]]], writes=[b_sgb[t_]])
                P.op("dve", lambda e, t_=t_, bk=bk: e.tensor_tensor(out=m2[t_], in0=psum[bk[2]][:], in1=sgb[t_], op=ALU.mult),
                     reads=[b_ps[bk[2]], b_sgb[t_]], writes=[b_m2[t_]])
                P.op("dve", lambda e, t_=t_, s_=s_, tc_=tc_: e.tensor_tensor(out=mTc[s_][:, tc_], in0=m1[t_], in1=m2[t_], op=ALU.add),
                     reads=[b_m1[t_], b_m2[t_]], writes=[b_mTc[s_]])
            P.dma("sp", f"mTc{s_}", lambda e, s_=s_, c=c: e.dma_start(out=mT_d[c], in_=mTc[s_]), reads=[b_mTc[s_]])
        P.barrier()

        P.tag = 'ph3b'
        A_.reset(); B_.reset(); W.reset()
        g1_bc = W.alloc([D], F32)
        wpan2 = [W.alloc([KC, 512], BF16) for _ in range(2)]
        b_wpan2 = P.bufs(2, "wpan2")
        bmb = W.alloc([512], F32)
        b_bmb = B("bmb")
        sc_bc = W.alloc([KC, 128], BF16)
        b_scbc = B("scbc")
        b_g1 = B("g1")
        mod_bcast(2, g1_bc, b_g1, wpan2, b_wpan2, bmb, b_bmb, (0, 1), sc_bc, b_scbc)
        mTt = A_.alloc([KC, 1024], BF16)
        b_mTt = B("mTt")
        xt = B_.alloc([8, D], F32)
        b_xt = P.bufs(8, "xt")
        x1s = [W.alloc([512], F32) for _ in range(2)]
        b_x1s = P.bufs(2, "x1s")
        ctr = 0
        for tile in range(2):
            t0 = tile * 1024
            P.dma("sp", "mTt", lambda e, t0=t0: e.dma_start(out=mTt, in_=mT_d[:, :, t0:t0 + 1024].rearrange("k p t -> p k t")),
                  writes=[b_mTt])
            for tb in range(8):
                P.dma("sp", f"xt{tb}", lambda e, t0=t0, tb=tb: e.dma_start(out=xt[:, tb, :], in_=x_d[t0 + tb * 128:t0 + (tb + 1) * 128, :]),
                      writes=[b_xt[tb]])
            for ct in range(4):
                s_ = ct % 2
                P.dma("pool", f"wpan{s_}", lambda e, s_=s_, ct=ct: e.dma_start(out=wpan2[s_], in_=w_out_r[:, :, ct * 512:(ct + 1) * 512]),
                      writes=[b_wpan2[s_]])
                for tb in range(8):
                    pb = ctr % 4
                    r2 = ctr % 2
                    ctr += 1
                    cols = slice(ct * 512, (ct + 1) * 512)
                    for k in range(KC):
                        P.op("pe", lambda e, k=k, pb=pb, tb=tb, s_=s_: e.matmul(
                            psum[pb][:], lhsT=mTt[:, k, tb * 128:(tb + 1) * 128], rhs=wpan2[s_][:, k, :],
                            start=(k == 0), stop=(k == KC - 1)), reads=[b_mTt, b_wpan2[s_]], writes=[b_ps[pb]], inc=(k == KC - 1))
                    P.op("dve", lambda e, pb=pb, r2=r2, cols=cols: e.tensor_tensor(out=x1s[r2], in0=psum[pb][:], in1=g1_bc[:, cols], op=ALU.mult),
                         reads=[b_ps[pb], b_g1], writes=[b_x1s[r2]])
                    P.op("dve", lambda e, r2=r2, tb=tb, cols=cols: e.tensor_tensor(out=xt[:, tb, cols], in0=xt[:, tb, cols], in1=x1s[r2], op=ALU.add),
                         reads=[b_x1s[r2], b_xt[tb]], writes=[b_xt[tb]])
                    if ct == 3:
                        P.dma("sp", f"x1o{tb}", lambda e, t0=t0, tb=tb: e.dma_start(out=x1_d[t0 + tb * 128:t0 + (tb + 1) * 128, :], in_=xt[:, tb, :]),
                              reads=[b_xt[tb]])
        P.barrier()
    if upto >= 5:
        P.tag = 'ph4'
        TL = 1024
        NW = 342
        ALL = Arena(big, 0, RA + RB + RW)
        g2_d = nc.dram_tensor("g2_scr", [128, D], F32).ap()
        cwT = ALL.alloc([3, 88], F32)
        cbT = ALL.alloc([88], F32)
        nk0T = ALL.alloc([88], F32)
        nk2T = ALL.alloc([88], F32)
        keepc4 = ALL.alloc([1], F32)
        tmp4 = [ALL.alloc([256], F32) for _ in range(2)]
        fsm = [ALL.alloc([4], F32) for _ in range(2)]
        gT_all = ALL.alloc([NFC, TL], BF16)
        U0 = ALL.off
        b_g2 = B("g2")
        b_c4 = B("c4")
        b_wd = B("wd")
        b_tmp4 = P.bufs(2, "tmp4")
        b_fsm = P.bufs(2, "fsm")
        g2_bc = ALL.alloc([D], F32)
        cw_sb = ALL.alloc([4, 128], F32)
        wpan3 = [ALL.alloc([KC, 512], BF16) for _ in range(2)]
        b_wpan3 = P.bufs(2, "wpan3")
        bmb3 = ALL.alloc([512], F32)
        b_bmb3 = B("bmb3")
        sc_bc3 = ALL.alloc([KC, 128], BF16)
        b_scbc3 = B("scbc3")
        mod_bcast(5, g2_bc, b_g2, wpan3, b_wpan3, bmb3, b_bmb3, (0, 1), sc_bc3, b_scbc3)
        P.dma("sp", "g2st", lambda e, src=g2_bc: e.dma_start(out=g2_d, in_=src), reads=[b_g2])
        P.dma("sp", "c41", lambda e: e.dma_start(out=cw_sb[0:88, 0:3, :], in_=convw_d.rearrange("t (c p) -> c t p", p=128)), writes=[b_c4])
        P.dma("sp", "c42", lambda e: e.dma_start(out=cw_sb[0:88, 3, :], in_=convb_d.rearrange("(c p) -> c p", p=128)), writes=[b_c4])
        P.dma("sp", "c43", lambda e: e.dma_start(out=keepc4, in_=keep_d.partition_broadcast(128)), writes=[b_c4])
        for t_ in range(4):
            P.op("pe", lambda e, t_=t_: e.transpose(out=psum[7][:, t_ * 88:(t_ + 1) * 88], in_=cw_sb[0:88, t_, :], identity=ident_f[0:88, 0:88]),
                 reads=[b_c4, b_ident], writes=[b_ps[7]], inc=(t_ == 3))
        P.op("dve", lambda e: e.tensor_copy(out=cwT, in_=psum[7][:, 0:264].rearrange("p (t c) -> p t c", c=88)), reads=[b_ps[7]], writes=[b_c4])
        P.op("dve", lambda e: e.tensor_copy(out=cbT, in_=psum[7][:, 264:352]), reads=[b_ps[7]], writes=[b_c4])
        P.op("dve", lambda e: e.tensor_scalar(out=keepc4, in0=keepc4, scalar1=-1.0, scalar2=None, op0=ALU.add), reads=[b_c4], writes=[b_c4])
        P.op("dve", lambda e: e.tensor_scalar(out=nk0T, in0=cwT[:, 0, :], scalar1=keepc4[:, 0:1], scalar2=None, op0=ALU.mult),
             reads=[b_c4], writes=[b_c4])
        P.op("dve", lambda e: e.tensor_scalar(out=nk2T, in0=cwT[:, 2, :], scalar1=keepc4[:, 0:1], scalar2=None, op0=ALU.mult),
             reads=[b_c4], writes=[b_c4])
        P.barrier()

        for tile in range(2):
            t0 = tile * TL
            ALL.off = U0
            h2T = ALL.alloc([KC, TL + 2], BF16)
            upan = [[ALL.alloc([KC, 512], BF16) for _ in range(2)] for _ in range(2)]
            ua = ALL.alloc([TL + 2], F32)
            acc = [ALL.alloc([TL], F32) for _ in range(2)]
            sil = ALL.alloc([TL], BF16)
            xb = [upan[1][0].rearrange("p k c -> p (k c)").bitcast(F32)[:, 0:D]]
            jk = upan[1][1].rearrange("p k c -> p (k c)")[:, 0:D]
            b_h2T = [B(f"h2T{tile}_{b}") for b in range(9)]
            b_upan = P.bufs(2, "upan")
            b_xb = [b_upan[1]]
            b_jk = b_upan[1]
            b_gT = [B(f"gT{tile}_{i}") for i in range(NFC)]
            b_ua = B("ua")
            b_acc = P.bufs(2, "acc")
            b_sil = B("sil")

            def gT(i):
                return gT_all[:, i, :]

            P.op("pool", lambda e: e.memset(h2T, 0.0), writes=b_h2T)
            P.op("pool", lambda e: e.memset(xb[0], 0.0), writes=[b_xb[0]])
            if tile == 1:
                P.dma("sp", "xb40", lambda e, t0=t0: e.dma_start(out=xb[0][0:1, :], in_=x1_d[t0 - 1:t0, :]), writes=[b_xb[0]])
            else:
                P.dma("sp", "xb40", lambda e, t0=t0: e.dma_start(out=xb[0][1:2, :], in_=x1_d[t0 + TL:t0 + TL + 1, :]), writes=[b_xb[0]])

            def norm_blk(src_fn, dsts, hb):
                if src_fn is not None:
                    P.dma("sp", "xb40", lambda e: e.dma_start(out=xb[0], in_=src_fn()), writes=[b_xb[0]])
                ssq = small[:, 40:41]
                P.op("act", lambda e: e.activation(out=jk, in_=xb[0], func=AF.Square, accum_out=ssq),
                     reads=[b_xb[0]], writes=[b_jk, b_small])
                P.op("act", lambda e: e.activation(out=ssq, in_=ssq, func=AF.Ln, scale=1.0 / D, bias=EPS), reads=[b_small], writes=[b_small])
                P.op("act", lambda e: e.activation(out=ssq, in_=ssq, func=AF.Exp, scale=-0.5), reads=[b_small], writes=[b_small])
                P.op("dve", lambda e: e.tensor_scalar(out=xb[0], in0=xb[0], scalar1=ssq, scalar2=None, op0=ALU.mult),
                     reads=[b_small, b_xb[0]], writes=[b_xb[0]])
                for k4 in range(4):
                    pb = 4 + k4
                    for kk in range(4):
                        k = k4 * 4 + kk
                        P.op("pe", lambda e, k=k, kk=kk, pb=pb: e.transpose(
                            out=psum[pb][:, kk * 128:(kk + 1) * 128], in_=xb[0][:, k * 128:(k + 1) * 128], identity=ident_f[:]),
                            reads=[b_xb[0], b_ident], writes=[b_ps[pb]], inc=(kk == 3))
                    for kk in range(4):
                        k = k4 * 4 + kk
                        for (pc0, n, dc0) in dsts:
                            if k4 % 2 == 0:
                                P.op("act", lambda e, k=k, kk=kk, pb=pb, pc0=pc0, n=n, dc0=dc0: e.activation(
                                    out=h2T[:, k, dc0:dc0 + n], in_=psum[pb][:, kk * 128 + pc0:kk * 128 + pc0 + n],
                                    func=AF.Identity, scale=s2T[:, k:k + 1], bias=modT[:, 48 + k:48 + k + 1]),
                                    reads=[b_ps[pb], b_s, b_modT], writes=[hb])
                            else:
                                P.op("dve", lambda e, k=k, kk=kk, pb=pb, pc0=pc0, n=n, dc0=dc0: e.tensor_scalar(
                                    out=h2T[:, k, dc0:dc0 + n], in0=psum[pb][:, kk * 128 + pc0:kk * 128 + pc0 + n],
                                    scalar1=s2T[:, k:k + 1], scalar2=modT[:, 48 + k:48 + k + 1], op0=ALU.mult, op1=ALU.add),
                                    reads=[b_ps[pb], b_s, b_modT], writes=[hb])
            if tile == 1:
                norm_blk(None, [(0, 1, 0)], b_h2T[8])
            else:
                norm_blk(None, [(1, 1, TL + 1)], b_h2T[8])
            for b in range(8):
                norm_blk(lambda b=b, t0=t0: x1_d[t0 + b * 128:t0 + (b + 1) * 128, :], [(0, 128, 1 + b * 128)], b_h2T[b])

            for i in range(NFC):
                grp, gi = divmod(i, 4)
                s_ = grp % 2
                if gi == 0:
                    P.dma("pool", f"upan{s_}a", lambda e, s_=s_, grp=grp: e.dma_start(out=upan[s_][0], in_=w_up_r[:, :, grp * 512:(grp + 1) * 512]),
                          writes=[b_upan[s_]])
                    P.dma("pool", f"upan{s_}b", lambda e, s_=s_, grp=grp: e.dma_start(
                        out=upan[s_][1], in_=w_up_r[:, :, D_FF + grp * 512:D_FF + (grp + 1) * 512]), writes=[b_upan[s_]])
                for half in range(2):
                    c = i if half == 0 else NFC + i
                    for n3 in range(3):
                        pb = half * 3 + n3
                        for k in range(KC):
                            P.op("pe", lambda e, k=k, pb=pb, half=half, n3=n3, s_=s_, gi=gi: e.matmul(
                                psum[pb][:, 0:NW], lhsT=upan[s_][half][:, k, gi * 128:(gi + 1) * 128],
                                rhs=h2T[:, k, n3 * NW:(n3 + 1) * NW], start=(k == 0), stop=(k == KC - 1)),
                                reads=[b_upan[s_]] + b_h2T, writes=[b_ps[pb]], inc=(k == KC - 1))
                    ac = acc[half]
                    bac = b_acc[half]
                    for n3 in range(3):
                        pb = half * 3 + n3
                        P.op("act", lambda e, pb=pb, n3=n3: e.copy(out=ua[:, n3 * NW:(n3 + 1) * NW], in_=psum[pb][:, 0:NW]),
                             reads=[b_ps[pb]], writes=[b_ua])
                    eng = "dve"
                    P.op("act", lambda e, ac=ac, c=c: e.activation(out=ac, in_=ua[:, 1:TL + 1], func=AF.Identity,
                                                                   scale=cwT[:, 1, c:c + 1], bias=cbT[:, c:c + 1]),
                         reads=[b_ua, b_c4], writes=[bac])
                    P.op(eng, lambda e, ac=ac, c=c: e.scalar_tensor_tensor(out=ac, in0=ua[:, 0:TL], scalar=cwT[:, 0, c:c + 1], in1=ac,
                                                                           op0=ALU.mult, op1=ALU.add),
                         reads=[b_ua, b_c4, bac], writes=[bac])
                    P.op(eng, lambda e, ac=ac, c=c: e.scalar_tensor_tensor(out=ac, in0=ua[:, 2:TL + 2], scalar=cwT[:, 2, c:c + 1], in1=ac,
                                                                           op0=ALU.mult, op1=ALU.add),
                         reads=[b_ua, b_c4, bac], writes=[bac])
                    acv = ac.rearrange("p (s t) -> p s t", t=256)
                    ul = ua[:, 0:TL].rearrange("p (s t) -> p s t", t=256)
                    ur = ua[:, 2:TL + 2].rearrange("p (s t) -> p s t", t=256)
                    P.op(eng, lambda e, acv=acv, ul=ul, c=c: e.scalar_tensor_tensor(
                        out=acv[:, :, 0:1], in0=ul[:, :, 0:1], scalar=nk0T[:, c:c + 1], in1=acv[:, :, 0:1], op0=ALU.mult, op1=ALU.add),
                        reads=[b_ua, b_c4, bac], writes=[bac])
                    P.op(eng, lambda e, acv=acv, ur=ur, c=c: e.scalar_tensor_tensor(
                        out=acv[:, :, 255:256], in0=ur[:, :, 255:256], scalar=nk2T[:, c:c + 1], in1=acv[:, :, 255:256],
                        op0=ALU.mult, op1=ALU.add), reads=[b_ua, b_c4, bac], writes=[bac])
                P.op("act", lambda e: e.activation(out=sil, in_=acc[0], func=AF.Silu), reads=[b_acc[0]], writes=[b_sil])
                P.op("dve", lambda e, i=i: e.tensor_tensor(out=gT(i), in0=sil, in1=acc[1], op=ALU.mult),
                     reads=[b_sil, b_acc[1]], writes=[b_gT[i]])
            P.barrier()

            ALL.off = U0
            x2 = ALL.alloc([8, D], F32)
            wd = [ALL.alloc([NFC, 256], BF16) for _ in range(2)]
            g2s = [ALL.alloc([256], F32) for _ in range(2)]
            fn_bc = wd[0].rearrange("p k c -> p (k c)").bitcast(F32)[:, 0:D]
            fjunk = wd[1].rearrange("p k c -> p (k c)")[:, 0:TL]
            b_wdd = P.bufs(2, "wdd")
            b_g2s = P.bufs(2, "g2s")
            b_x2 = [B(f"x2{tile}_{b}") for b in range(8)]
            P.dma("sp", "x2ld", lambda e, t0=t0: e.dma_start(out=x2, in_=x1_d[t0:t0 + TL, :].rearrange("(b p) c -> p b c", p=128)),
                  writes=b_x2)
            ctr = 0
            for ct in range(8):
                cols = slice(ct * 256, (ct + 1) * 256)
                w_ = ct % 2
                P.dma("pool", f"wdpan{w_}", lambda e, cols=cols, w_=w_: e.dma_start(out=wd[w_], in_=w_dn_r[:, :, cols]), writes=[b_wdd[w_]])
                P.dma("sp", f"g2s{w_}", lambda e, cols=cols, w_=w_: e.dma_start(out=g2s[w_], in_=g2_d[:, cols]), writes=[b_g2s[w_]])
                for tb in range(8):
                    pb = ctr % 4
                    tq = ctr % 2
                    ctr += 1
                    for kc in range(NFC):
                        P.op("pe", lambda e, kc=kc, pb=pb, tb=tb, w_=w_: e.matmul(
                            psum[pb][:, 0:256], lhsT=gT(kc)[:, tb * 128:(tb + 1) * 128], rhs=wd[w_][:, kc, :],
                            start=(kc == 0), stop=(kc == NFC - 1)), reads=[b_gT[kc], b_wdd[w_]], writes=[b_ps[pb]], inc=(kc == NFC - 1))
                    P.op("dve", lambda e, pb=pb, tq=tq, w_=w_: e.tensor_tensor(out=tmp4[tq], in0=psum[pb][:, 0:256], in1=g2s[w_],
                                                                              op=ALU.mult), reads=[b_ps[pb], b_g2s[w_]], writes=[b_tmp4[tq]])
                    P.op("dve", lambda e, tq=tq, tb=tb, cols=cols: e.tensor_tensor(out=x2[:, tb, cols], in0=x2[:, tb, cols], in1=tmp4[tq],
                                                                                  op=ALU.add), reads=[b_tmp4[tq], b_x2[tb]], writes=[b_x2[tb]])
            P.dma("sp", "c40", lambda e: e.dma_start(out=fn_bc, in_=fnorm_d.partition_broadcast(128)), writes=[b_wdd[0]])
            b_fj = b_wdd[1]
            jk2 = upan_alias = None
            for tb in range(8):
                f = tb % 2
                P.op("act", lambda e, tb=tb, f=f: e.activation(out=fjunk, in_=x2[:, tb, 0:TL], func=AF.Square, accum_out=fsm[f][:, 0:1]),
                     reads=[b_x2[tb]], writes=[b_fj, b_fsm[f]])
                P.op("act", lambda e, tb=tb, f=f: e.activation(out=fjunk, in_=x2[:, tb, TL:D], func=AF.Square, accum_out=fsm[f][:, 1:2]),
                     reads=[b_x2[tb]], writes=[b_fj, b_fsm[f]])
                P.op("dve", lambda e, f=f: e.tensor_tensor(out=fsm[f][:, 2:3], in0=fsm[f][:, 0:1], in1=fsm[f][:, 1:2], op=ALU.add),
                     reads=[b_fsm[f]], writes=[b_fsm[f]])
                P.op("act", lambda e, f=f: e.activation(out=fsm[f][:, 2:3], in_=fsm[f][:, 2:3], func=AF.Ln, scale=1.0 / D, bias=EPS),
                     reads=[b_fsm[f]], writes=[b_fsm[f]])
                P.op("act", lambda e, f=f: e.activation(out=fsm[f][:, 3:4], in_=fsm[f][:, 2:3], func=AF.Exp, scale=-0.5),
                     reads=[b_fsm[f]], writes=[b_fsm[f]])
                P.op("dve", lambda e, tb=tb, f=f: e.scalar_tensor_tensor(out=x2[:, tb, :], in0=x2[:, tb, :], scalar=fsm[f][:, 3:4], in1=fn_bc,
                                                                         op0=ALU.mult, op1=ALU.mult),
                     reads=[b_x2[tb], b_fsm[f], b_wdd[0]], writes=[b_x2[tb]])
                P.dma("sp", f"yout{f}", lambda e, tb=tb, t0=t0: e.dma_start(out=y_d[t0 + tb * 128:t0 + (tb + 1) * 128, :], in_=x2[:, tb, :]),
                      reads=[b_x2[tb]])
            P.barrier()
    if dbg:
        if "hT" in dbg_d:
            P.dma("sp", "dbg", lambda e: e.dma_start(out=dbg_d["hT"], in_=hT), reads=b_hT)
        if "yT" in dbg_d:
            P.dma("sp", "dbg", lambda e: e.dma_start(out=dbg_d["yT"], in_=yT), reads=[b for l in b_yT for b in l])
        if "modT" in dbg_d:
            P.dma("sp", "dbg", lambda e: e.dma_start(out=dbg_d["modT"], in_=modT[:]), reads=[b_modT])

    P.barrier()
    ok, stuck, val = simulate(P)
    print('SIM', ok, stuck if not ok else '', {k: len(v) for k, v in P.ops.items()}, 'nsem', len(P.sem_keys()))
    assert ok

    with ExitStack() as es2:
        sems = {k: es2.enter_context(nc.semaphore("s_" + k.replace(":", "_"))) for k in P.sem_keys()}
        with nc.Block() as block:
            @block.tensor
            def _(e):
                P.emit("pe", e, sems)

            @block.scalar
            def _(e):
                P.emit("act", e, sems)

            @block.vector
            def _(e):
                P.emit("dve", e, sems)

            @block.gpsimd
            def _(e):
                P.emit("pool", e, sems)

            @block.sync
            def _(e):
                P.emit("sp", e, sems)
    es.close()
    return nc


def rope_tables():
    quarter = 16
    t = np.arange(T)
    row = (t // 64).astype(np.float32)
    col = (t % 64).astype(np.float32)
    inv_freq = np.power(np.float32(10000.0), -np.arange(quarter, dtype=np.float32) / np.float32(quarter)).astype(np.float32)
    ang_r = row[:, None] * inv_freq
    ang_c = col[:, None] * inv_freq
    ang = np.concatenate([ang_r, ang_r, ang_c, ang_c], axis=-1).astype(np.float32)
    cos = np.cos(ang).astype(np.float32).T
    sin = np.sin(ang).astype(np.float32).T
    return (np.ascontiguousarray(np.concatenate([cos, cos], axis=0)),
            np.ascontiguousarray(np.concatenate([sin, sin], axis=0)))


def rot_matrix():
    r = np.zeros((128, 128), np.float32)
    for base in (0, 64):
        for m in range(64):
            blk = m // 16
            if blk == 0:
                r[base + m + 16, base + m] = -1.0
            elif blk == 1:
                r[base + m - 16, base + m] = 1.0
            elif blk == 2:
                r[base + m + 16, base + m] = -1.0
            else:
                r[base + m - 16, base + m] = 1.0
    return r


def make_in_maps(inp):
    maps = []
    cosT, sinT = rope_tables()
    shared = {
        "w_mod": np.ascontiguousarray(inp["w_mod"][0]),
        "b_mod": np.ascontiguousarray(inp["b_mod"][0]),
        "norm1": np.ascontiguousarray(inp["norm1"][0]),
        "norm2": np.ascontiguousarray(inp["norm2"][0]),
        "final_norm": np.ascontiguousarray(inp["final_norm"]),
        "ident": np.eye(128, dtype=np.float32),
        "w_in": np.ascontiguousarray(inp["w_in"][0]),
        "rrot": rot_matrix(),
        "lamv": np.ascontiguousarray(np.stack([inp["lam_q1"][0], inp["lam_k1"][0], inp["lam_q2"][0], inp["lam_k2"][0]])),
        "diff_norm": np.ascontiguousarray(inp["diff_norm"][0]),
        "b_gates": np.ascontiguousarray(inp["b_gates"][0]),
        "w_pa": np.ascontiguousarray(inp["w_pa"][0]),
        "w_pb": np.ascontiguousarray(inp["w_pb"][0]),
        "w_out": np.ascontiguousarray(inp["w_out"][0]),
        "w_up": np.ascontiguousarray(inp["w_up"][0]),
        "w_down": np.ascontiguousarray(inp["w_down"][0]),
        "conv_w": np.ascontiguousarray(inp["conv_w"][0]),
        "conv_b": np.ascontiguousarray(inp["conv_b"][0]),
        "umask": np.ascontiguousarray(np.triu(np.ones((128, 128), np.float32))),
        "lmask": np.ascontiguousarray(np.tril(np.ones((128, 128), np.float32))),
        "mlstm_norm": np.ascontiguousarray(inp["mlstm_norm"][0]),
    }
    for core in range(8):
        m = dict(shared)
        if core < 4:
            m["x"] = np.ascontiguousarray(inp["x_sample"][core])
            m["cvec"] = np.ascontiguousarray(inp["c"][core])
            m["ck"] = np.ascontiguousarray(inp["cache_k"][core, 0])
            m["cv"] = np.ascontiguousarray(inp["cache_v"][core, 0])
            m["cosT"] = cosT
            m["sinT"] = sinT
            m["abias"] = np.zeros((128, NSEG * 18), np.float32)
            m["c0"] = np.ascontiguousarray(inp["state_C"][core, 0])
            m["n0"] = np.ascontiguousarray(inp["state_n"][core, 0])
            m["m0"] = np.ascontiguousarray(inp["state_m"][core, 0])
            m["keep"] = np.ones((1,), np.float32)
        else:
            j = core - 4
            m["x"] = np.ascontiguousarray(inp["x_prompt"][8 * j:8 * j + 8].reshape(T, D))
            m["cvec"] = np.ascontiguousarray(inp["c_ctx"])
            m["ck"] = np.zeros((H, 256, 128), np.float32)
            m["cv"] = np.zeros((H, 256, 128), np.float32)
            m["cosT"] = np.ones((128, T), np.float32)
            m["sinT"] = np.zeros((128, T), np.float32)
            m["c0"] = np.zeros((2, H, 64, 128), np.float32)
            m["n0"] = np.zeros((2, H, 64), np.float32)
            m["m0"] = np.zeros((2, H), np.float32)
            m["keep"] = np.zeros((1,), np.float32)
            ab = np.full((NSEG, 18), NEG, np.float32)
            for s_ in range(NSEG):
                ab[s_, 2 * s_:2 * s_ + 2] = 0.0
            m["abias"] = np.ascontiguousarray(np.broadcast_to(ab.reshape(1, -1), (128, NSEG * 18)))
        maps.append(m)
    return maps


def kernel(**inputs):
    inp = {k: np.asarray(v, dtype=np.float32) for k, v in inputs.items()}
    nc = build_program()
    maps = make_in_maps(inp)
    res = run_bass_kernel_spmd(nc, maps, core_ids=list(range(8)))
    r = res.results
    y_sample = np.stack([np.asarray(r[c]["y"], np.float32) for c in range(4)], axis=0)
    y_prompt = np.concatenate([np.asarray(r[c]["y"], np.float32).reshape(8, SEG, D) for c in range(4, 8)], axis=0)
    nk = np.concatenate([np.asarray(r[c]["nk"], np.float32) for c in range(4, 8)], axis=0)[:, None]
    nv = np.concatenate([np.asarray(r[c]["nv"], np.float32) for c in range(4, 8)], axis=0)[:, None]
    nC = np.concatenate([np.asarray(r[c]["nC"], np.float32) for c in range(4, 8)], axis=0)[:, None]
    nn = np.concatenate([np.asarray(r[c]["nn"], np.float32) for c in range(4, 8)], axis=0)[:, None]
    nm = np.concatenate([np.asarray(r[c]["nm"], np.float32) for c in range(4, 8)], axis=0)[:, None]
    return (y_prompt, y_sample, nk, nv, nC, nn, nm)
```

```python
import math
import numpy as np
import concourse.bass as bass
import concourse.mybir as mybir
from concourse.bass_utils import run_bass_kernel_spmd

F32 = mybir.dt.float32
BF16 = mybir.dt.bfloat16
AF = mybir.ActivationFunctionType
ALU = mybir.AluOpType
AX = mybir.AxisListType

D = 2048
T = 2048
NB = 16
KC = 16
SEG = 256
NSEG = 8
H = 8
N_IN = 10272
D_FF = 5632
NFC = D_FF // 128
EPS = 1e-6
GATE_CAP = 15.0
LAMBDA_INIT = 0.8 - 0.6 * math.exp(0.0)
NEG = -30000.0

OFF_QA = 0
OFF_KA = 512
OFF_VA = 1024
OFF_OA = 2048
OFF_G = 3072
OFF_QB = 3104
OFF_KB = 4128
OFF_VB = 5152
OFF_GA = 6176
OFF_GB = 8224


class Buf:
    __slots__ = ("name", "w", "r", "excl")

    def __init__(self, name):
        self.name = name
        self.excl = False
        self.w = None
        self.r = []


class Prog:
    ENGS = ("pe", "act", "dve", "pool", "sp")

    def __init__(self, nc):
        self.nc = nc
        self.ops = {e: [] for e in self.ENGS}
        self.cnt = {e: 0 for e in self.ENGS}
        self.pending = {e: False for e in self.ENGS}
        self.waited = {e: {} for e in self.ENGS}
        self.sems = {}
        self.dma_cnt = {}
        self.nbuf = 0
        self.tag = ''
        import os
        self.skip = set(x for x in os.environ.get('SKIP', '').split(',') if x)

    def buf(self, name=None):
        self.nbuf += 1
        return Buf(name or f"b{self.nbuf}")

    def bufs(self, n, name="b"):
        return [self.buf(f"{name}{i}") for i in range(n)]

    def _wait(self, eng, key, val):
        if val <= 0:
            return
        if self.waited[eng].get(key, 0) >= val:
            return
        self.waited[eng][key] = val
        self.ops[eng].append(("wait", key, val))

    def _deps(self, eng, reads, writes, skip_self):
        need = {}

        def add(dep):
            if skip_self and dep[0] == eng:
                return
            if need.get(dep[0], 0) < dep[1]:
                need[dep[0]] = dep[1]
        for b in reads:
            if b.w is not None:
                add(b.w)
        for b in writes:
            if b.w is not None:
                add(b.w)
            for rr in b.r:
                add(rr)
        for key, val in need.items():
            self._wait(eng, key, val)

    def op(self, eng, fn, reads=(), writes=(), inc=True):
        if self.tag in self.skip:
            return
        skip_self = eng == "pe"
        xr = [b for b in reads if b.excl]
        if xr:
            reads = [b for b in reads if not b.excl]
            writes = list(writes) + [b for b in xr if b not in writes]
        self._deps(eng, reads, writes, skip_self)
        idx = self.cnt[eng] + 1
        for b in reads:
            b.r.append((eng, idx))
        for b in writes:
            b.w = (eng, idx)
            b.r = []
        if inc:
            self.cnt[eng] = idx
            self.pending[eng] = False
        else:
            self.pending[eng] = True
        self.ops[eng].append(("op", fn, inc))

    def dma(self, q, slot, fn, reads=(), writes=()):
        if self.tag in self.skip:
            return
        self._deps(q, reads, writes, False)
        key = "d:" + slot
        n = self.dma_cnt.get(key, 0) + 16
        self.dma_cnt[key] = n
        for b in reads:
            b.r.append((key, n))
        for b in writes:
            b.w = (key, n)
            b.r = []
        self.ops[q].append(("dma", fn, key))

    def barrier(self):
        for e in self.ENGS:
            assert not self.pending[e]
        for e in self.ENGS:
            for e2 in self.ENGS:
                if e2 != e:
                    self._wait(e, e2, self.cnt[e2])
            for key, n in self.dma_cnt.items():
                self._wait(e, key, n)

    def sem_keys(self):
        return list(self.ENGS) + list(self.dma_cnt.keys())

    def emit(self, eng, engine_obj, sems):
        for o in self.ops[eng]:
            if o[0] == "wait":
                engine_obj.wait_ge(sems[o[1]], o[2])
            elif o[0] == "op":
                ins = o[1](engine_obj)
                if o[2]:
                    ins.then_inc(sems[eng], 1)
            else:
                o[1](engine_obj).then_inc(sems[o[2]], 16)


def simulate(P):
    ptr = {e: 0 for e in P.ENGS}
    val = {}
    prog = True
    while prog:
        prog = False
        for e in P.ENGS:
            ops = P.ops[e]
            while ptr[e] < len(ops):
                o = ops[ptr[e]]
                if o[0] == "wait":
                    if val.get(o[1], 0) < o[2]:
                        break
                elif o[0] == "op":
                    if o[2]:
                        val[e] = val.get(e, 0) + 1
                else:
                    val[o[2]] = val.get(o[2], 0) + 16
                ptr[e] += 1
                prog = True
    stuck = {e: (ptr[e], len(P.ops[e]), P.ops[e][ptr[e]][:3] if ptr[e] < len(P.ops[e]) else None) for e in P.ENGS}
    ok = all(ptr[e] == len(P.ops[e]) for e in P.ENGS)
    return ok, stuck, val
class Arena:
    def __init__(self, tens, base_bytes, nbytes):
        self.t = tens
        self.base = base_bytes
        self.size = nbytes
        self.off = 0

    def reset(self):
        self.off = 0

    def alloc(self, shape, dt):
        n = 1
        for s_ in shape:
            n *= s_
        nb = n * (4 if dt == F32 else 2)
        nb_al = (nb + 31) // 32 * 32
        assert self.off + nb_al <= self.size, (self.off, nb_al, self.size)
        o = (self.base + self.off) // 4
        ap = self.t[:, o:o + nb_al // 4]
        self.off += nb_al
        if dt != F32:
            ap = ap.bitcast(dt)
        ap = ap[:, 0:n]
        if len(shape) == 2:
            ap = ap.rearrange("p (a b) -> p a b", b=shape[1])
        elif len(shape) == 3:
            ap = ap.rearrange("p (a b c) -> p a b c", b=shape[1], c=shape[2])
        return ap


def build_program(dbg=None, upto=99):
    from contextlib import ExitStack
    nc = bass.Bass("TRN2", target_bir_lowering=False)
    P = Prog(nc)

    def din(name, shape, dt=F32):
        return nc.dram_tensor(name, list(shape), dt, kind="ExternalInput").ap()

    def dout(name, shape, dt=F32):
        return nc.dram_tensor(name, list(shape), dt, kind="ExternalOutput").ap()

    x_d = din("x", [T, D])
    cvec_d = din("cvec", [D])
    w_mod_d = din("w_mod", [D, 6 * D])
    b_mod_d = din("b_mod", [6 * D])
    norm1_d = din("norm1", [D])
    norm2_d = din("norm2", [D])
    fnorm_d = din("final_norm", [D])
    ident_d = din("ident", [128, 128])
    w_in_d = din("w_in", [D, N_IN])
    ck_d = din("ck", [H, 256, 128])
    cv_d = din("cv", [H, 256, 128])
    cosT_d = din("cosT", [128, T])
    sinT_d = din("sinT", [128, T])
    rrot_d = din("rrot", [128, 128])
    abias_d = din("abias", [128, NSEG * 18])
    lamv_d = din("lamv", [4, 64])
    dnorm_d = din("diff_norm", [128])
    bgates_d = din("b_gates", [32])
    umask_d = din("umask", [128, 128])
    lmask_d = din("lmask", [128, 128])
    m0_d = din("m0", [2, H])
    keep_d = din("keep", [1])
    mlnorm_d = din("mlstm_norm", [1024])
    c0_d = din("c0", [2, H, 64, 128])
    w_pa_d = din("w_pa", [1024, D])
    w_pb_d = din("w_pb", [1024, D])
    w_out_d = din("w_out", [D, D])
    w_up_d = din("w_up", [D, 2 * D_FF])
    w_down_d = din("w_down", [D_FF, D])
    convw_d = din("conv_w", [3, 2 * D_FF])
    convb_d = din("conv_b", [2 * D_FF])
    n0_d = din("n0", [2, H, 64])

    y_d = dout("y", [T, D])
    nk_d = dout("nk", [NSEG, H, SEG, 128])
    nv_d = dout("nv", [NSEG, H, SEG, 128])
    nC_d = dout("nC", [NSEG, 2, H, 64, 128])
    nn_d = dout("nn", [NSEG, 2, H, 64])
    nm_d = dout("nm", [NSEG, 2, H])

    dbg_d = {}
    if dbg:
        for nm, (shape, dt) in dbg.items():
            dbg_d[nm] = dout("dbg_" + nm, shape, dt)

    es = ExitStack()

    def sb(name, shape, dt):
        return es.enter_context(nc.sbuf_tensor(name, list(shape), dt))

    ident_f = sb("ident_f", [128, 128], F32)
    ident_b = sb("ident_b", [128, 128], BF16)
    modT = sb("modT", [128, 96], F32)
    s1T = sb("s1T", [128, KC], F32)
    s2T = sb("s2T", [128, KC], F32)
    nrmT = sb("nrmT", [128, 2, KC], F32)
    small = sb("small", [128, 64], F32)
    sc_col = sb("sc_col", [128, KC], BF16)
    RA, RB, RW = 64 * 1024, 64 * 1024, 78 * 1024
    big = sb("big", [128, (RA + RB + RW) // 4], F32)
    A_ = Arena(big, 0, RA)
    B_ = Arena(big, RA, RB)
    W = Arena(big, RA + RB, RW)

    psum = [es.enter_context(nc.psum_tensor(f"ps{i}", [128, 512], F32)) for i in range(8)]
    psum_b = [p[:].bitcast(BF16) for p in psum]

    B = P.buf
    b_ident = B("ident")
    b_modT = B("modT")
    b_s = B("s12")
    b_nrm = B("nrm")
    b_cv = B("cv")
    b_sc = B("sc")
    b_bmT = B("bmT")
    b_ps = P.bufs(8, "ps")
    for b_ in b_ps:
        b_.excl = True
    b_small = B("small")

    w_in_r = w_in_d.rearrange("(k p) c -> p k c", p=128)

    P.dma("sp", "misc0", lambda e: e.dma_start(out=ident_f[:], in_=ident_d), writes=[b_ident])
    P.op("pool", lambda e: e.tensor_copy(out=ident_b[:], in_=ident_f[:]), reads=[b_ident], writes=[b_ident])

    W.reset()
    bmT = W.alloc([96], F32)
    cvT = W.alloc([KC], F32)

    def fm_load(slot, dst, src, wb):
        P.dma("sp", slot, lambda e: e.dma_start(out=dst, in_=src.rearrange("(k p) -> p k", p=128),
                                                allow_slow_non_contiguous=True), writes=[wb])
    fm_load("misc", cvT, cvec_d, b_cv)
    fm_load("misc2", bmT, b_mod_d, b_bmT)
    fm_load("misc3", nrmT[:, 0, :], norm1_d, b_nrm)
    fm_load("misc5", nrmT[:, 1, :], norm2_d, b_nrm)

    P.op("act", lambda e: e.activation(out=small[:, 0:KC], in_=cvT, func=AF.Sigmoid), reads=[b_cv], writes=[b_small])
    P.op("dve", lambda e: e.tensor_tensor(out=sc_col[:], in0=small[:, 0:KC], in1=cvT, op=ALU.mult),
         reads=[b_small, b_cv], writes=[b_sc])

    wmr = w_mod_d.rearrange("(k p) c -> p k c", p=128)
    wpan = [W.alloc([KC, 512], BF16) for _ in range(2)]
    b_wpan = P.bufs(2, "wpan")
    ps_mod = psum[0]
    def mod_panel(pi):
        if True:
            grp = (0, 1, 3, 4)[pi // 4]
            sub = pi % 4
            pn = grp * 4 + sub
            slot = pi % 2
            c0 = pn * 512
            P.dma("pool", f"wpan{slot}",
                  lambda e, slot=slot, c0=c0: e.dma_start(out=wpan[slot], in_=wmr[:, :, c0:c0 + 512]),
                  writes=[b_wpan[slot]])
            for jj in range(4):
                j = pn * 4 + jj
                for k in range(KC):
                    P.op("pe", lambda e, slot=slot, jj=jj, j=j, k=k: e.matmul(
                        ps_mod[:, j:j + 1], lhsT=wpan[slot][:, k, jj * 128:(jj + 1) * 128],
                        rhs=sc_col[:, k:k + 1], start=(k == 0), stop=(k == KC - 1)),
                        reads=[b_wpan[slot], b_sc], writes=[b_ps[0]], inc=(k == KC - 1))
    def mod_finish(c0_, c1_):
        P.op("dve", lambda e: e.tensor_tensor(out=modT[:, c0_:c1_], in0=ps_mod[:, c0_:c1_], in1=bmT[:, c0_:c1_], op=ALU.add),
             reads=[b_ps[0], b_bmT], writes=[b_modT])

    for pi in range(8):
        mod_panel(pi)
    mod_finish(0, 32)
    P.op("dve", lambda e: e.scalar_tensor_tensor(out=s1T[:], in0=modT[:, 16:32], scalar=1.0, in1=nrmT[:, 0, :],
                                                 op0=ALU.add, op1=ALU.mult), reads=[b_modT, b_nrm], writes=[b_s])

    def mod_bcast(grp, dst, b_dst, pan, b_pan, bmb, b_bmb, pbanks, sc_bc, b_scbc):
        P.op("dve", lambda e: e.tensor_copy(out=sc_bc, in_=sc_col[:].unsqueeze(2).to_broadcast([128, KC, 128])),
             reads=[b_sc], writes=[b_scbc])
        for sub in range(4):
            pn = grp * 4 + sub
            slot = sub % 2
            c0 = pn * 512
            pb = pbanks[sub % 2]
            P.dma("pool", f"wpan{slot}",
                  lambda e, slot=slot, c0=c0: e.dma_start(out=pan[slot], in_=wmr[:, :, c0:c0 + 512]),
                  writes=[b_pan[slot]])
            for k in range(KC):
                P.op("pe", lambda e, slot=slot, k=k, pb=pb: e.matmul(
                    psum[pb][:], lhsT=sc_bc[:, k, :], rhs=pan[slot][:, k, :],
                    start=(k == 0), stop=(k == KC - 1)),
                    reads=[b_pan[slot], b_scbc], writes=[b_ps[pb]], inc=(k == KC - 1))
            P.dma("sp", "bmbc", lambda e, c0=c0: e.dma_start(
                out=bmb, in_=b_mod_d[c0:c0 + 512].partition_broadcast(128)), writes=[b_bmb])
            P.op("dve", lambda e, pb=pb, sub=sub: e.tensor_tensor(
                out=dst[:, sub * 512:(sub + 1) * 512], in0=psum[pb][:], in1=bmb, op=ALU.add),
                reads=[b_ps[pb], b_bmb], writes=[b_dst])

    def norm_to_featmajor(src_fn, nblk, sT, shT_cols, dst, dst_col0, dst_bufs, xblk, b_xblk, junk, b_junk, hook=None):
        for b in range(nblk):
            if hook is not None:
                hook(b)
            slot = b % 2
            P.dma("sp", f"xblk{slot}", lambda e, slot=slot, b=b: e.dma_start(out=xblk[slot], in_=src_fn(b)),
                  writes=[b_xblk[slot]])
            ssq = small[:, 32 + slot:33 + slot]
            P.op("act", lambda e, slot=slot, ssq=ssq: e.activation(out=junk, in_=xblk[slot], func=AF.Square,
                                                                    accum_out=ssq),
                 reads=[b_xblk[slot]], writes=[b_junk, b_small])
            P.op("act", lambda e, ssq=ssq: e.activation(out=ssq, in_=ssq, func=AF.Ln, scale=1.0 / D, bias=EPS),
                 reads=[b_small], writes=[b_small])
            P.op("act", lambda e, ssq=ssq: e.activation(out=ssq, in_=ssq, func=AF.Exp, scale=-0.5),
                 reads=[b_small], writes=[b_small])
            P.op("dve", lambda e, slot=slot, ssq=ssq: e.tensor_scalar(out=xblk[slot], in0=xblk[slot], scalar1=ssq,
                                                                       scalar2=None, op0=ALU.mult),
                 reads=[b_small, b_xblk[slot]], writes=[b_xblk[slot]])
            for k4 in range(4):
                pb = 4 + (k4 % 4)
                for kk in range(4):
                    k = k4 * 4 + kk
                    P.op("pe", lambda e, slot=slot, k=k, kk=kk, pb=pb: e.transpose(
                        out=psum[pb][:, kk * 128:(kk + 1) * 128], in_=xblk[slot][:, k * 128:(k + 1) * 128],
                        identity=ident_f[:]), reads=[b_xblk[slot], b_ident], writes=[b_ps[pb]], inc=(kk == 3))
                for kk in range(4):
                    k = k4 * 4 + kk
                    c0 = dst_col0 + b * 128
                    if k4 % 2 == 0:
                        P.op("act", lambda e, k=k, kk=kk, pb=pb, c0=c0: e.activation(
                            out=dst[:, k, c0:c0 + 128], in_=psum[pb][:, kk * 128:(kk + 1) * 128],
                            func=AF.Identity, scale=sT[:, k:k + 1], bias=modT[:, shT_cols + k:shT_cols + k + 1]),
                            reads=[b_ps[pb], b_s, b_modT], writes=[dst_bufs[b]])
                    else:
                        P.op("dve", lambda e, k=k, kk=kk, pb=pb, c0=c0: e.tensor_scalar(
                            out=dst[:, k, c0:c0 + 128], in0=psum[pb][:, kk * 128:(kk + 1) * 128],
                            scalar1=sT[:, k:k + 1], scalar2=modT[:, shT_cols + k:shT_cols + k + 1],
                            op0=ALU.mult, op1=ALU.add),
                            reads=[b_ps[pb], b_s, b_modT], writes=[dst_bufs[b]])

    A_.reset()
    B_.reset()
    hT = A_.alloc([KC, T], BF16)
    b_hT = [B(f"hT{b}") for b in range(NB)]
    xblk = [B_.alloc([D], F32) for _ in range(2)]
    b_xblk = P.bufs(2, "xblk")
    junk = B_.alloc([D], BF16)
    b_junk = B("junk")

    def mod_hook(b):
        if b % 2 == 0 and b > 0:
            mod_panel(8 + b // 2 - 1)
    norm_to_featmajor(lambda b: x_d[b * 128:(b + 1) * 128, :], NB, s1T, 0, hT, 0, b_hT, xblk, b_xblk, junk, b_junk, hook=mod_hook)
    mod_panel(15)
    mod_finish(48, 80)
    P.op("dve", lambda e: e.scalar_tensor_tensor(out=s2T[:], in0=modT[:, 64:80], scalar=1.0, in1=nrmT[:, 1, :],
                                                 op0=ALU.add, op1=ALU.mult), reads=[b_modT, b_nrm], writes=[b_s])
    P.barrier()
    B_.reset()
    yT = B_.alloc([16, T], BF16)
    b_yT = [[B(f"yT{c}_{b}") for b in range(NB)] for c in range(16)]
    if upto >= 2:
        W.reset()
        Vpan = W.alloc([KC, 256], BF16)
        KQpan = [W.alloc([KC, 128], BF16) for _ in range(4)]
        Vext = W.alloc([18, 2, 130], BF16)
        KT = W.alloc([2304], BF16)
        cs = [W.alloc([2, 512], F32) for _ in range(2)]
        xf = [W.alloc([512], F32) for _ in range(2)]
        t1 = W.alloc([512], F32)
        t2 = W.alloc([512], F32)
        P12 = [W.alloc([512], BF16) for _ in range(4)]
        Qpad = [W.alloc([512], BF16) for _ in range(4)]
        kst = [W.alloc([4, 128], F32) for _ in range(2)]
        vst = [W.alloc([2, 128], F32) for _ in range(2)]
        osb = [W.alloc([128], F32) for _ in range(2)]
        ybt = [W.alloc([128], BF16) for _ in range(2)]
        sm = [W.alloc([8], F32) for _ in range(2)]
        ajunk = W.alloc([128], BF16)
        obuf = [W.alloc([2, 258], F32) for _ in range(2)]
        ckf = W.alloc([2, 128], F32)
        abias = W.alloc([NSEG * 18], F32)
        lamv = W.alloc([4, 64], F32)
        lamt = W.alloc([2, 64], F32)
        lams = W.alloc([4], F32)
        dn8 = W.alloc([128], F32)
        rrot = W.alloc([128], F32)

        b_Vpan = B("Vpan")
        b_KQpan = P.bufs(4, "KQpan")
        b_Vext = [B(f"Vext{b}") for b in range(18)]
        b_Vone = B("Vone")
        b_KT = [B(f"KT{i}") for i in range(5)]
        b_cs = P.bufs(2, "cs")
        b_xf = P.bufs(2, "xf")
        b_t1 = B("t1")
        b_t2 = B("t2")
        b_P12 = P.bufs(4, "P12")
        b_Qpad = P.bufs(4, "Qpad")
        b_kst = P.bufs(2, "kst")
        b_vst = P.bufs(2, "vst")
        b_osb = P.bufs(2, "osb")
        b_ybt = P.bufs(2, "ybt")
        b_sm = P.bufs(2, "sm")
        b_ajunk = B("ajunk")
        b_obuf = P.bufs(2, "obuf")
        b_ckf = B("ckf")
        b_const = B("aconst")

        P.tag = 'aconst'
        P.dma("sp", "ac0", lambda e: e.dma_start(out=abias, in_=abias_d), writes=[b_const])
        P.dma("sp", "ac1", lambda e: e.dma_start(out=lamv, in_=lamv_d.partition_broadcast(128)), writes=[b_const])
        P.dma("sp", "ac2", lambda e: e.dma_start(out=dn8, in_=dnorm_d.partition_broadcast(128)), writes=[b_const])
        P.dma("sp", "ac3", lambda e: e.dma_start(out=rrot, in_=rrot_d), writes=[b_const])
        P.op("dve", lambda e: e.tensor_scalar(out=dn8, in0=dn8, scalar1=1.0 - LAMBDA_INIT, scalar2=None, op0=ALU.mult),
             reads=[b_const], writes=[b_const])
        P.tag = 'lam'
        P.op("dve", lambda e: e.tensor_tensor(out=lamt[:, 0, :], in0=lamv[:, 0, :], in1=lamv[:, 1, :], op=ALU.mult),
             reads=[b_const], writes=[b_const])
        P.op("dve", lambda e: e.tensor_tensor(out=lamt[:, 1, :], in0=lamv[:, 2, :], in1=lamv[:, 3, :], op=ALU.mult),
             reads=[b_const], writes=[b_const])
        P.op("dve", lambda e: e.tensor_reduce(out=lams[:, 0:2], in_=lamt, axis=AX.X, op=ALU.add),
             reads=[b_const], writes=[b_const])
        P.op("act", lambda e: e.activation(out=lams[:, 0:2], in_=lams[:, 0:2], func=AF.Exp),
             reads=[b_const], writes=[b_const])
        P.op("dve", lambda e: e.tensor_tensor(out=lams[:, 2:3], in0=lams[:, 1:2], in1=lams[:, 0:1], op=ALU.subtract),
             reads=[b_const], writes=[b_const])
        P.op("dve", lambda e: e.tensor_scalar(out=lams[:, 3:4], in0=lams[:, 2:3], scalar1=-LAMBDA_INIT, scalar2=None,
                                              op0=ALU.add), reads=[b_const], writes=[b_const])
        neg_lam = lams[:, 3:4]
        P.tag = 'amemset'
        P.op("pool", lambda e: e.memset(Vext[:, :, :, 128:129], 1.0), writes=[b_Vone])
        for q in range(4):
            P.op("pool", lambda e, q=q: e.memset(Qpad[q], 0.0), writes=[b_Qpad[q]])

        rope_ctr = [0]

        def rope(ps_idx, tt, writer):
            i = rope_ctr[0]
            rope_ctr[0] += 1
            s_ = i % 2
            P.dma("pool", f"cs{s_}", lambda e: e.dma_start(out=cs[s_][:, 0, :], in_=cosT_d[:, tt * 512:(tt + 1) * 512]),
                  writes=[b_cs[s_]])
            P.dma("pool", f"cs{s_}", lambda e: e.dma_start(out=cs[s_][:, 1, :], in_=sinT_d[:, tt * 512:(tt + 1) * 512]),
                  writes=[b_cs[s_]])
            P.op("dve", lambda e: e.tensor_copy(out=xf[s_], in_=psum[ps_idx][:]), reads=[b_ps[ps_idx]], writes=[b_xf[s_]])
            P.op("pe", lambda e: e.matmul(psum[7][:], lhsT=rrot, rhs=xf[s_], start=True, stop=True),
                 reads=[b_xf[s_], b_const], writes=[b_ps[7]])
            P.op("dve", lambda e: e.tensor_tensor(out=t1, in0=xf[s_], in1=cs[s_][:, 0, :], op=ALU.mult),
                 reads=[b_xf[s_], b_cs[s_]], writes=[b_t1])
            P.op("dve", lambda e: e.tensor_tensor(out=t2, in0=psum[7][:], in1=cs[s_][:, 1, :], op=ALU.mult),
                 reads=[b_ps[7], b_cs[s_]], writes=[b_t2])
            writer(s_)

        import os
        ATT_HG = int(os.environ.get('ATT_HG', '4'))
        ATT_STOP = os.environ.get('ATT_STOP', 'full')
        for hg in range(ATT_HG):
            c0 = OFF_VB + hg * 256
            P.tag = 'vproj'
            P.dma("pool", "Vpan", lambda e, c0=c0: e.dma_start(out=Vpan, in_=w_in_r[:, :, c0:c0 + 256]), writes=[b_Vpan])
            for b in range(NB):
                pb = 4 + b % 2
                for k in range(KC):
                    P.op("pe", lambda e, b=b, k=k, pb=pb: e.matmul(
                        psum[pb][:, 0:256], lhsT=hT[:, k, b * 128:(b + 1) * 128], rhs=Vpan[:, k, :],
                        start=(k == 0), stop=(k == KC - 1)), reads=[b_hT[b], b_Vpan], writes=[b_ps[pb]], inc=(k == KC - 1))
                s_ = b % 2
                P.tag = 'vevac'
                P.op("dve", lambda e, b=b, pb=pb: e.tensor_copy(
                    out=Vext[:, b, :, 0:128], in_=psum[pb][:, 0:256].rearrange("p (h d) -> p h d", d=128)),
                    reads=[b_ps[pb]], writes=[b_Vext[b]])
                P.tag = 'vst'
                P.op("dve", lambda e, s_=s_, pb=pb: e.tensor_copy(
                    out=vst[s_], in_=psum[pb][:, 0:256].rearrange("p (h d) -> p h d", d=128)),
                    reads=[b_ps[pb]], writes=[b_vst[s_]])
                seg, pos0 = b // 2, (b % 2) * 128
                P.tag = 'nvdma'
                P.dma("sp", f"vst{s_}", lambda e, s_=s_, seg=seg, pos0=pos0, hg=hg: e.dma_start(
                    out=nv_d[seg, 2 * hg:2 * hg + 2, pos0:pos0 + 128, :].rearrange("h p d -> p h d"), in_=vst[s_]),
                    reads=[b_vst[s_]])
                P.tag = 'vproj'
            P.tag = 'cvdma'
            for hl in range(2):
                for bb in range(2):
                    P.dma("pool", f"cv{hl}{bb}", lambda e, hl=hl, bb=bb, hg=hg: e.dma_start(
                        out=Vext[:, 16 + bb, hl, 0:128], in_=cv_d[2 * hg + hl, bb * 128:(bb + 1) * 128, :]),
                        writes=[b_Vext[16 + bb]])
            P.tag = 'att'
            for hl in range(2):
                if ATT_STOP == 'v':
                    break
                h = hg * 2 + hl
                Kp, Qp = KQpan[2 * hl], KQpan[2 * hl + 1]
                bKp, bQp = b_KQpan[2 * hl], b_KQpan[2 * hl + 1]
                ck0 = OFF_KB + h * 128
                cq0 = OFF_QB + h * 128
                P.dma("pool", f"KQ{2 * hl}", lambda e, Kp=Kp, ck0=ck0: e.dma_start(out=Kp, in_=w_in_r[:, :, ck0:ck0 + 128]),
                      writes=[bKp])
                P.dma("pool", f"KQ{2 * hl + 1}", lambda e, Qp=Qp, cq0=cq0: e.dma_start(out=Qp, in_=w_in_r[:, :, cq0:cq0 + 128]),
                      writes=[bQp])
                for tt in range(4):
                    pb = 4 + tt % 2
                    for k in range(KC):
                        P.op("pe", lambda e, tt=tt, k=k, pb=pb, Kp=Kp: e.matmul(
                            psum[pb][:], lhsT=Kp[:, k, :], rhs=hT[:, k, tt * 512:(tt + 1) * 512],
                            start=(k == 0), stop=(k == KC - 1)),
                            reads=[bKp] + b_hT[4 * tt:4 * tt + 4], writes=[b_ps[pb]], inc=(k == KC - 1))

                    def kwriter(s_, tt=tt, h=h):
                        P.op("dve", lambda e: e.tensor_tensor(out=KT[:, tt * 512:(tt + 1) * 512], in0=t1, in1=t2, op=ALU.add),
                             reads=[b_t1, b_t2], writes=[b_KT[tt]])
                        for j in range(4):
                            P.op("pe", lambda e, j=j: e.transpose(out=psum[7][:, j * 128:(j + 1) * 128],
                                                                  in_=xf[s_][:, j * 128:(j + 1) * 128], identity=ident_f[:]),
                                 reads=[b_xf[s_], b_ident], writes=[b_ps[7]], inc=(j == 3))
                        ks = tt % 2
                        P.op("dve", lambda e: e.tensor_copy(out=kst[ks], in_=psum[7][:].rearrange("p (j d) -> p j d", d=128)),
                             reads=[b_ps[7]], writes=[b_kst[ks]])
                        for sg in range(2):
                            seg = 2 * tt + sg
                            P.dma("sp", f"kst{ks}", lambda e, sg=sg, seg=seg: e.dma_start(
                                out=nk_d[seg, h, :, :].rearrange("(b p) d -> p b d", p=128),
                                in_=kst[ks][:, 2 * sg:2 * sg + 2, :]), reads=[b_kst[ks]])
                    rope(pb, tt, kwriter)
                P.dma("pool", "ckf", lambda e, h=h: e.dma_start(out=ckf, in_=ck_d[h].rearrange("(b p) d -> p b d", p=128)),
                      writes=[b_ckf])
                for bb in range(2):
                    P.op("pe", lambda e, bb=bb: e.transpose(out=psum[7][:, bb * 128:(bb + 1) * 128], in_=ckf[:, bb, :],
                                                            identity=ident_f[:]),
                         reads=[b_ckf, b_ident], writes=[b_ps[7]], inc=(bb == 1))
                P.op("dve", lambda e: e.tensor_copy(out=KT[:, 2048:2304], in_=psum[7][:, 0:256]),
                     reads=[b_ps[7]], writes=[b_KT[4]])

                if ATT_STOP == 'k':
                    continue
                def qproj(tt, h=h, Qp=Qp, bQp=bQp):
                    pb = 4
                    for k in range(KC):
                        P.op("pe", lambda e, k=k: e.matmul(
                            psum[pb][:], lhsT=Qp[:, k, :], rhs=hT[:, k, tt * 512:(tt + 1) * 512],
                            start=(k == 0), stop=(k == KC - 1)),
                            reads=[bQp] + b_hT[4 * tt:4 * tt + 4], writes=[b_ps[pb]], inc=(k == KC - 1))

                    def qwriter(s_):
                        for sg in range(2):
                            q = (tt % 2) * 2 + sg
                            P.op("dve", lambda e, q=q, sg=sg: e.tensor_tensor(
                                out=Qpad[q][0:64, 0:256], in0=t1[0:64, sg * 256:(sg + 1) * 256],
                                in1=t2[0:64, sg * 256:(sg + 1) * 256], op=ALU.add),
                                reads=[b_t1, b_t2], writes=[b_Qpad[q]])
                            P.op("dve", lambda e, q=q, sg=sg: e.tensor_tensor(
                                out=Qpad[q][64:128, 256:512], in0=t1[64:128, sg * 256:(sg + 1) * 256],
                                in1=t2[64:128, sg * 256:(sg + 1) * 256], op=ALU.add),
                                reads=[b_t1, b_t2], writes=[b_Qpad[q]])
                    rope(pb, tt, qwriter)

                SB = (0, 1, 5, 6)

                def smm(seg, kb, q):
                    sbk = SB[kb % 4]
                    P.op("pe", lambda e: e.matmul(psum[sbk][:], lhsT=KT[:, kb * 128:(kb + 1) * 128], rhs=Qpad[q],
                                                  start=True, stop=True),
                         reads=[b_KT[min(kb // 4, 4)], b_Qpad[q]], writes=[b_ps[sbk]])

                def attend(seg, mid_hook=None, h=h, hl=hl):
                    tt, sg = seg // 2, seg % 2
                    q = (tt % 2) * 2 + sg
                    smm(seg, 0, q)
                    smm(seg, 1, q)
                    smm(seg, 2, q)
                    for kb in range(18):
                        sbk = SB[kb % 4]
                        pj = kb % 4
                        P.op("act", lambda e, kb=kb, sbk=sbk, pj=pj: e.activation(
                            out=P12[pj], in_=psum[sbk][:], func=AF.Exp, scale=0.125,
                            bias=abias[:, seg * 18 + kb:seg * 18 + kb + 1]),
                            reads=[b_ps[sbk], b_const], writes=[b_P12[pj]])
                        for i in range(2):
                            for qb in range(2):
                                P.op("pe", lambda e, kb=kb, pj=pj, i=i, qb=qb: e.matmul(
                                    psum[2 + i][:, qb * 129:(qb + 1) * 129],
                                    lhsT=P12[pj][:, i * 256 + qb * 128:i * 256 + (qb + 1) * 128],
                                    rhs=Vext[:, kb, hl, 0:129], start=(kb == 0 and qb == 0), stop=(kb == 17),
                                    skip_group_check=True),
                                    reads=[b_P12[pj], b_Vext[kb], b_Vone], writes=[b_ps[2 + i]],
                                    inc=(i == 1 and qb == 1))
                        if kb + 3 < 18:
                            smm(seg, kb + 3, q)
                        if kb == 5 and mid_hook is not None:
                            mid_hook()
                    par = seg % 2
                    P.op("dve", lambda e, par=par: e.tensor_copy(out=obuf[par][:, 0, :], in_=psum[2][:, 0:258]),
                         reads=[b_ps[2]], writes=[b_obuf[par]])
                    P.op("dve", lambda e, par=par: e.tensor_copy(out=obuf[par][:, 1, :], in_=psum[3][:, 0:258]),
                         reads=[b_ps[3]], writes=[b_obuf[par]])

                def finalize(seg, h=h):
                    par = seg % 2
                    for qb in range(2):
                        f = qb % 2
                        O1 = obuf[par][:, 0, qb * 129:(qb + 1) * 129]
                        O2 = obuf[par][:, 1, qb * 129:(qb + 1) * 129]
                        smf = sm[f]
                        P.op("dve", lambda e, O1=O1, smf=smf: e.reciprocal(out=smf[:, 0:1], in_=O1[:, 128:129]),
                             reads=[b_obuf[par]], writes=[b_sm[f]])
                        P.op("dve", lambda e, O2=O2, smf=smf: e.reciprocal(out=smf[:, 1:2], in_=O2[:, 128:129]),
                             reads=[b_obuf[par]], writes=[b_sm[f]])
                        P.op("dve", lambda e, smf=smf: e.tensor_tensor(out=smf[:, 2:3], in0=smf[:, 1:2], in1=neg_lam, op=ALU.mult),
                             reads=[b_sm[f], b_const], writes=[b_sm[f]])
                        P.op("dve", lambda e, O1=O1, smf=smf, f=f: e.tensor_scalar(out=osb[f], in0=O1[:, 0:128], scalar1=smf[:, 0:1],
                                                                                    scalar2=None, op0=ALU.mult),
                             reads=[b_obuf[par], b_sm[f]], writes=[b_osb[f]])
                        P.op("dve", lambda e, O2=O2, smf=smf, f=f: e.scalar_tensor_tensor(
                            out=osb[f], in0=O2[:, 0:128], scalar=smf[:, 2:3], in1=osb[f], op0=ALU.mult, op1=ALU.add),
                            reads=[b_obuf[par], b_sm[f], b_osb[f]], writes=[b_osb[f]])
                        P.op("act", lambda e, smf=smf, f=f: e.activation(out=ajunk, in_=osb[f], func=AF.Square,
                                                                         accum_out=smf[:, 3:4]),
                             reads=[b_osb[f]], writes=[b_ajunk, b_sm[f]])
                        P.op("act", lambda e, smf=smf: e.activation(out=smf[:, 4:5], in_=smf[:, 3:4], func=AF.Ln,
                                                                    scale=1.0 / 128, bias=EPS),
                             reads=[b_sm[f]], writes=[b_sm[f]])
                        P.op("act", lambda e, smf=smf: e.activation(out=smf[:, 5:6], in_=smf[:, 4:5], func=AF.Exp, scale=-0.5),
                             reads=[b_sm[f]], writes=[b_sm[f]])
                        P.op("dve", lambda e, smf=smf, f=f: e.scalar_tensor_tensor(
                            out=ybt[f], in0=osb[f], scalar=smf[:, 5:6], in1=dn8, op0=ALU.mult, op1=ALU.mult),
                            reads=[b_osb[f], b_sm[f], b_const], writes=[b_ybt[f]])
                        P.op("pe", lambda e, f=f: e.transpose(out=psum_b[7][:, f * 128:(f + 1) * 128], in_=ybt[f],
                                                              identity=ident_b[:]),
                             reads=[b_ybt[f], b_ident], writes=[b_ps[7]])
                        blk = seg * 2 + qb
                        P.op("dve", lambda e, f=f, blk=blk: e.tensor_copy(
                            out=yT[:, 8 + h, blk * 128:(blk + 1) * 128], in_=psum_b[7][:, f * 128:(f + 1) * 128]),
                            reads=[b_ps[7]], writes=[b_yT[8 + h][blk]])

                qproj(0)
                for tt in range(4):
                    if tt + 1 < 4:
                        qproj(tt + 1)
                    if ATT_STOP == 'q':
                        continue
                    for seg in (2 * tt, 2 * tt + 1):
                        attend(seg, (lambda seg=seg: finalize(seg - 1)) if seg > 0 else None)
                finalize(7)
        P.barrier()
    if upto >= 3:
        P.tag = 'mlstm'
        W.reset()
        Gpan = W.alloc([KC, 32], BF16)
        Gt = W.alloc([NB, 32], F32)
        LF = W.alloc([NB, 2, 8], F32)
        IV = W.alloc([NB, 2, 8], F32)
        CS = Gt
        Et = W.alloc([NB, 16], F32)
        Ut = W.alloc([NB, 16], F32)
        EBt = W.alloc([NB, 16], F32)
        At = W.alloc([NB, 16], F32)
        bg_bc = W.alloc([32], F32)
        Umask = W.alloc([128], F32)
        Lmask = W.alloc([128], F32)
        ones_f = W.alloc([128], F32)
        SCL = W.alloc([NSEG, 16], F32)
        EM0 = W.alloc([16], F32)
        keepc = W.alloc([1], F32)
        mln_bc = W.alloc([256], F32)
        amaxc = W.alloc([2], F32)
        mrow = W.alloc([16 * 16 + 16 * 16 + 16 + 16 + NSEG * 16], F32)
        VOpan = W.alloc([KC, 256], BF16)
        QaT = W.alloc([T], BF16)
        KaTpad = [W.alloc([T], BF16) for _ in range(2)]
        Katok = W.alloc([NB, 128], BF16)
        Vaext = W.alloc([NB, 2, 130], BF16)
        Hbuf = W.alloc([NB, 2, 128], BF16)
        _hb = Hbuf.rearrange('p b h d -> p (b h d)')
        sgoT = W.alloc([2, T], BF16)
        QKpan = [_hb[:, i * 2048:(i + 1) * 2048].rearrange('p (k c) -> p k c', c=128) for i in range(2)]
        Cst2 = [W.alloc([130], F32) for _ in range(2)]
        EBp = W.alloc([NB, 2], F32)
        SCLp = W.alloc([NSEG, 2], F32)
        smd = [[W.alloc([4], F32) for _ in range(2)] for _ in range(2)]
        Cb = [[W.alloc([130], BF16) for _ in range(2)] for _ in range(2)]
        Pm8 = [[W.alloc([128], BF16) for _ in range(4)] for _ in range(2)]
        Kupad = [[W.alloc([128], BF16) for _ in range(2)] for _ in range(2)]
        stg = [W.alloc([130], F32) for _ in range(2)]
        sgtmp = W.alloc([512], F32)
        yat4 = [W.alloc([128], BF16) for _ in range(4)]
        mtmp4 = [W.alloc([128], F32) for _ in range(4)]
        msm = [W.alloc([8], F32) for _ in range(4)]
        msm8 = [[W.alloc([2], F32) for _ in range(4)] for _ in range(2)]
        mjunk = W.alloc([128], BF16)

        b_g = B("gates")
        b_mc = B("mconst")
        b_mrow = B("mrow")
        b_scl = B("scl")
        b_QKpan = P.bufs(2, "QKpan")
        b_VOpan = B("VOpan")
        b_QaT = B("QaT")
        b_KaT = P.bufs(2, "KaT")
        b_Katok = B("Katok")
        b_Va = B("Va")
        b_Hb = [[B(f"Hb{b}_{hl}") for hl in range(2)] for b in range(NB)]
        b_Cst2 = [B(f"Cst2{d_}") for d_ in range(2)]
        b_EBp = B("EBp")
        b_smd = [[B(f"smd{a_}{d_}") for d_ in range(2)] for a_ in range(2)]
        b_sgoT = B("sgoT")
        b_sgtmp = B("sgtmp")
        b_Cb = [[B(f"Cb{d_}{hl}") for hl in range(2)] for d_ in range(2)]
        b_Pm8 = [P.bufs(4, "PmA"), P.bufs(4, "PmB")]
        b_msm8 = [P.bufs(4, "msmA"), P.bufs(4, "msmB")]
        b_Ku = [[B(f"Ku{d_}{hl}") for hl in range(2)] for d_ in range(2)]
        b_stg = P.bufs(2, "stg")
        b_yat4 = P.bufs(4, "yat")
        b_mtmp4 = P.bufs(4, "mtmp")
        b_msm = P.bufs(4, "msm")
        b_mjunk = B("mjunk")

        P.dma("sp", "mc0", lambda e: e.dma_start(out=bg_bc, in_=bgates_d.partition_broadcast(128)), writes=[b_mc])
        P.dma("sp", "mc1", lambda e: e.dma_start(out=Umask, in_=umask_d), writes=[b_mc])
        P.dma("sp", "mc2", lambda e: e.dma_start(out=Lmask, in_=lmask_d), writes=[b_mc])
        P.dma("sp", "mc3", lambda e: e.dma_start(out=EM0, in_=m0_d.rearrange("a b -> (a b)").partition_broadcast(128)),
              writes=[b_mc])
        P.dma("sp", "mc4", lambda e: e.dma_start(out=keepc, in_=keep_d.partition_broadcast(128)), writes=[b_mc])
        P.op("pool", lambda e: e.memset(ones_f, 1.0), writes=[b_mc])
        P.op("pool", lambda e: e.memset(Vaext[:, :, :, 128:129], 1.0), writes=[b_Va])
        for hl in range(2):
            P.op("pool", lambda e, hl=hl: e.memset(KaTpad[hl], 0.0), writes=[b_KaT[hl]])
            for d_ in range(2):
                P.op("pool", lambda e, hl=hl, d_=d_: e.memset(Kupad[d_][hl], 0.0), writes=[b_Ku[d_][hl]])
        MR_A, MR_T, MR_M, MR_X, MR_O = 0, 256, 512, 528, 544
        P.op("pool", lambda e: e.tensor_copy(out=mrow[0:1, MR_M:MR_M + 16], in_=EM0[0:1, :]), reads=[b_mc], writes=[b_mrow])
        P.op("act", lambda e: e.activation(out=EM0, in_=EM0, func=AF.Exp), reads=[b_mc, b_mrow], writes=[b_mc])

        P.dma("pool", "Gpan", lambda e: e.dma_start(out=Gpan, in_=w_in_r[:, :, OFF_G:OFF_G + 32]), writes=[b_g])
        for b in range(NB):
            for k in range(KC):
                P.op("pe", lambda e, b=b, k=k: e.matmul(psum[4][:, b * 32:(b + 1) * 32], lhsT=hT[:, k, b * 128:(b + 1) * 128],
                                                        rhs=Gpan[:, k, :], start=(k == 0), stop=(k == KC - 1)),
                     reads=[b_hT[b], b_g], writes=[b_ps[4]], inc=(k == KC - 1))
        P.op("dve", lambda e: e.tensor_tensor(out=Gt, in0=psum[4][:].rearrange("p (b c) -> p b c", c=32),
                                              in1=bg_bc.unsqueeze(1).to_broadcast([128, NB, 32]), op=ALU.add),
             reads=[b_ps[4], b_mc], writes=[b_g])
        P.op("act", lambda e: e.activation(out=Gt, in_=Gt, func=AF.Tanh, scale=1.0 / GATE_CAP), reads=[b_g], writes=[b_g])
        G4 = Gt.rearrange("p b (t h) -> p b t h", h=8)
        P.op("dve", lambda e: e.tensor_scalar(out=IV, in0=G4[:, :, 0::2, :], scalar1=GATE_CAP, scalar2=None, op0=ALU.mult),
             reads=[b_g], writes=[b_g])
        P.op("act", lambda e: e.activation(out=LF, in_=G4[:, :, 1::2, :], func=AF.Exp, scale=-GATE_CAP), reads=[b_g], writes=[b_g])
        P.op("act", lambda e: e.activation(out=LF, in_=LF, func=AF.Ln, bias=1.0), reads=[b_g], writes=[b_g])
        P.op("dve", lambda e: e.tensor_scalar(out=LF, in0=LF, scalar1=-1.0, scalar2=None, op0=ALU.mult), reads=[b_g], writes=[b_g])
        for b in range(NB):
            P.op("pe", lambda e, b=b: e.matmul(psum[5][:, b * 32:b * 32 + 8], lhsT=Umask, rhs=LF[:, b, 0, :], start=True, stop=True),
                 reads=[b_g, b_mc], writes=[b_ps[5]], inc=False)
            P.op("pe", lambda e, b=b: e.matmul(psum[5][:, b * 32 + 8:b * 32 + 16], lhsT=Lmask, rhs=LF[:, b, 1, :], start=True, stop=True),
                 reads=[b_g, b_mc], writes=[b_ps[5]], inc=False)
            P.op("pe", lambda e, b=b: e.matmul(psum[5][:, b * 32 + 16:b * 32 + 32], lhsT=ones_f,
                                               rhs=LF[:, b, :, :].rearrange("p t h -> p (t h)"), start=True, stop=True),
                 reads=[b_g, b_mc], writes=[b_ps[5]], inc=True)
        P.op("dve", lambda e: e.tensor_copy(out=CS, in_=psum[5][:].rearrange("p (b c) -> p b c", c=32)),
             reads=[b_ps[5]], writes=[b_g])
        IV16 = IV.rearrange("p b t h -> p b (t h)")
        P.op("act", lambda e: e.activation(out=Et, in_=CS[:, :, 0:16], func=AF.Exp), reads=[b_g], writes=[b_g])
        P.op("act", lambda e: e.activation(out=EBt, in_=CS[:, :, 16:32], func=AF.Exp), reads=[b_g], writes=[b_g])
        P.op("dve", lambda e: e.tensor_tensor(out=At, in0=IV16, in1=CS[:, :, 0:16], op=ALU.subtract), reads=[b_g], writes=[b_g])
        P.op("act", lambda e: e.activation(out=Ut, in_=At, func=AF.Exp), reads=[b_g], writes=[b_g])

        Aflat = At.rearrange("p b c -> p (b c)")
        for half in range(2):
            P.op("pe", lambda e, half=half: e.transpose(out=psum[6][:, 0:128], in_=Aflat[:, half * 128:(half + 1) * 128],
                                                        identity=ident_f[:]), reads=[b_g, b_ident], writes=[b_ps[6]])
            P.op("dve", lambda e, half=half: e.tensor_reduce(out=amaxc[:, half:half + 1], in_=psum[6][:, 0:128], axis=AX.X, op=ALU.max),
                 reads=[b_ps[6]], writes=[b_mc])
        for half in range(2):
            P.op("pe", lambda e, half=half: e.transpose(out=psum[6][0:1, half * 128:(half + 1) * 128], in_=amaxc[:, half:half + 1],
                                                        identity=ident_f[:]), reads=[b_mc, b_ident], writes=[b_ps[6]])
        P.op("dve", lambda e: e.tensor_copy(out=mrow[0:1, MR_A:MR_A + 256], in_=psum[6][0:1, 0:256]), reads=[b_ps[6]], writes=[b_mrow])
        P.op("dve", lambda e: e.tensor_copy(out=mrow[0:1, MR_T:MR_T + 256].rearrange("p (b c) -> p b c", c=16),
                                            in_=CS[0:1, :, 16:32]), reads=[b_g], writes=[b_mrow])

        def mr(off, n=8):
            return mrow[0:1, off:off + n]
        for d_ in range(2):
            order = list(range(NB)) if d_ == 0 else list(range(NB - 1, -1, -1))
            mcur = mr(MR_M + d_ * 8)
            for idx, blk in enumerate(order):
                seg = blk // 2
                first_of_seg = (blk % 2 == 0) if d_ == 0 else (blk % 2 == 1)
                if first_of_seg and idx > 0:
                    P.op("dve", lambda e, mcur=mcur: e.tensor_scalar(out=mcur, in0=mcur, scalar1=keepc[0:1, 0:1], scalar2=None,
                                                                      op0=ALU.mult), reads=[b_mrow, b_mc], writes=[b_mrow])
                am = mr(MR_A + blk * 16 + d_ * 8)
                tt_ = mr(MR_T + blk * 16 + d_ * 8)
                P.op("dve", lambda e, mcur=mcur, am=am: e.tensor_tensor(out=mcur, in0=mcur, in1=am, op=ALU.max),
                     reads=[b_mrow], writes=[b_mrow])
                P.op("dve", lambda e, mcur=mcur, tt_=tt_: e.tensor_tensor(out=mcur, in0=mcur, in1=tt_, op=ALU.add),
                     reads=[b_mrow], writes=[b_mrow])
                if not first_of_seg:
                    mo = mr(MR_O + seg * 16 + d_ * 8)
                    P.op("dve", lambda e, mcur=mcur, mo=mo: e.tensor_copy(out=mo, in_=mcur), reads=[b_mrow], writes=[b_mrow])
        P.dma("sp", "nm", lambda e: e.dma_start(out=nm_d.rearrange("s a h -> (s a h)").rearrange("(o n) -> o n", o=1),
                                                in_=mrow[0:1, MR_O:MR_O + NSEG * 16]), reads=[b_mrow])
        P.op("pe", lambda e: e.matmul(psum[6][:, 0:128], lhsT=ones_f[0:1, :], rhs=mrow[0:1, MR_O:MR_O + 128], start=True, stop=True),
             reads=[b_mrow, b_mc], writes=[b_ps[6]])
        P.op("act", lambda e: e.activation(out=SCL.rearrange("p s c -> p (s c)"), in_=psum[6][:, 0:128], func=AF.Exp, scale=-1.0),
             reads=[b_ps[6]], writes=[b_scl])

        import os
        NHP = int(os.environ.get('M_HP', '4'))
        for hp in range(NHP):
            P.barrier()
            P.dma("pool", "mc5", lambda e, hp=hp: e.dma_start(out=mln_bc, in_=mlnorm_d[hp * 256:(hp + 1) * 256].partition_broadcast(128)),
                  writes=[b_mc])
            P.dma("pool", "QKpan0", lambda e, hp=hp: e.dma_start(out=QKpan[0], in_=w_in_r[:, :, OFF_QA + hp * 128:OFF_QA + (hp + 1) * 128]),
                  writes=[b_QKpan[0]])
            P.dma("pool", "QKpan1", lambda e, hp=hp: e.dma_start(out=QKpan[1], in_=w_in_r[:, :, OFF_KA + hp * 128:OFF_KA + (hp + 1) * 128]),
                  writes=[b_QKpan[1]])
            P.dma("pool", "VOpan", lambda e, hp=hp: e.dma_start(out=VOpan, in_=w_in_r[:, :, OFF_VA + hp * 256:OFF_VA + (hp + 1) * 256]),
                  writes=[b_VOpan])
            for which in range(2):
                for tt in range(4):
                    pb = 4 + tt % 2
                    for k in range(KC):
                        P.op("pe", lambda e, which=which, tt=tt, k=k, pb=pb: e.matmul(
                            psum[pb][:], lhsT=QKpan[which][:, k, :], rhs=hT[:, k, tt * 512:(tt + 1) * 512],
                            start=(k == 0), stop=(k == KC - 1)),
                            reads=[b_QKpan[which]] + b_hT[4 * tt:4 * tt + 4], writes=[b_ps[pb]], inc=(k == KC - 1))
                    if which == 0:
                        P.op("act", lambda e, tt=tt, pb=pb: e.activation(out=QaT[:, tt * 512:(tt + 1) * 512], in_=psum[pb][:],
                                                                         func=AF.Copy, scale=0.125),
                             reads=[b_ps[pb]], writes=[b_QaT])
                    else:
                        P.op("act", lambda e, tt=tt, pb=pb: e.copy(out=KaTpad[0][0:64, tt * 512:(tt + 1) * 512], in_=psum[pb][0:64, :]),
                             reads=[b_ps[pb]], writes=[b_KaT[0]])
                        P.op("dve", lambda e, tt=tt, pb=pb: e.tensor_copy(out=KaTpad[1][64:128, tt * 512:(tt + 1) * 512],
                                                                          in_=psum[pb][64:128, :]),
                             reads=[b_ps[pb]], writes=[b_KaT[1]])
            for b in range(NB):
                for hl in range(2):
                    P.op("pe", lambda e, b=b, hl=hl: e.transpose(out=psum_b[7][:, hl * 128:(hl + 1) * 128],
                                                                 in_=KaTpad[hl][:, b * 128:(b + 1) * 128], identity=ident_b[:]),
                         reads=[b_KaT[hl], b_ident], writes=[b_ps[7]], inc=(hl == 1))
                for hl in range(2):
                    P.op("dve", lambda e, b=b, hl=hl: e.tensor_copy(out=Katok[:, b, hl * 64:(hl + 1) * 64],
                                                                    in_=psum_b[7][:, hl * 128 + hl * 64:hl * 128 + (hl + 1) * 64]),
                         reads=[b_ps[7]], writes=[b_Katok])
            for b in range(NB):
                pb = 4 + b % 2
                for k in range(KC):
                    P.op("pe", lambda e, b=b, k=k, pb=pb: e.matmul(psum[pb][:, 0:256], lhsT=hT[:, k, b * 128:(b + 1) * 128],
                                                                   rhs=VOpan[:, k, :], start=(k == 0), stop=(k == KC - 1)),
                         reads=[b_hT[b], b_VOpan], writes=[b_ps[pb]], inc=(k == KC - 1))
                P.op("act", lambda e, b=b, pb=pb: e.copy(out=Vaext[:, b, :, 0:128],
                                                         in_=psum[pb][:, 0:256].rearrange("p (h d) -> p h d", d=128)),
                     reads=[b_ps[pb]], writes=[b_Va])
            P.barrier()
            P.dma("pool", "VOpan", lambda e, hp=hp: e.dma_start(out=VOpan, in_=w_in_r[:, :, OFF_OA + hp * 256:OFF_OA + (hp + 1) * 256]),
                  writes=[b_VOpan])
            for cc in range(2):
                for tt in range(4):
                    pb = 4 + tt % 2
                    for k in range(KC):
                        P.op("pe", lambda e, cc=cc, tt=tt, k=k, pb=pb: e.matmul(
                            psum[pb][:], lhsT=VOpan[:, k, cc * 128:(cc + 1) * 128], rhs=hT[:, k, tt * 512:(tt + 1) * 512],
                            start=(k == 0), stop=(k == KC - 1)),
                            reads=[b_VOpan] + b_hT[4 * tt:4 * tt + 4], writes=[b_ps[pb]], inc=(k == KC - 1))
                    P.op("act", lambda e, pb=pb: e.activation(out=sgtmp, in_=psum[pb][:], func=AF.Exp, scale=-1.0),
                         reads=[b_ps[pb]], writes=[b_sgtmp])
                    P.op("dve", lambda e: e.tensor_scalar(out=sgtmp, in0=sgtmp, scalar1=1.0, scalar2=None, op0=ALU.add),
                         reads=[b_sgtmp], writes=[b_sgtmp])
                    P.op("dve", lambda e: e.reciprocal(out=sgtmp, in_=sgtmp), reads=[b_sgtmp], writes=[b_sgtmp])
                    P.op("act", lambda e, cc=cc, tt=tt: e.copy(out=sgoT[:, cc, tt * 512:(tt + 1) * 512], in_=sgtmp),
                         reads=[b_sgtmp], writes=[b_sgoT])
            for d_ in range(2):
                for hl in range(2):
                    rows = slice(hl * 64, (hl + 1) * 64)
                    dh = d_ * 8 + 2 * hp + hl
                    P.op("dve", lambda e, d_=d_, rows=rows, dh=dh: e.tensor_copy(out=EBp[rows, :, d_:d_ + 1], in_=EBt[rows, :, dh:dh + 1]),
                         reads=[b_g], writes=[b_EBp])
                    P.op("dve", lambda e, d_=d_, rows=rows, dh=dh: e.tensor_copy(out=SCLp[rows, :, d_:d_ + 1], in_=SCL[rows, :, dh:dh + 1]),
                         reads=[b_scl], writes=[b_EBp])
            for d_ in range(2):
                P.op("pool", lambda e, d_=d_: e.memset(Cst2[d_], 0.0), writes=[b_Cst2[d_]])
                for hl in range(2):
                    h = 2 * hp + hl
                    dh = d_ * 8 + h
                    rows = slice(hl * 64, (hl + 1) * 64)
                    P.op("pool", lambda e, d_=d_, hl=hl: e.memset(Cb[d_][hl], 0.0), writes=[b_Cb[d_][hl]])
                    P.dma("pool", f"c0{d_}{hl}", lambda e, d_=d_, h=h, rows=rows: e.dma_start(
                        out=Cst2[d_][rows, 0:128], in_=c0_d[d_, h]), writes=[b_Cst2[d_]])
                    P.dma("pool", f"n0{d_}{hl}", lambda e, d_=d_, h=h, rows=rows: e.dma_start(
                        out=Cst2[d_][rows, 128:129], in_=n0_d[d_, h].rearrange("(p o) -> p o", o=1)), writes=[b_Cst2[d_]])
                    P.op("dve", lambda e, d_=d_, rows=rows, dh=dh: e.tensor_scalar(
                        out=Cst2[d_][rows, 0:129], in0=Cst2[d_][rows, 0:129], scalar1=EM0[rows, dh:dh + 1], scalar2=None,
                        op0=ALU.mult), reads=[b_Cst2[d_], b_mc], writes=[b_Cst2[d_]])
                    P.op("act", lambda e, d_=d_, hl=hl, rows=rows: e.copy(out=Cb[d_][hl][rows, 0:129], in_=Cst2[d_][rows, 0:129]),
                         reads=[b_Cst2[d_]], writes=[b_Cb[d_][hl]])
            for j in range(NB):
                units = []
                for d_ in range(2):
                    blk = j if d_ == 0 else NB - 1 - j
                    for hl in range(2):
                        units.append((d_ * 2 + hl, d_, hl, blk))
                jp = j % 2
                sbk = jp
                nbks = (2, 3) if jp == 0 else (4, 5)
                gbks = (6, 7)
                last = (j == NB - 1)

                def geo(u, d_, hl, blk):
                    h = 2 * hp + hl
                    dh = d_ * 8 + h
                    rows = slice(hl * 64, (hl + 1) * 64)
                    cols = slice(blk * 128, (blk + 1) * 128)
                    return h, dh, rows, cols
                for (u, d_, hl, blk) in units:
                    h, dh, rows, cols = geo(u, d_, hl, blk)
                    P.op("pe", lambda e, u=u, hl=hl, cols=cols, sbk=sbk: e.matmul(
                        psum[sbk][:, u * 128:(u + 1) * 128], lhsT=KaTpad[hl][:, cols], rhs=QaT[:, cols], start=True, stop=True),
                        reads=[b_KaT[hl], b_QaT], writes=[b_ps[sbk]], inc=(u == 3))
                for (u, d_, hl, blk) in units:
                    h, dh, rows, cols = geo(u, d_, hl, blk)
                    ucol = Ut[:, blk, dh:dh + 1]
                    P.op("act", lambda e, d_=d_, hl=hl, blk=blk, ucol=ucol: e.activation(
                        out=Kupad[d_][hl][:, hl * 64:(hl + 1) * 64], in_=Katok[:, blk, hl * 64:(hl + 1) * 64],
                        func=AF.Copy, scale=ucol), reads=[b_Katok, b_g], writes=[b_Ku[d_][hl]])
                for (u, d_, hl, blk) in units:
                    h, dh, rows, cols = geo(u, d_, hl, blk)
                    ucol = Ut[:, blk, dh:dh + 1]
                    mask = Umask if d_ == 0 else Lmask
                    pm = Pm8[jp][u]
                    P.op("dve", lambda e, u=u, sbk=sbk, pm=pm, ucol=ucol, mask=mask: e.scalar_tensor_tensor(
                        out=pm, in0=psum[sbk][:, u * 128:(u + 1) * 128], scalar=ucol, in1=mask, op0=ALU.mult, op1=ALU.mult),
                        reads=[b_ps[sbk], b_g, b_mc], writes=[b_Pm8[jp][u]])
                for (u, d_, hl, blk) in units:
                    h, dh, rows, cols = geo(u, d_, hl, blk)
                    gbk = gbks[u // 2]
                    P.op("pe", lambda e, gbk=gbk, d_=d_, hl=hl, blk=blk, u=u: e.matmul(
                        psum[gbk][:, 0:129], lhsT=Kupad[d_][hl], rhs=Vaext[:, blk, hl, 0:129], start=(u % 2 == 0), stop=(u % 2 == 1),
                        skip_group_check=True),
                        reads=[b_Ku[d_][hl], b_Va], writes=[b_ps[gbk]], inc=(u % 2 == 1))
                for d_ in range(2):
                    blk = j if d_ == 0 else NB - 1 - j
                    gbk = gbks[d_]
                    Cs = Cst2[d_]
                    P.op("dve", lambda e, Cs=Cs, gbk=gbk: e.tensor_tensor(out=Cs[:, 0:129], in0=psum[gbk][:, 0:129], in1=Cs[:, 0:129], op=ALU.add),
                         reads=[b_ps[gbk], b_Cst2[d_]], writes=[b_Cst2[d_]])
                for d_ in range(2):
                    blk = j if d_ == 0 else NB - 1 - j
                    Cs = Cst2[d_]
                    P.op("dve", lambda e, Cs=Cs, blk=blk, d_=d_: e.tensor_scalar(
                        out=Cs[:, 0:129], in0=Cs[:, 0:129], scalar1=EBp[:, blk, d_:d_ + 1], scalar2=None, op0=ALU.mult),
                        reads=[b_Cst2[d_], b_EBp], writes=[b_Cst2[d_]])
                for d_ in range(2):
                    blk = j if d_ == 0 else NB - 1 - j
                    seg = blk // 2
                    seg_end = (blk % 2 == 1) if d_ == 0 else (blk % 2 == 0)
                    Cs = Cst2[d_]
                    if seg_end:
                        st = stg[d_]
                        P.op("dve", lambda e, Cs=Cs, st=st, seg=seg, d_=d_: e.tensor_scalar(
                            out=st[:, 0:129], in0=Cs[:, 0:129], scalar1=SCLp[:, seg, d_:d_ + 1], scalar2=None, op0=ALU.mult),
                            reads=[b_Cst2[d_], b_EBp], writes=[b_stg[d_]])
                        P.dma("sp", f"stg{d_}", lambda e, st=st, seg=seg, d_=d_, hp=hp: e.dma_start(
                            out=nC_d[seg, d_, 2 * hp:2 * hp + 2].rearrange("h p c -> (h p) c"), in_=st[:, 0:128]), reads=[b_stg[d_]])
                        P.dma("sp", f"stg{d_}", lambda e, st=st, seg=seg, d_=d_, hp=hp: e.dma_start(
                            out=nn_d[seg, d_, 2 * hp:2 * hp + 2].rearrange("h (p o) -> (h p) o", o=1), in_=st[:, 128:129]), reads=[b_stg[d_]])
                        if not last:
                            P.op("dve", lambda e, Cs=Cs: e.tensor_scalar(
                                out=Cs[:, 0:129], in0=Cs[:, 0:129], scalar1=keepc[:, 0:1], scalar2=None, op0=ALU.mult),
                                reads=[b_Cst2[d_], b_mc], writes=[b_Cst2[d_]])
                for (u, d_, hl, blk) in units:
                    h, dh, rows, cols = geo(u, d_, hl, blk)
                    nbk = nbks[u // 2]
                    c0 = (u % 2) * 129
                    pm = Pm8[jp][u]
                    P.op("pe", lambda e, nbk=nbk, c0=c0, pm=pm, blk=blk, hl=hl, u=u: e.matmul(
                        psum[nbk][:, c0:c0 + 129], lhsT=pm, rhs=Vaext[:, blk, hl, 0:129], start=(u % 2 == 0), stop=False,
                        skip_group_check=True),
                        reads=[b_Pm8[jp][u], b_Va], writes=[b_ps[nbk]], inc=False)
                    P.op("pe", lambda e, nbk=nbk, c0=c0, cols=cols, d_=d_, hl=hl: e.matmul(
                        psum[nbk][:, c0:c0 + 129], lhsT=QaT[:, cols], rhs=Cb[d_][hl][:, 0:129], start=False, stop=True,
                        skip_group_check=True),
                        reads=[b_QaT, b_Cb[d_][hl]], writes=[b_ps[nbk]], inc=(u % 2 == 1))
                if not last:
                    for (u, d_, hl, blk) in units:
                        h, dh, rows, cols = geo(u, d_, hl, blk)
                        Cs = Cst2[d_]
                        P.op("act", lambda e, Cs=Cs, rows=rows, d_=d_, hl=hl: e.copy(out=Cb[d_][hl][rows, 0:129], in_=Cs[rows, 0:129]),
                             reads=[b_Cst2[d_]], writes=[b_Cb[d_][hl]])
                for d_ in range(2):
                    blk = j if d_ == 0 else NB - 1 - j
                    nbk = nbks[d_]
                    sd = smd[jp][d_]
                    P.op("act", lambda e, nbk=nbk, sd=sd: e.activation(out=sd[:, 0:2], in_=psum[nbk][:, 128:258:129], func=AF.Abs),
                         reads=[b_ps[nbk]], writes=[b_smd[jp][d_]])
                for which in range(4):
                    for d_ in range(2):
                        blk = j if d_ == 0 else NB - 1 - j
                        dh0 = d_ * 8 + 2 * hp
                        E2 = Et[:, blk, dh0:dh0 + 2]
                        sd = smd[jp][d_]
                        bsd = b_smd[jp][d_]
                        if which == 0:
                            P.op("dve", lambda e, sd=sd, E2=E2: e.tensor_tensor(out=sd[:, 0:2], in0=sd[:, 0:2], in1=E2, op=ALU.mult),
                                 reads=[bsd, b_g], writes=[bsd])
                        elif which == 1:
                            P.op("dve", lambda e, sd=sd: e.tensor_scalar(out=sd[:, 0:2], in0=sd[:, 0:2], scalar1=1.0, scalar2=None, op0=ALU.max),
                                 reads=[bsd], writes=[bsd])
                        elif which == 2:
                            P.op("dve", lambda e, sd=sd: e.reciprocal(out=sd[:, 0:2], in_=sd[:, 0:2]), reads=[bsd], writes=[bsd])
                        else:
                            P.op("dve", lambda e, sd=sd, E2=E2: e.tensor_tensor(out=sd[:, 2:4], in0=sd[:, 0:2], in1=E2, op=ALU.mult),
                                 reads=[bsd, b_g], writes=[bsd])
                for (u, d_, hl, blk) in units:
                    h, dh, rows, cols = geo(u, d_, hl, blk)
                    nbk = nbks[u // 2]
                    c0 = (u % 2) * 129
                    sd = smd[jp][d_]
                    bsd = b_smd[jp][d_]
                    rcol = sd[:, 2 + hl:3 + hl]
                    if j < NB // 2:
                        P.op("act", lambda e, nbk=nbk, c0=c0, rcol=rcol, blk=blk, hl=hl: e.activation(
                            out=Hbuf[:, blk, hl, :], in_=psum[nbk][:, c0:c0 + 128], func=AF.Copy, scale=rcol),
                            reads=[b_ps[nbk], bsd], writes=[b_Hb[blk][hl]])
                    else:
                        P.op("dve", lambda e, nbk=nbk, c0=c0, rcol=rcol, blk=blk, hl=hl: e.scalar_tensor_tensor(
                            out=Hbuf[:, blk, hl, :], in0=psum[nbk][:, c0:c0 + 128], scalar=rcol, in1=Hbuf[:, blk, hl, :],
                            op0=ALU.mult, op1=ALU.add),
                            reads=[b_ps[nbk], bsd, b_Hb[blk][hl]], writes=[b_Hb[blk][hl]])
            for b in range(NB):
                bp = b % 2
                pbt = 6 + bp
                for hl in range(2):
                    h = 2 * hp + hl
                    q_ = bp * 2 + hl
                    sm_ = msm[q_]
                    bsm = b_msm[q_]
                    hs = Hbuf[:, b, hl, :]
                    ya_ = yat4[q_]
                    P.op("act", lambda e, hs=hs, sm_=sm_: e.activation(out=mjunk, in_=hs, func=AF.Square, accum_out=sm_[:, 2:3]),
                         reads=[b_Hb[b][hl]], writes=[b_mjunk, bsm])
                    P.op("act", lambda e, sm_=sm_: e.activation(out=sm_[:, 3:4], in_=sm_[:, 2:3], func=AF.Ln, scale=1.0 / 128, bias=EPS),
                         reads=[bsm], writes=[bsm])
                    P.op("act", lambda e, sm_=sm_: e.activation(out=sm_[:, 4:5], in_=sm_[:, 3:4], func=AF.Exp, scale=-0.5),
                         reads=[bsm], writes=[bsm])
                    P.op("dve", lambda e, hs=hs, sm_=sm_, hl=hl, ya_=ya_: e.scalar_tensor_tensor(
                        out=ya_, in0=hs, scalar=sm_[:, 4:5], in1=mln_bc[:, hl * 128:(hl + 1) * 128], op0=ALU.mult, op1=ALU.mult),
                        reads=[b_Hb[b][hl], bsm, b_mc], writes=[b_yat4[q_]])
                    P.op("pe", lambda e, hl=hl, ya_=ya_, pbt=pbt: e.transpose(out=psum_b[pbt][:, hl * 128:(hl + 1) * 128], in_=ya_,
                                                                              identity=ident_b[:]),
                         reads=[b_yat4[q_], b_ident], writes=[b_ps[pbt]])
                    P.op("dve", lambda e, hl=hl, h=h, b=b, pbt=pbt: e.tensor_tensor(
                        out=yT[:, h, b * 128:(b + 1) * 128], in0=psum_b[pbt][:, hl * 128:(hl + 1) * 128],
                        in1=sgoT[:, hl, b * 128:(b + 1) * 128], op=ALU.mult),
                        reads=[b_ps[pbt], b_sgoT], writes=[b_yT[h][b]])
        P.barrier()
    mT_d = nc.dram_tensor("mT_scr", [KC, 128, T], BF16).ap()
    x1_d = nc.dram_tensor("x1_scr", [T, D], F32).ap()
    w_pa_r = w_pa_d.rearrange("(k p) c -> p k c", p=128)
    w_pb_r = w_pb_d.rearrange("(k p) c -> p k c", p=128)
    w_out_r = w_out_d.rearrange("(k p) c -> p k c", p=128)
    w_up_r = w_up_d.rearrange("(k p) c -> p k c", p=128)
    w_dn_r = w_down_d.rearrange("(k p) c -> p k c", p=128)
    if upto >= 4:
        P.tag = 'ph3a'
        W.reset()
        pa_pan = [W.alloc([8, 128], BF16) for _ in range(2)]
        pb_pan = [W.alloc([8, 128], BF16) for _ in range(2)]
        ga_pan = [W.alloc([KC, 128], BF16) for _ in range(2)]
        gb_pan = [W.alloc([KC, 128], BF16) for _ in range(2)]
        sga = [W.alloc([512], F32) for _ in range(2)]
        sgb = [W.alloc([512], F32) for _ in range(2)]
        m1 = [W.alloc([512], F32) for _ in range(2)]
        m2 = [W.alloc([512], F32) for _ in range(2)]
        mTc = [W.alloc([T], BF16) for _ in range(2)]
        b_pan3 = [P.bufs(2, "pa"), P.bufs(2, "pb"), P.bufs(2, "ga"), P.bufs(2, "gb")]
        b_sga = P.bufs(2, "sga")
        b_sgb = P.bufs(2, "sgb")
        b_m1 = P.bufs(2, "m1")
        b_m2 = P.bufs(2, "m2")
        b_mTc = P.bufs(2, "mTc")
        all_hT = list(b_hT)
        for c in range(KC):
            s_ = c % 2
            cc = slice(c * 128, (c + 1) * 128)
            P.dma("pool", f"pa{s_}", lambda e, s_=s_, cc=cc: e.dma_start(out=pa_pan[s_], in_=w_pa_r[:, :, cc]), writes=[b_pan3[0][s_]])
            P.dma("pool", f"pb{s_}", lambda e, s_=s_, cc=cc: e.dma_start(out=pb_pan[s_], in_=w_pb_r[:, :, cc]), writes=[b_pan3[1][s_]])
            P.dma("pool", f"ga{s_}", lambda e, s_=s_, c=c: e.dma_start(
                out=ga_pan[s_], in_=w_in_r[:, :, OFF_GA + c * 128:OFF_GA + (c + 1) * 128]), writes=[b_pan3[2][s_]])
            P.dma("pool", f"gb{s_}", lambda e, s_=s_, c=c: e.dma_start(
                out=gb_pan[s_], in_=w_in_r[:, :, OFF_GB + c * 128:OFF_GB + (c + 1) * 128]), writes=[b_pan3[3][s_]])
            for tt in range(4):
                t_ = tt % 2
                tc_ = slice(tt * 512, (tt + 1) * 512)
                bk = [0 + t_, 2 + t_, 4 + t_, 6 + t_]
                yb_a = [b_yT[k][4 * tt + j] for k in range(8) for j in range(4)]
                yb_b = [b_yT[8 + k][4 * tt + j] for k in range(8) for j in range(4)]
                for k in range(8):
                    P.op("pe", lambda e, k=k, s_=s_, tc_=tc_, bk=bk: e.matmul(psum[bk[0]][:], lhsT=pa_pan[s_][:, k, :], rhs=yT[:, k, tc_],
                                                                             start=(k == 0), stop=(k == 7)),
                         reads=[b_pan3[0][s_]] + yb_a, writes=[b_ps[bk[0]]], inc=(k == 7))
                for k in range(KC):
                    P.op("pe", lambda e, k=k, s_=s_, tc_=tc_, bk=bk: e.matmul(psum[bk[1]][:], lhsT=ga_pan[s_][:, k, :], rhs=hT[:, k, tc_],
                                                                             start=(k == 0), stop=(k == KC - 1)),
                         reads=[b_pan3[2][s_]] + all_hT[4 * tt:4 * tt + 4], writes=[b_ps[bk[1]]], inc=(k == KC - 1))
                for k in range(8):
                    P.op("pe", lambda e, k=k, s_=s_, tc_=tc_, bk=bk: e.matmul(psum[bk[2]][:], lhsT=pb_pan[s_][:, k, :], rhs=yT[:, 8 + k, tc_],
                                                                             start=(k == 0), stop=(k == 7)),
                         reads=[b_pan3[1][s_]] + yb_b, writes=[b_ps[bk[2]]], inc=(k == 7))
                for k in range(KC):
                    P.op("pe", lambda e, k=k, s_=s_, tc_=tc_, bk=bk: e.matmul(psum[bk[3]][:], lhsT=gb_pan[s_][:, k, :], rhs=hT[:, k, tc_],
                                                                             start=(k == 0), stop=(k == KC - 1)),
                         reads=[b_pan3[3][s_]] + all_hT[4 * tt:4 * tt + 4], writes=[b_ps[bk[3]]], inc=(k == KC - 1))
                P.op("act", lambda e, t_=t_, bk=bk: e.activation(out=sga[t_], in_=psum[bk[1]][:], func=AF.Sigmoid),
                     reads=[b_ps[bk[1]]], writes=[b_sga[t_]])
                P.op("dve", lambda e, t_=t_, bk=bk: e.tensor_tensor(out=m1[t_], in0=psum[bk[0]][:], in1=sga[t_], op=ALU.mult),
                     reads=[b_ps[bk[0]], b_sga[t_]], writes=[b_m1[t_]])
                P.op("act", lambda e, t_=t_, bk=bk: e.activation(out=sgb[t_], in_=psum[bk[3]][:], func=AF.Sigmoid),
                     reads=[b_ps[bk[3]]], writes=[b_sgb[t_]])
                P.op("dve", lambda e, t_=t_, bk=bk: e.tensor_tensor(out=m2[t_], in0=psum[bk[2]][:], in1=sgb[t_], op=ALU.mult),
                     reads=[b_ps[bk[2]], b_sgb[t_]], writes=[b_m2[t_]])
                P.op("dve", lambda e, t_=t_, s_=s_, tc_=tc_: e.tensor_tensor(out=mTc[s_][:, tc_], in0=m1[t_], in1=m2[t_], op=ALU.add),
                     reads=[b_m1[t_], b_m2[t_]], writes=[b_mTc[s_]])
            P.dma("sp", f"mTc{s_}", lambda e, s_=s_, c=c: e.dma_start(out=mT_d[c], in_=mTc[s_]), reads=[b_mTc[s_]])
        P.barrier()

        P.tag = 'ph3b'
        A_.reset(); B_.reset(); W.reset()
        g1_bc = W.alloc([D], F32)
        wpan2 = [W.alloc([KC, 512], BF16) for _ in range(2)]
        b_wpan2 = P.bufs(2, "wpan2")
        bmb = W.alloc([512], F32)
        b_bmb = B("bmb")
        sc_bc = W.alloc([KC, 128], BF16)
        b_scbc = B("scbc")
        b_g1 = B("g1")
        mod_bcast(2, g1_bc, b_g1, wpan2, b_wpan2, bmb, b_bmb, (0, 1), sc_bc, b_scbc)
        mTt = A_.alloc([KC, 1024], BF16)
        b_mTt = B("mTt")
        xt = B_.alloc([8, D], F32)
        b_xt = P.bufs(8, "xt")
        x1s = [W.alloc([512], F32) for _ in range(2)]
        b_x1s = P.bufs(2, "x1s")
        ctr = 0
        for tile in range(2):
            t0 = tile * 1024
            P.dma("sp", "mTt", lambda e, t0=t0: e.dma_start(out=mTt, in_=mT_d[:, :, t0:t0 + 1024].rearrange("k p t -> p k t")),
                  writes=[b_mTt])
            for tb in range(8):
                P.dma("sp", f"xt{tb}", lambda e, t0=t0, tb=tb: e.dma_start(out=xt[:, tb, :], in_=x_d[t0 + tb * 128:t0 + (tb + 1) * 128, :]),
                      writes=[b_xt[tb]])
            for ct in range(4):
                s_ = ct % 2
                P.dma("pool", f"wpan{s_}", lambda e, s_=s_, ct=ct: e.dma_start(out=wpan2[s_], in_=w_out_r[:, :, ct * 512:(ct + 1) * 512]),
                      writes=[b_wpan2[s_]])
                for tb in range(8):
                    pb = ctr % 4
                    r2 = ctr % 2
                    ctr += 1
                    cols = slice(ct * 512, (ct + 1) * 512)
                    for k in range(KC):
                        P.op("pe", lambda e, k=k, pb=pb, tb=tb, s_=s_: e.matmul(
                            psum[pb][:], lhsT=mTt[:, k, tb * 128:(tb + 1) * 128], rhs=wpan2[s_][:, k, :],
                            start=(k == 0), stop=(k == KC - 1)), reads=[b_mTt, b_wpan2[s_]], writes=[b_ps[pb]], inc=(k == KC - 1))
                    P.op("dve", lambda e, pb=pb, r2=r2, cols=cols: e.tensor_tensor(out=x1s[r2], in0=psum[pb][:], in1=g1_bc[:, cols], op=ALU.mult),
                         reads=[b_ps[pb], b_g1], writes=[b_x1s[r2]])
                    P.op("dve", lambda e, r2=r2, tb=tb, cols=cols: e.tensor_tensor(out=xt[:, tb, cols], in0=xt[:, tb, cols], in1=x1s[r2], op=ALU.add),
                         reads=[b_x1s[r2], b_xt[tb]], writes=[b_xt[tb]])
                    if ct == 3:
                        P.dma("sp", f"x1o{tb}", lambda e, t0=t0, tb=tb: e.dma_start(out=x1_d[t0 + tb * 128:t0 + (tb + 1) * 128, :], in_=xt[:, tb, :]),
                              reads=[b_xt[tb]])
        P.barrier()
    if upto >= 5:
        P.tag = 'ph4'
        TL = 1024
        NW = 342
        ALL = Arena(big, 0, RA + RB + RW)
        g2_d = nc.dram_tensor("g2_scr", [128, D], F32).ap()
        cwT = ALL.alloc([3, 88], F32)
        cbT = ALL.alloc([88], F32)
        nk0T = ALL.alloc([88], F32)
        nk2T = ALL.alloc([88], F32)
        keepc4 = ALL.alloc([1], F32)
        tmp4 = [ALL.alloc([256], F32) for _ in range(2)]
        fsm = [ALL.alloc([4], F32) for _ in range(2)]
        gT_all = ALL.alloc([NFC, TL], BF16)
        U0 = ALL.off
        b_g2 = B("g2")
        b_c4 = B("c4")
        b_wd = B("wd")
        b_tmp4 = P.bufs(2, "tmp4")
        b_fsm = P.bufs(2, "fsm")
        g2_bc = ALL.alloc([D], F32)
        cw_sb = ALL.alloc([4, 128], F32)
        wpan3 = [ALL.alloc([KC, 512], BF16) for _ in range(2)]
        b_wpan3 = P.bufs(2, "wpan3")
        bmb3 = ALL.alloc([512], F32)
        b_bmb3 = B("bmb3")
        sc_bc3 = ALL.alloc([KC, 128], BF16)
        b_scbc3 = B("scbc3")
        mod_bcast(5, g2_bc, b_g2, wpan3, b_wpan3, bmb3, b_bmb3, (0, 1), sc_bc3, b_scbc3)
        P.dma("sp", "g2st", lambda e, src=g2_bc: e.dma_start(out=g2_d, in_=src), reads=[b_g2])
        P.dma("sp", "c41", lambda e: e.dma_start(out=cw_sb[0:88, 0:3, :], in_=convw_d.rearrange("t (c p) -> c t p", p=128)), writes=[b_c4])
        P.dma("sp", "c42", lambda e: e.dma_start(out=cw_sb[0:88, 3, :], in_=convb_d.rearrange("(c p) -> c p", p=128)), writes=[b_c4])
        P.dma("sp", "c43", lambda e: e.dma_start(out=keepc4, in_=keep_d.partition_broadcast(128)), writes=[b_c4])
        for t_ in range(4):
            P.op("pe", lambda e, t_=t_: e.transpose(out=psum[7][:, t_ * 88:(t_ + 1) * 88], in_=cw_sb[0:88, t_, :], identity=ident_f[0:88, 0:88]),
                 reads=[b_c4, b_ident], writes=[b_ps[7]], inc=(t_ == 3))
        P.op("dve", lambda e: e.tensor_copy(out=cwT, in_=psum[7][:, 0:264].rearrange("p (t c) -> p t c", c=88)), reads=[b_ps[7]], writes=[b_c4])
        P.op("dve", lambda e: e.tensor_copy(out=cbT, in_=psum[7][:, 264:352]), reads=[b_ps[7]], writes=[b_c4])
        P.op("dve", lambda e: e.tensor_scalar(out=keepc4, in0=keepc4, scalar1=-1.0, scalar2=None, op0=ALU.add), reads=[b_c4], writes=[b_c4])
        P.op("dve", lambda e: e.tensor_scalar(out=nk0T, in0=cwT[:, 0, :], scalar1=keepc4[:, 0:1], scalar2=None, op0=ALU.mult),
             reads=[b_c4], writes=[b_c4])
        P.op("dve", lambda e: e.tensor_scalar(out=nk2T, in0=cwT[:, 2, :], scalar1=keepc4[:, 0:1], scalar2=None, op0=ALU.mult),
             reads=[b_c4], writes=[b_c4])
        P.barrier()

        for tile in range(2):
            t0 = tile * TL
            ALL.off = U0
            h2T = ALL.alloc([KC, TL + 2], BF16)
            upan = [[ALL.alloc([KC, 512], BF16) for _ in range(2)] for _ in range(2)]
            ua = ALL.alloc([TL + 2], F32)
            acc = [ALL.alloc([TL], F32) for _ in range(2)]
            sil = ALL.alloc([TL], BF16)
            xb = [upan[1][0].rearrange("p k c -> p (k c)").bitcast(F32)[:, 0:D],
                  upan[1][1].rearrange("p k c -> p (k c)").bitcast(F32)[:, 0:D]]
            jk = acc[0].bitcast(BF16)[:, 0:D]
            b_h2T = [B(f"h2T{tile}_{b}") for b in range(9)]
            b_upan = P.bufs(2, "upan")
            b_xb = P.bufs(2, "xb4")
            b_jk = None
            b_gT = [B(f"gT{tile}_{i}") for i in range(NFC)]
            b_ua = B("ua")
            b_acc = P.bufs(2, "acc")
            b_sil = B("sil")
            b_jk = b_acc[0]

            def gT(i):
                return gT_all[:, i, :]

            P.op("pool", lambda e: e.memset(h2T, 0.0), writes=b_h2T)
            P.op("pool", lambda e: e.memset(xb[0], 0.0), writes=[b_xb[0]])
            if tile == 1:
                P.dma("sp", "xb40", lambda e, t0=t0: e.dma_start(out=xb[0][0:1, :], in_=x1_d[t0 - 1:t0, :]), writes=[b_xb[0]])
            else:
                P.dma("sp", "xb40", lambda e, t0=t0: e.dma_start(out=xb[0][1:2, :], in_=x1_d[t0 + TL:t0 + TL + 1, :]), writes=[b_xb[0]])

            def norm_blk(src_fn, dsts, hb, sl=0):
                xb_ = [xb[sl]]
                bx_ = [b_xb[sl]]
                if src_fn is not None:
                    P.dma("sp", f"xb4{sl}", lambda e: e.dma_start(out=xb_[0], in_=src_fn()), writes=[bx_[0]])
                ssq = small[:, 40:41]
                P.op("act", lambda e: e.activation(out=jk, in_=xb_[0], func=AF.Square, accum_out=ssq),
                     reads=[bx_[0]], writes=[b_jk, b_small])
                P.op("act", lambda e: e.activation(out=ssq, in_=ssq, func=AF.Ln, scale=1.0 / D, bias=EPS), reads=[b_small], writes=[b_small])
                P.op("act", lambda e: e.activation(out=ssq, in_=ssq, func=AF.Exp, scale=-0.5), reads=[b_small], writes=[b_small])
                P.op("dve", lambda e: e.tensor_scalar(out=xb_[0], in0=xb_[0], scalar1=ssq, scalar2=None, op0=ALU.mult),
                     reads=[b_small, bx_[0]], writes=[bx_[0]])
                for k4 in range(4):
                    pb = 4 + k4
                    for kk in range(4):
                        k = k4 * 4 + kk
                        P.op("pe", lambda e, k=k, kk=kk, pb=pb: e.transpose(
                            out=psum[pb][:, kk * 128:(kk + 1) * 128], in_=xb_[0][:, k * 128:(k + 1) * 128], identity=ident_f[:]),
                            reads=[bx_[0], b_ident], writes=[b_ps[pb]], inc=(kk == 3))
                    for kk in range(4):
                        k = k4 * 4 + kk
                        for (pc0, n, dc0) in dsts:
                            if k4 % 2 == 0:
                                P.op("act", lambda e, k=k, kk=kk, pb=pb, pc0=pc0, n=n, dc0=dc0: e.activation(
                                    out=h2T[:, k, dc0:dc0 + n], in_=psum[pb][:, kk * 128 + pc0:kk * 128 + pc0 + n],
                                    func=AF.Identity, scale=s2T[:, k:k + 1], bias=modT[:, 48 + k:48 + k + 1]),
                                    reads=[b_ps[pb], b_s, b_modT], writes=[hb])
                            else:
                                P.op("dve", lambda e, k=k, kk=kk, pb=pb, pc0=pc0, n=n, dc0=dc0: e.tensor_scalar(
                                    out=h2T[:, k, dc0:dc0 + n], in0=psum[pb][:, kk * 128 + pc0:kk * 128 + pc0 + n],
                                    scalar1=s2T[:, k:k + 1], scalar2=modT[:, 48 + k:48 + k + 1], op0=ALU.mult, op1=ALU.add),
                                    reads=[b_ps[pb], b_s, b_modT], writes=[hb])
            if tile == 1:
                norm_blk(None, [(0, 1, 0)], b_h2T[8])
            else:
                norm_blk(None, [(1, 1, TL + 1)], b_h2T[8])
            for b in range(8):
                norm_blk(lambda b=b, t0=t0: x1_d[t0 + b * 128:t0 + (b + 1) * 128, :], [(0, 128, 1 + b * 128)], b_h2T[b], sl=(b + 1) % 2)

            for i in range(NFC):
                grp, gi = divmod(i, 4)
                s_ = grp % 2
                if gi == 0:
                    P.dma("pool", f"upan{s_}a", lambda e, s_=s_, grp=grp: e.dma_start(out=upan[s_][0], in_=w_up_r[:, :, grp * 512:(grp + 1) * 512]),
                          writes=[b_upan[s_]] + (b_xb if s_ == 1 else []))
                    P.dma("pool", f"upan{s_}b", lambda e, s_=s_, grp=grp: e.dma_start(
                        out=upan[s_][1], in_=w_up_r[:, :, D_FF + grp * 512:D_FF + (grp + 1) * 512]), writes=[b_upan[s_]] + (b_xb if s_ == 1 else []))
                for half in range(2):
                    c = i if half == 0 else NFC + i
                    for n3 in range(3):
                        pb = half * 3 + n3
                        for k in range(KC):
                            P.op("pe", lambda e, k=k, pb=pb, half=half, n3=n3, s_=s_, gi=gi: e.matmul(
                                psum[pb][:, 0:NW], lhsT=upan[s_][half][:, k, gi * 128:(gi + 1) * 128],
                                rhs=h2T[:, k, n3 * NW:(n3 + 1) * NW], start=(k == 0), stop=(k == KC - 1)),
                                reads=[b_upan[s_]] + b_h2T, writes=[b_ps[pb]], inc=(k == KC - 1))
                    ac = acc[half]
                    bac = b_acc[half]
                    for n3 in range(3):
                        pb = half * 3 + n3
                        P.op("act", lambda e, pb=pb, n3=n3: e.copy(out=ua[:, n3 * NW:(n3 + 1) * NW], in_=psum[pb][:, 0:NW]),
                             reads=[b_ps[pb]], writes=[b_ua])
                    eng = "dve"
                    P.op("act", lambda e, ac=ac, c=c: e.activation(out=ac, in_=ua[:, 1:TL + 1], func=AF.Identity,
                                                                   scale=cwT[:, 1, c:c + 1], bias=cbT[:, c:c + 1]),
                         reads=[b_ua, b_c4], writes=[bac])
                    P.op(eng, lambda e, ac=ac, c=c: e.scalar_tensor_tensor(out=ac, in0=ua[:, 0:TL], scalar=cwT[:, 0, c:c + 1], in1=ac,
                                                                           op0=ALU.mult, op1=ALU.add),
                         reads=[b_ua, b_c4, bac], writes=[bac])
                    P.op(eng, lambda e, ac=ac, c=c: e.scalar_tensor_tensor(out=ac, in0=ua[:, 2:TL + 2], scalar=cwT[:, 2, c:c + 1], in1=ac,
                                                                           op0=ALU.mult, op1=ALU.add),
                         reads=[b_ua, b_c4, bac], writes=[bac])
                    acv = ac.rearrange("p (s t) -> p s t", t=256)
                    ul = ua[:, 0:TL].rearrange("p (s t) -> p s t", t=256)
                    ur = ua[:, 2:TL + 2].rearrange("p (s t) -> p s t", t=256)
                    P.op(eng, lambda e, acv=acv, ul=ul, c=c: e.scalar_tensor_tensor(
                        out=acv[:, :, 0:1], in0=ul[:, :, 0:1], scalar=nk0T[:, c:c + 1], in1=acv[:, :, 0:1], op0=ALU.mult, op1=ALU.add),
                        reads=[b_ua, b_c4, bac], writes=[bac])
                    P.op(eng, lambda e, acv=acv, ur=ur, c=c: e.scalar_tensor_tensor(
                        out=acv[:, :, 255:256], in0=ur[:, :, 255:256], scalar=nk2T[:, c:c + 1], in1=acv[:, :, 255:256],
                        op0=ALU.mult, op1=ALU.add), reads=[b_ua, b_c4, bac], writes=[bac])
                P.op("act", lambda e: e.activation(out=sil, in_=acc[0], func=AF.Silu), reads=[b_acc[0]], writes=[b_sil])
                P.op("dve", lambda e, i=i: e.tensor_tensor(out=gT(i), in0=sil, in1=acc[1], op=ALU.mult),
                     reads=[b_sil, b_acc[1]], writes=[b_gT[i]])
            P.barrier()

            ALL.off = U0
            x2 = ALL.alloc([8, D], F32)
            wd = [ALL.alloc([NFC, 256], BF16) for _ in range(2)]
            g2s = [ALL.alloc([256], F32) for _ in range(2)]
            fn_bc = wd[0].rearrange("p k c -> p (k c)").bitcast(F32)[:, 0:D]
            fjunk = wd[1].rearrange("p k c -> p (k c)")[:, 0:TL]
            b_wdd = P.bufs(2, "wdd")
            b_g2s = P.bufs(2, "g2s")
            b_x2 = [B(f"x2{tile}_{b}") for b in range(8)]
            P.dma("sp", "x2ld", lambda e, t0=t0: e.dma_start(out=x2, in_=x1_d[t0:t0 + TL, :].rearrange("(b p) c -> p b c", p=128)),
                  writes=b_x2)
            ctr = 0
            for ct in range(8):
                cols = slice(ct * 256, (ct + 1) * 256)
                w_ = ct % 2
                P.dma("pool", f"wdpan{w_}", lambda e, cols=cols, w_=w_: e.dma_start(out=wd[w_], in_=w_dn_r[:, :, cols]), writes=[b_wdd[w_]])
                P.dma("sp", f"g2s{w_}", lambda e, cols=cols, w_=w_: e.dma_start(out=g2s[w_], in_=g2_d[:, cols]), writes=[b_g2s[w_]])
                for tb in range(8):
                    pb = ctr % 4
                    tq = ctr % 2
                    ctr += 1
                    for kc in range(NFC):
                        P.op("pe", lambda e, kc=kc, pb=pb, tb=tb, w_=w_: e.matmul(
                            psum[pb][:, 0:256], lhsT=gT(kc)[:, tb * 128:(tb + 1) * 128], rhs=wd[w_][:, kc, :],
                            start=(kc == 0), stop=(kc == NFC - 1)), reads=[b_gT[kc], b_wdd[w_]], writes=[b_ps[pb]], inc=(kc == NFC - 1))
                    P.op("dve", lambda e, pb=pb, tq=tq, w_=w_: e.tensor_tensor(out=tmp4[tq], in0=psum[pb][:, 0:256], in1=g2s[w_],
                                                                              op=ALU.mult), reads=[b_ps[pb], b_g2s[w_]], writes=[b_tmp4[tq]])
                    P.op("dve", lambda e, tq=tq, tb=tb, cols=cols: e.tensor_tensor(out=x2[:, tb, cols], in0=x2[:, tb, cols], in1=tmp4[tq],
                                                                                  op=ALU.add), reads=[b_tmp4[tq], b_x2[tb]], writes=[b_x2[tb]])
            P.dma("sp", "c40", lambda e: e.dma_start(out=fn_bc, in_=fnorm_d.partition_broadcast(128)), writes=[b_wdd[0]])
            b_fj = b_wdd[1]
            jk2 = upan_alias = None
            for tb in range(8):
                f = tb % 2
                P.op("act", lambda e, tb=tb, f=f: e.activation(out=fjunk, in_=x2[:, tb, 0:TL], func=AF.Square, accum_out=fsm[f][:, 0:1]),
                     reads=[b_x2[tb]], writes=[b_fj, b_fsm[f]])
                P.op("act", lambda e, tb=tb, f=f: e.activation(out=fjunk, in_=x2[:, tb, TL:D], func=AF.Square, accum_out=fsm[f][:, 1:2]),
                     reads=[b_x2[tb]], writes=[b_fj, b_fsm[f]])
                P.op("dve", lambda e, f=f: e.tensor_tensor(out=fsm[f][:, 2:3], in0=fsm[f][:, 0:1], in1=fsm[f][:, 1:2], op=ALU.add),
                     reads=[b_fsm[f]], writes=[b_fsm[f]])
                P.op("act", lambda e, f=f: e.activation(out=fsm[f][:, 2:3], in_=fsm[f][:, 2:3], func=AF.Ln, scale=1.0 / D, bias=EPS),
                     reads=[b_fsm[f]], writes=[b_fsm[f]])
                P.op("act", lambda e, f=f: e.activation(out=fsm[f][:, 3:4], in_=fsm[f][:, 2:3], func=AF.Exp, scale=-0.5),
                     reads=[b_fsm[f]], writes=[b_fsm[f]])
                P.op("dve", lambda e, tb=tb, f=f: e.scalar_tensor_tensor(out=x2[:, tb, :], in0=x2[:, tb, :], scalar=fsm[f][:, 3:4], in1=fn_bc,
                                                                         op0=ALU.mult, op1=ALU.mult),
                     reads=[b_x2[tb], b_fsm[f], b_wdd[0]], writes=[b_x2[tb]])
                P.dma("sp", f"yout{f}", lambda e, tb=tb, t0=t0: e.dma_start(out=y_d[t0 + tb * 128:t0 + (tb + 1) * 128, :], in_=x2[:, tb, :]),
                      reads=[b_x2[tb]])
            P.barrier()
    if dbg:
        if "hT" in dbg_d:
            P.dma("sp", "dbg", lambda e: e.dma_start(out=dbg_d["hT"], in_=hT), reads=b_hT)
        if "yT" in dbg_d:
            P.dma("sp", "dbg", lambda e: e.dma_start(out=dbg_d["yT"], in_=yT), reads=[b for l in b_yT for b in l])
        if "modT" in dbg_d:
            P.dma("sp", "dbg", lambda e: e.dma_start(out=dbg_d["modT"], in_=modT[:]), reads=[b_modT])

    P.barrier()
    ok, stuck, val = simulate(P)
    print('SIM', ok, stuck if not ok else '', {k: len(v) for k, v in P.ops.items()}, 'nsem', len(P.sem_keys()))
    assert ok

    with ExitStack() as es2:
        sems = {k: es2.enter_context(nc.semaphore("s_" + k.replace(":", "_"))) for k in P.sem_keys()}
        with nc.Block() as block:
            @block.tensor
            def _(e):
                P.emit("pe", e, sems)

            @block.scalar
            def _(e):
                P.emit("act", e, sems)

            @block.vector
            def _(e):
                P.emit("dve", e, sems)

            @block.gpsimd
            def _(e):
                P.emit("pool", e, sems)

            @block.sync
            def _(e):
                P.emit("sp", e, sems)
    es.close()
    return nc


def rope_tables():
    quarter = 16
    t = np.arange(T)
    row = (t // 64).astype(np.float32)
    col = (t % 64).astype(np.float32)
    inv_freq = np.power(np.float32(10000.0), -np.arange(quarter, dtype=np.float32) / np.float32(quarter)).astype(np.float32)
    ang_r = row[:, None] * inv_freq
    ang_c = col[:, None] * inv_freq
    ang = np.concatenate([ang_r, ang_r, ang_c, ang_c], axis=-1).astype(np.float32)
    cos = np.cos(ang).astype(np.float32).T
    sin = np.sin(ang).astype(np.float32).T
    return (np.ascontiguousarray(np.concatenate([cos, cos], axis=0)),
            np.ascontiguousarray(np.concatenate([sin, sin], axis=0)))


def rot_matrix():
    r = np.zeros((128, 128), np.float32)
    for base in (0, 64):
        for m in range(64):
            blk = m // 16
            if blk == 0:
                r[base + m + 16, base + m] = -1.0
            elif blk == 1:
                r[base + m - 16, base + m] = 1.0
            elif blk == 2:
                r[base + m + 16, base + m] = -1.0
            else:
                r[base + m - 16, base + m] = 1.0
    return r


def make_in_maps(inp):
    maps = []
    cosT, sinT = rope_tables()
    shared = {
        "w_mod": np.ascontiguousarray(inp["w_mod"][0]),
        "b_mod": np.ascontiguousarray(inp["b_mod"][0]),
        "norm1": np.ascontiguousarray(inp["norm1"][0]),
        "norm2": np.ascontiguousarray(inp["norm2"][0]),
        "final_norm": np.ascontiguousarray(inp["final_norm"]),
        "ident": np.eye(128, dtype=np.float32),
        "w_in": np.ascontiguousarray(inp["w_in"][0]),
        "rrot": rot_matrix(),
        "lamv": np.ascontiguousarray(np.stack([inp["lam_q1"][0], inp["lam_k1"][0], inp["lam_q2"][0], inp["lam_k2"][0]])),
        "diff_norm": np.ascontiguousarray(inp["diff_norm"][0]),
        "b_gates": np.ascontiguousarray(inp["b_gates"][0]),
        "w_pa": np.ascontiguousarray(inp["w_pa"][0]),
        "w_pb": np.ascontiguousarray(inp["w_pb"][0]),
        "w_out": np.ascontiguousarray(inp["w_out"][0]),
        "w_up": np.ascontiguousarray(inp["w_up"][0]),
        "w_down": np.ascontiguousarray(inp["w_down"][0]),
        "conv_w": np.ascontiguousarray(inp["conv_w"][0]),
        "conv_b": np.ascontiguousarray(inp["conv_b"][0]),
        "umask": np.ascontiguousarray(np.triu(np.ones((128, 128), np.float32))),
        "lmask": np.ascontiguousarray(np.tril(np.ones((128, 128), np.float32))),
        "mlstm_norm": np.ascontiguousarray(inp["mlstm_norm"][0]),
    }
    for core in range(8):
        m = dict(shared)
        if core < 4:
            m["x"] = np.ascontiguousarray(inp["x_sample"][core])
            m["cvec"] = np.ascontiguousarray(inp["c"][core])
            m["ck"] = np.ascontiguousarray(inp["cache_k"][core, 0])
            m["cv"] = np.ascontiguousarray(inp["cache_v"][core, 0])
            m["cosT"] = cosT
            m["sinT"] = sinT
            m["abias"] = np.zeros((128, NSEG * 18), np.float32)
            m["c0"] = np.ascontiguousarray(inp["state_C"][core, 0])
            m["n0"] = np.ascontiguousarray(inp["state_n"][core, 0])
            m["m0"] = np.ascontiguousarray(inp["state_m"][core, 0])
            m["keep"] = np.ones((1,), np.float32)
        else:
            j = core - 4
            m["x"] = np.ascontiguousarray(inp["x_prompt"][8 * j:8 * j + 8].reshape(T, D))
            m["cvec"] = np.ascontiguousarray(inp["c_ctx"])
            m["ck"] = np.zeros((H, 256, 128), np.float32)
            m["cv"] = np.zeros((H, 256, 128), np.float32)
            m["cosT"] = np.ones((128, T), np.float32)
            m["sinT"] = np.zeros((128, T), np.float32)
            m["c0"] = np.zeros((2, H, 64, 128), np.float32)
            m["n0"] = np.zeros((2, H, 64), np.float32)
            m["m0"] = np.zeros((2, H), np.float32)
            m["keep"] = np.zeros((1,), np.float32)
            ab = np.full((NSEG, 18), NEG, np.float32)
            for s_ in range(NSEG):
                ab[s_, 2 * s_:2 * s_ + 2] = 0.0
            m["abias"] = np.ascontiguousarray(np.broadcast_to(ab.reshape(1, -1), (128, NSEG * 18)))
        maps.append(m)
    return maps


def kernel(**inputs):
    inp = {k: np.asarray(v, dtype=np.float32) for k, v in inputs.items()}
    nc = build_program()
    maps = make_in_maps(inp)
    res = run_bass_kernel_spmd(nc, maps, core_ids=list(range(8)))
    r = res.results
    y_sample = np.stack([np.asarray(r[c]["y"], np.float32) for c in range(4)], axis=0)
    y_prompt = np.concatenate([np.asarray(r[c]["y"], np.float32).reshape(8, SEG, D) for c in range(4, 8)], axis=0)
    nk = np.concatenate([np.asarray(r[c]["nk"], np.float32) for c in range(4, 8)], axis=0)[:, None]
    nv = np.concatenate([np.asarray(r[c]["nv"], np.float32) for c in range(4, 8)], axis=0)[:, None]
    nC = np.concatenate([np.asarray(r[c]["nC"], np.float32) for c in range(4, 8)], axis=0)[:, None]
    nn = np.concatenate([np.asarray(r[c]["nn"], np.float32) for c in range(4, 8)], axis=0)[:, None]
    nm = np.concatenate([np.asarray(r[c]["nm"], np.float32) for c in range(4, 8)], axis=0)[:, None]
    return (y_prompt, y_sample, nk, nv, nC, nn, nm)
```
